# Optimizing a Trainium2 kernel written in Bass

```python
import jax
import jax.numpy as jnp
from jax import lax
import numpy as np

D_MODEL = 1024
BATCH = 16
SEQ = 2048
DEPTH = 4

GRID_W = 64
CTX_LEN = 256
EPS = 1e-6
N_BRANCH = 3
N_MODS = 6

N_HEADS = 8
Q_RANK = 256
KV_RANK = 128
NOPE_DIM = 64
ROPE_DIM = 32
V_DIM = 64
MLA_W = N_HEADS * V_DIM
SM_SCALE = (NOPE_DIM + ROPE_DIM) ** -0.5
ROPE_THETA = 10000.0
Q_BLOCK = 128

LRU_W = D_MODEL
LRU_BLOCKS = 8
LRU_BS = LRU_W // LRU_BLOCKS
LRU_CONV = 4
LRU_C = 8.0

POOL_WINDOWS = (2, 4, 8, 16)
POOL_W = D_MODEL // 2
POOL_G = POOL_W // len(POOL_WINDOWS)

D_FF = 2816
FFN_CONV = 3

COL_KV = KV_RANK
COL_KR = COL_KV + ROPE_DIM
COL_UX = COL_KR + LRU_W
COL_Q = COL_UX + Q_RANK
COL_UY = COL_Q + LRU_W
COL_POOL = COL_UY + POOL_W
IN_COLS = COL_POOL + N_BRANCH * D_MODEL

kernel_name = 'hybrid_mla_rglru_pool_dit'


def rms_norm(x, g):
    xf = x.astype(jnp.float32)
    y = xf * lax.rsqrt(jnp.mean(xf * xf, axis=-1, keepdims=True) + EPS)
    return (y * g.astype(jnp.float32)).astype(x.dtype)


def adaln_params(cond, w, b):
    m = (jax.nn.silu(cond) @ w + b)[..., None, :]
    return jnp.split(m, N_MODS, axis=-1)


def modulate(h, shift, scale):
    return h * (1.0 + scale) + shift


def dw_conv(x, w, b, left, right):
    y = lax.conv_general_dilated(x, w[:, None, :], window_strides=(1,), padding=[(left, right)],
                                 dimension_numbers=('NWC', 'WIO', 'NWC'),
                                 feature_group_count=x.shape[-1])
    return y + b


def axial_rope_tables(row_pos, col_pos):
    half = ROPE_DIM // 2
    inv = ROPE_THETA ** (-jnp.arange(0, half, 2, dtype=jnp.float32) / half)
    ang_r = row_pos.astype(jnp.float32)[:, None] * inv
    ang_c = col_pos.astype(jnp.float32)[:, None] * inv
    return (jnp.cos(ang_r), jnp.sin(ang_r), jnp.cos(ang_c), jnp.sin(ang_c))


def rotate_half(x, cos, sin):
    x1, x2 = jnp.split(x, 2, axis=-1)
    return jnp.concatenate([x1 * cos - x2 * sin, x2 * cos + x1 * sin], axis=-1)


def apply_axial_rope(x, tabs):
    cr, sr, cc, sc = tabs
    x_row, x_col = jnp.split(x, 2, axis=-1)
    return jnp.concatenate([rotate_half(x_row, cr, sr), rotate_half(x_col, cc, sc)], axis=-1).astype(x.dtype)


def mla_q(cq, lp):
    q = (rms_norm(cq, lp['q_norm_g']) @ lp['w_uq']) * SM_SCALE
    q = q.reshape(*cq.shape[:-1], N_HEADS, NOPE_DIM + ROPE_DIM)
    return q[..., :NOPE_DIM], q[..., NOPE_DIM:]


def mla_kv(ckv, lp):
    kv = rms_norm(ckv, lp['kv_norm_g']) @ lp['w_ukv']
    kv = kv.reshape(*ckv.shape[:-1], N_HEADS, NOPE_DIM + V_DIM)
    return kv[..., :NOPE_DIM], kv[..., NOPE_DIM:]


def attend(qn, qr, kn, kr, v):
    s = jnp.einsum('bqhd,bkhd->bhqk', qn, kn) + jnp.einsum('bqhr,bkr->bhqk', qr, kr)
    p = jax.nn.softmax(s.astype(jnp.float32), axis=-1).astype(v.dtype)
    return jnp.einsum('bhqk,bkhd->bqhd', p, v)


def blockwise_attention(qn, qr, kn, kr, v):
    b, n, h, _ = qn.shape
    nb = n // Q_BLOCK

    def to_blocks(t):
        return jnp.moveaxis(t.reshape(b, nb, Q_BLOCK, *t.shape[2:]), 1, 0)

    out = lax.map(lambda qs: attend(qs[0], qs[1], kn, kr, v), (to_blocks(qn), to_blocks(qr)))
    return jnp.moveaxis(out, 0, 1).reshape(b, n, h * V_DIM)


def rglru_coeffs(u, lp, d):
    ub = u.reshape(*u.shape[:-1], LRU_BLOCKS, LRU_BS)
    r = jax.nn.sigmoid((jnp.einsum('blnc,ncd->blnd', ub, lp['lru_wa'][d]).reshape(u.shape)
                        + lp['lru_ba'][d]).astype(jnp.float32))
    i = jax.nn.sigmoid((jnp.einsum('blnc,ncd->blnd', ub, lp['lru_wi'][d]).reshape(u.shape)
                        + lp['lru_bi'][d]).astype(jnp.float32))
    log_a = -LRU_C * r * jax.nn.softplus(-lp['lru_lambda'][d].astype(jnp.float32))
    a = jnp.exp(log_a)
    bx = jnp.sqrt(-jnp.expm1(2.0 * log_a)) * i * u.astype(jnp.float32)
    return a, bx


def linear_scan(a, bx, h0, reverse, keep_states):
    def step(h, ab):
        h = ab[0] * h + ab[1]
        return h, (h if keep_states else None)

    h_last, hs = lax.scan(step, h0, (jnp.swapaxes(a, 0, 1), jnp.swapaxes(bx, 0, 1)), reverse=reverse)
    return h_last, (jnp.swapaxes(hs, 0, 1) if keep_states else None)


def centred_mean(u, window):
    n = u.shape[1]
    left = window // 2
    right = window - 1 - left
    cs = jnp.pad(jnp.cumsum(u, axis=1), ((0, 0), (1, 0), (0, 0)))
    t = jnp.arange(n)
    lo = jnp.clip(t - left, 0, n)
    hi = jnp.clip(t + right + 1, 0, n)
    cnt = (hi - lo).astype(jnp.float32)
    return (cs[:, hi] - cs[:, lo]) / cnt[None, :, None]


def pool_mix(u, lp):
    groups = jnp.split(u.astype(jnp.float32), len(POOL_WINDOWS), axis=-1)
    d = jnp.stack([centred_mean(g, w) - g for g, w in zip(groups, POOL_WINDOWS)], axis=-2).astype(u.dtype)
    y = jnp.einsum('blgc,gcd->blgd', d, lp['pool_w']).reshape(u.shape) + lp['pool_b']
    return y * lp['pool_scale']


def merge_branches(att, rec, pool, gate_logits, lp):
    g = jax.nn.sigmoid(gate_logits.reshape(*gate_logits.shape[:-1], N_BRANCH, D_MODEL))
    m = (g[..., 0, :] * (att @ lp['proj_mla']) + g[..., 1, :] * (rec @ lp['proj_lru'])
         + g[..., 2, :] * (pool @ lp['proj_pool']))
    return m @ lp['w_out']


def conv_ffn(h, lp):
    val, gate = jnp.split(h @ lp['ffn_up'], 2, axis=-1)
    gate = dw_conv(gate, lp['ffn_conv_w'], lp['ffn_conv_b'], FFN_CONV // 2, FFN_CONV - 1 - FFN_CONV // 2)
    return (jax.nn.silu(gate) * val) @ lp['ffn_down']


def hybrid_layer(x, xc, c, c_ctx, lp, tabs, last):
    bsz, n_lat, _ = x.shape
    n_ctx = xc.shape[1]
    conv_l, conv_r = LRU_CONV // 2, LRU_CONV - 1 - LRU_CONV // 2
    mx = adaln_params(c, lp['ada_w'], lp['ada_b'])
    mc = adaln_params(c_ctx, lp['ada_w'], lp['ada_b'])

    hx = modulate(rms_norm(x, lp['norm1_g']), mx[0], mx[1])
    hc = modulate(rms_norm(xc, lp['norm1_g']), mc[0], mc[1])
    px = hx @ lp['w_in']
    pc = hc @ (lp['w_in'][:, :COL_UX] if last else lp['w_in'])

    kn_c, v_c = mla_kv(pc[..., :COL_KV], lp)
    kr_c = pc[..., COL_KV:COL_KR]
    u_c = dw_conv(pc[..., COL_KR:COL_UX], lp['lru_conv_w'], lp['lru_conv_b'], conv_l, conv_r)
    h0 = jnp.zeros((bsz, LRU_W), jnp.float32)
    af_c, bf_c = rglru_coeffs(u_c, lp, 0)
    ab_c, bb_c = rglru_coeffs(u_c, lp, 1)
    hf_c_last, hf_c = linear_scan(af_c, bf_c, h0, False, not last)
    hb_c_last, hb_c = linear_scan(ab_c, bb_c, h0, True, not last)

    tabs_h = tuple(t[:, None, :] for t in tabs)
    kn_x, v_x = mla_kv(px[..., :COL_KV], lp)
    kr_x = apply_axial_rope(px[..., COL_KV:COL_KR], tabs)
    qn_x, qr_x = mla_q(px[..., COL_UX:COL_Q], lp)
    qr_x = apply_axial_rope(qr_x, tabs_h)
    att_x = blockwise_attention(qn_x, qr_x,
                                jnp.concatenate([kn_x, kn_c], axis=1),
                                jnp.concatenate([kr_x, kr_c], axis=1),
                                jnp.concatenate([v_x, v_c], axis=1))

    u_x = dw_conv(px[..., COL_KR:COL_UX], lp['lru_conv_w'], lp['lru_conv_b'], conv_l, conv_r)
    af_x, bf_x = rglru_coeffs(u_x, lp, 0)
    ab_x, bb_x = rglru_coeffs(u_x, lp, 1)
    _, hf_x = linear_scan(af_x, bf_x, hf_c_last, False, True)
    _, hb_x = linear_scan(ab_x, bb_x, hb_c_last, True, True)
    rec_x = ((hf_x + hb_x) * jax.nn.gelu(px[..., COL_Q:COL_UY].astype(jnp.float32))).astype(x.dtype)

    pool_x = pool_mix(px[..., COL_UY:COL_POOL], lp)

    x = x + mx[2] * merge_branches(att_x, rec_x, pool_x, px[..., COL_POOL:], lp)
    x = x + mx[5] * conv_ffn(modulate(rms_norm(x, lp['norm2_g']), mx[3], mx[4]), lp)
    if last:
        return x, None

    qn_c, qr_c = mla_q(pc[..., COL_UX:COL_Q], lp)
    att_c = attend(qn_c, qr_c, kn_c, kr_c, v_c).reshape(bsz, n_ctx, MLA_W)
    rec_c = ((hf_c + hb_c) * jax.nn.gelu(pc[..., COL_Q:COL_UY].astype(jnp.float32))).astype(xc.dtype)
    pool_c = pool_mix(pc[..., COL_UY:COL_POOL], lp)
    xc = xc + mc[2] * merge_branches(att_c, rec_c, pool_c, pc[..., COL_POOL:], lp)
    xc = xc + mc[5] * conv_ffn(modulate(rms_norm(xc, lp['norm2_g']), mc[3], mc[4]), lp)
    return x, xc


def setup_inputs(seed: int = 0) -> dict:
    key = jax.random.key(seed)
    ks = iter(jax.random.split(key, 48))
    f32 = jnp.float32
    L = DEPTH

    def nrm(shape, fan_in, s=1.0):
        return (s * fan_in ** -0.5) * jax.random.normal(next(ks), shape, f32)

    def gain(shape):
        return 1.0 + 0.02 * jax.random.normal(next(ks), shape, f32)

    def small(shape):
        return 0.02 * jax.random.normal(next(ks), shape, f32)

    x = jax.random.normal(next(ks), (BATCH, SEQ, D_MODEL), f32)
    c = jax.random.normal(next(ks), (BATCH, D_MODEL), f32)
    ctx = jax.random.normal(next(ks), (BATCH, CTX_LEN, D_MODEL), f32)
    c_ctx = jax.random.normal(next(ks), (D_MODEL,), f32)
    a_c = jax.random.uniform(next(ks), (L, 2, LRU_W), f32, 0.9, 0.999)
    s = a_c ** (1.0 / LRU_C)
    lru_lambda = jnp.log(s) - jnp.log1p(-s)
    return {
        'x': x,
        'c': c,
        'ctx': ctx,
        'c_ctx': c_ctx,
        'ada_w': nrm((L, D_MODEL, N_MODS * D_MODEL), D_MODEL, 0.5),
        'ada_b': small((L, N_MODS * D_MODEL)),
        'norm1_g': gain((L, D_MODEL)),
        'norm2_g': gain((L, D_MODEL)),
        'w_in': nrm((L, D_MODEL, IN_COLS), D_MODEL),
        'q_norm_g': gain((L, Q_RANK)),
        'w_uq': nrm((L, Q_RANK, N_HEADS * (NOPE_DIM + ROPE_DIM)), Q_RANK),
        'kv_norm_g': gain((L, KV_RANK)),
        'w_ukv': nrm((L, KV_RANK, N_HEADS * (NOPE_DIM + V_DIM)), KV_RANK),
        'lru_conv_w': nrm((L, LRU_CONV, LRU_W), LRU_CONV),
        'lru_conv_b': small((L, LRU_W)),
        'lru_wa': nrm((L, 2, LRU_BLOCKS, LRU_BS, LRU_BS), LRU_BS),
        'lru_ba': small((L, 2, LRU_W)),
        'lru_wi': nrm((L, 2, LRU_BLOCKS, LRU_BS, LRU_BS), LRU_BS),
        'lru_bi': small((L, 2, LRU_W)),
        'lru_lambda': lru_lambda,
        'pool_w': nrm((L, len(POOL_WINDOWS), POOL_G, POOL_G), POOL_G),
        'pool_b': small((L, POOL_W)),
        'pool_scale': gain((L, POOL_W)),
        'proj_mla': nrm((L, MLA_W, D_MODEL), MLA_W),
        'proj_lru': nrm((L, LRU_W, D_MODEL), LRU_W),
        'proj_pool': nrm((L, POOL_W, D_MODEL), POOL_W),
        'w_out': nrm((L, D_MODEL, D_MODEL), D_MODEL),
        'ffn_up': nrm((L, D_MODEL, 2 * D_FF), D_MODEL),
        'ffn_conv_w': nrm((L, FFN_CONV, D_FF), FFN_CONV),
        'ffn_conv_b': small((L, D_FF)),
        'ffn_down': nrm((L, D_FF, D_MODEL), D_FF),
        'final_norm_g': gain((D_MODEL,)),
    }


def reference(x, c, ctx, c_ctx, ada_w, ada_b, norm1_g, norm2_g, w_in, q_norm_g, w_uq, kv_norm_g, w_ukv,
              lru_conv_w, lru_conv_b, lru_wa, lru_ba, lru_wi, lru_bi, lru_lambda, pool_w, pool_b,
              pool_scale, proj_mla, proj_lru, proj_pool, w_out, ffn_up, ffn_conv_w, ffn_conv_b,
              ffn_down, final_norm_g):
    n_lat = x.shape[1]
    rows = n_lat // GRID_W
    row_pos = jnp.repeat(jnp.arange(rows), GRID_W)
    col_pos = jnp.tile(jnp.arange(GRID_W), rows)
    tabs = axial_rope_tables(row_pos, col_pos)
    xc = ctx
    for l in range(DEPTH):
        lp = dict(ada_w=ada_w[l], ada_b=ada_b[l], norm1_g=norm1_g[l], norm2_g=norm2_g[l], w_in=w_in[l],
                  q_norm_g=q_norm_g[l], w_uq=w_uq[l], kv_norm_g=kv_norm_g[l], w_ukv=w_ukv[l],
                  lru_conv_w=lru_conv_w[l], lru_conv_b=lru_conv_b[l], lru_wa=lru_wa[l], lru_ba=lru_ba[l],
                  lru_wi=lru_wi[l], lru_bi=lru_bi[l], lru_lambda=lru_lambda[l], pool_w=pool_w[l],
                  pool_b=pool_b[l], pool_scale=pool_scale[l], proj_mla=proj_mla[l], proj_lru=proj_lru[l],
                  proj_pool=proj_pool[l], w_out=w_out[l], ffn_up=ffn_up[l], ffn_conv_w=ffn_conv_w[l],
                  ffn_conv_b=ffn_conv_b[l], ffn_down=ffn_down[l])
        x, xc = hybrid_layer(x, xc, c, c_ctx, lp, tabs, l == DEPTH - 1)
    return rms_norm(x, final_norm_g)
```

```python
import contextlib
import numpy as np
import concourse.bass as bass
import concourse.mybir as mybir
from concourse.bass_utils import run_bass_kernel_spmd

F32 = mybir.dt.float32
BF16 = mybir.dt.bfloat16
ALU = mybir.AluOpType
AF = mybir.ActivationFunctionType

NDSEM = 24
SAME_ENG_SYNC = True


class Op:
    __slots__ = ("eng", "fn", "dma", "deps", "need_inc", "count", "sem_i", "sem_val")

    def __init__(self, eng, fn, dma):
        self.eng = eng
        self.fn = fn
        self.dma = dma
        self.deps = []
        self.need_inc = False
        self.count = 0
        self.sem_i = 0
        self.sem_val = 0


class Prog:
    ENGS = ("pe", "act", "dve", "pool", "sp")

    def __init__(self, nc):
        self.nc = nc
        self.ops = {e: [] for e in self.ENGS}
        self.writers = {}
        self.readers = {}
        self.ndma = {e: 0 for e in self.ENGS}
        self.bar = None
        self.pending_dma = []

    def op(self, eng, fn, reads=(), writes=(), dma=False, nobar=False):
        o = Op(eng, fn, dma)
        deps = {}
        for k in reads:
            for w in self.writers.get(k, ()):
                deps[id(w)] = w
        for k in writes:
            for w in self.writers.get(k, ()):
                deps[id(w)] = w
            for r in self.readers.get(k, ()):
                deps[id(r)] = r
        if eng == "pe":
            nobar = True
        if self.bar is not None and not nobar:
            deps[id(self.bar)] = self.bar
        for d in deps.values():
            if d is o:
                continue
            if (not d.dma) and d.eng == eng and (not dma):
                if eng == "pe" or not SAME_ENG_SYNC:
                    continue
            o.deps.append(d)
            d.need_inc = True
        for k in reads:
            lst = self.readers.setdefault(k, [])
            if not dma:
                lst[:] = [r for r in lst if r.dma or r.eng != eng]
            lst.append(o)
        for k in writes:
            self.writers[k] = [o]
            self.readers[k] = []
        if dma:
            i = self.ndma[eng]
            self.ndma[eng] += 1
            o.sem_i = i % NDSEM
            o.sem_val = 16 * (i // NDSEM + 1)
            if not nobar:
                self.pending_dma.append(o)
        self.ops[eng].append(o)
        return o

    def dma(self, eng, out, in_, reads=(), writes=(), nobar=False):
        return self.op(eng, lambda e: e.dma_start(out=out, in_=in_), reads, writes, dma=True, nobar=nobar)

    def barrier(self, scratch):
        o = Op("dve", lambda e: e.memset(scratch, 0.0), False)
        for e in ("pe", "act", "pool"):
            for d in reversed(self.ops[e]):
                if not d.dma:
                    o.deps.append(d)
                    d.need_inc = True
                    break
        for d in reversed(self.ops["dve"]):
            if not d.dma:
                if SAME_ENG_SYNC:
                    o.deps.append(d)
                    d.need_inc = True
                break
        if self.bar is not None:
            o.deps.append(self.bar)
        for d in self.pending_dma:
            o.deps.append(d)
        self.pending_dma = []
        o.need_inc = True
        self.ops["dve"].append(o)
        self.bar = o
        return o

    def emit(self):
        nc = self.nc
        with contextlib.ExitStack() as st:
            csem = {e: st.enter_context(nc.semaphore("c_" + e)) for e in ("pe", "act", "dve", "pool")}
            dsem = {e: [st.enter_context(nc.semaphore("d_%s%d" % (e, i))) for i in range(NDSEM)]
                    for e in self.ENGS if self.ndma[e] > 0}
            for e, lst in self.ops.items():
                c = 0
                for o in lst:
                    if o.dma:
                        continue
                    if o.need_inc:
                        c += 1
                        o.count = c
            block = st.enter_context(nc.Block())

            def run(ename, handle):
                waited = {}

                def wait(sem, val):
                    key = id(sem)
                    if waited.get(key, 0) >= val:
                        return
                    waited[key] = val
                    handle.wait_ge(sem, val)

                for o in self.ops[ename]:
                    for d in o.deps:
                        if d.dma:
                            wait(dsem[d.eng][d.sem_i], d.sem_val)
                        else:
                            wait(csem[d.eng], d.count)
                    if o.dma:
                        if o.sem_val > 16:
                            wait(dsem[ename][o.sem_i], o.sem_val - 16)
                        o.fn(handle).then_inc(dsem[ename][o.sem_i], 16)
                    else:
                        ins = o.fn(handle)
                        if o.need_inc:
                            ins.then_inc(csem[ename], 1)
                if ename in dsem:
                    n = self.ndma[ename]
                    for i in range(min(n, NDSEM)):
                        cnt = (n - 1 - i) // NDSEM + 1
                        wait(dsem[ename][i], 16 * cnt)

            if self.ops["pe"]:
                @block.tensor
                def _(eng):
                    run("pe", eng)
            if self.ops["act"]:
                @block.scalar
                def _(eng):
                    run("act", eng)
            if self.ops["dve"]:
                @block.vector
                def _(eng):
                    run("dve", eng)
            if self.ops["pool"]:
                @block.gpsimd
                def _(eng):
                    run("pool", eng)
            if self.ops["sp"]:
                @block.sync
                def _(eng):
                    run("sp", eng)


D = 1024
KC = 8
SEQ = 2048
CTXL = 256
T = SEQ + CTXL
NLAYER = 4
NCORES = 8
EPS = 1e-6
SM_SCALE = 96 ** -0.5
LN_SM = -0.5 * float(np.log(96.0))
TILES = [(0, 512), (512, 512), (1024, 512), (1536, 512), (2048, 256)]
SEGS = [(0, 2048), (2048, 256)]
DFF = 2816
NJ = 22
POOL_WIN = (2, 4, 8, 16)

VOFF = {}
_o = 0
for _n, _w in (("n1g", 8), ("n2g", 8), ("adab", 48), ("qng", 2), ("kvng", 1), ("lcw", 32), ("lcb", 8), ("lba", 16), ("lbi", 16),
               ("llam", 16), ("pb", 4), ("psc", 4), ("fcw", 66), ("fcb", 22), ("fng", 8)):
    VOFF[_n] = _o
    _o += _w
NV = _o

COL_KV, COL_KR, COL_UX, COL_Q, COL_UY, COL_POOL = 128, 160, 1184, 1440, 2464, 2976


def _win_perm():
    cols = []
    for n in range(8):
        cols += list(range(COL_KR + n * 128, COL_KR + (n + 1) * 128))
        cols += list(range(COL_Q + n * 128, COL_Q + (n + 1) * 128))
    cols += list(range(COL_UY, COL_POOL))
    cols += list(range(COL_POOL, COL_POOL + 3072))
    cols += list(range(0, 128))
    cols += list(range(COL_UX, COL_Q))
    return np.array(cols)


WIN_LRU0, WIN_POOL0, WIN_GATE0, WIN_MLA0, WIN_NCOL = 0, 2048, 2560, 5632, 6016
KR_SWAP = np.array(list(range(8, 16)) + list(range(0, 8)) + list(range(24, 32)) + list(range(16, 24)))


def build_program(nlayer=NLAYER, nb=2, dbg=None, full_last=False, stop_after=None):
    nc = bass.Bass("TRN2", target_bir_lowering=False)
    dt_in = lambda name, shape: nc.dram_tensor(name, shape, F32, kind="ExternalInput").ap()
    xT = dt_in("xT", [nb, D, T])
    cT = dt_in("cT", [128, KC, 3])
    vecs_d = dt_in("vecs", [128, nlayer, NV])
    ropeC_d = dt_in("ropeC", [32, T])
    ropeS_d = dt_in("ropeS", [32, T])
    ptab_d = dt_in("ptab", [128, 4, 16])
    ada_w = dt_in("ada_w", [nlayer, D, 6 * D])
    w_inP = dt_in("w_inP", [nlayer, D, WIN_NCOL])
    w_kr2 = dt_in("w_kr2", [nlayer, D, 64])
    w_uq2 = dt_in("w_uq2", [nlayer, 256, 2, 768])
    w_ukvP = dt_in("w_ukvP", [nlayer, 128, 1024])
    lru_wa = dt_in("lru_wa", [nlayer, 2, 8, 128, 128])
    lru_wi = dt_in("lru_wi", [nlayer, 2, 8, 128, 128])
    pool_w = dt_in("pool_w", [nlayer, 4, 128, 128])
    proj_mla = dt_in("proj_mla", [nlayer, 512, D])
    proj_lru = dt_in("proj_lru", [nlayer, D, D])
    proj_pool = dt_in("proj_pool", [nlayer, 512, D])
    w_out = dt_in("w_out", [nlayer, D, D])
    ffn_upP = dt_in("ffn_upP", [nlayer, D, 2 * DFF])
    ffn_down = dt_in("ffn_down", [nlayer, DFF, D])
    yT = nc.dram_tensor("yT", [nb, D, SEQ], F32, kind="ExternalOutput").ap()
    dbg_out = {}
    if dbg:
        for name, shape in dbg.items():
            dbg_out[name] = nc.dram_tensor("dbg_" + name, shape[1], shape[0], kind="ExternalOutput").ap()
    xs = nc.dram_tensor("xs", [nb, D, T], F32, kind="Internal").ap()
    recT_h = nc.dram_tensor("recT_h", [nb, D, T], BF16, kind="Internal").ap()
    poolT_h = nc.dram_tensor("poolT_h", [nb, 512, T], BF16, kind="Internal").ap()
    gsc_h = nc.dram_tensor("gsc_h", [nb, 8, 128, 3, T], BF16, kind="Internal").ap()
    actT_h = nc.dram_tensor("actT_h", [nb, 5, 128, NJ, 512], BF16, kind="Internal").ap()

    RSZ = KC * T
    VSZ = 18 * 8 * 65
    NWB = 5
    WBSZ = 4096
    WKSZ = 10240
    O_R1, O_R2, O_R3 = 0, RSZ, 2 * RSZ
    O_RV = 3 * RSZ
    O_WB = O_RV + VSZ
    O_WK = O_WB + NWB * WBSZ
    NA = O_WK + WKSZ

    with contextlib.ExitStack() as st:
        AR = st.enter_context(nc.sbuf_tensor("arena", [128, NA], BF16))
        vec = st.enter_context(nc.sbuf_tensor("vec", [128, nlayer, NV], F32))
        modt = st.enter_context(nc.sbuf_tensor("modt", [128, nlayer, 48, 3], F32))
        der = st.enter_context(nc.sbuf_tensor("der", [128, nlayer, 3, 2, 8], F32))
        lder = st.enter_context(nc.sbuf_tensor("lder", [128, nlayer, 5, 16], F32))
        hg1 = st.enter_context(nc.sbuf_tensor("hg1", [128, nlayer, 3, 8], F32))
        ltmp = st.enter_context(nc.sbuf_tensor("ltmp", [128, 4, 16], F32))
        ptab = st.enter_context(nc.sbuf_tensor("ptab_s", [128, 4, 16], F32))
        ones_bf = st.enter_context(nc.sbuf_tensor("ones_bf", [128, 128], BF16))
        ones_f = st.enter_context(nc.sbuf_tensor("ones_f", [128, 64], F32))
        wkr = st.enter_context(nc.sbuf_tensor("wkr", [128, KC, 2, 96], BF16))
        csb = st.enter_context(nc.sbuf_tensor("csb", [128, KC, 3], F32))
        scb = st.enter_context(nc.sbuf_tensor("scb", [128, KC, 3], BF16))
        bscr = st.enter_context(nc.sbuf_tensor("bscr", [128, 2], F32))
        PSALL = st.enter_context(nc.psum_tensor("psall", [128, 4096], F32))
        PS = [PSALL[:, i * 512:(i + 1) * 512] for i in range(8)]
        P = Prog(nc)

        def abf(off, *shape):
            n = int(np.prod(shape))
            ap = AR[:, off:off + n]
            if len(shape) == 2:
                return ap.rearrange("p (a b) -> p a b", a=shape[0])
            if len(shape) == 3:
                return ap.rearrange("p (a b c) -> p a b c", a=shape[0], b=shape[1])
            return ap

        def af32(off, *shape):
            n = int(np.prod(shape))
            ap = AR[:, off:off + 2 * n].bitcast(F32)
            if len(shape) == 2:
                return ap.rearrange("p (a b) -> p a b", a=shape[0])
            if len(shape) == 3:
                return ap.rearrange("p (a b c) -> p a b c", a=shape[0], b=shape[1])
            return ap

        R1 = abf(O_R1, KC, T)
        R2 = abf(O_R2, KC, T)
        R3 = abf(O_R3, KC, T)
        VV = abf(O_RV, 18, 8, 65)
        PLT = abf(O_RV, 4, T)

        psc = [0]

        def nextps():
            i = psc[0] % 7
            psc[0] += 1
            return i

        wb_rot = list(range(NWB))

        def wb_take():
            i = wb_rot.pop(0)
            wb_rot.append(i)
            return i

        def wb_pin():
            return wb_rot.pop(0)

        def wb_unpin(i):
            wb_rot.insert(0, i)

        def wb_ap(i, *shape):
            return abf(O_WB + i * WBSZ, *shape)

        def load_slab(dram_ap, i, shape, pslice=None):
            dst = wb_ap(i, *shape)
            if pslice is not None:
                dst = dst[pslice[0]:pslice[1]]
            P.dma("pool", dst, dram_ap, writes=[("WB", i)], nobar=True)
            return wb_ap(i, *shape)

        def mm(ps_ap, pairs, reads, pskey):
            n = len(pairs)
            for idx, (l, r) in enumerate(pairs):
                P.op("pe", lambda e, l=l, r=r, idx=idx: e.matmul(ps_ap, l, r, start=(idx == 0), stop=(idx == n - 1)),
                     reads=reads, writes=[pskey])

        def ACT(out, in_, func, reads, writes, bias=0.0, scale=1.0):
            P.op("act", lambda e: e.activation(out=out, in_=in_, func=func, bias=bias, scale=scale), reads=reads, writes=writes)

        def STT(out, in0, scalar, in1, op0, op1, reads, writes, eng="dve"):
            P.op(eng, lambda e: e.scalar_tensor_tensor(out=out, in0=in0, scalar=scalar, in1=in1, op0=op0, op1=op1), reads=reads, writes=writes)

        def TT(out, in0, in1, op, reads, writes, eng="dve"):
            P.op(eng, lambda e: e.tensor_tensor(out=out, in0=in0, in1=in1, op=op), reads=reads, writes=writes)

        def TS(out, in0, s1, s2, op0, op1, reads, writes, eng="dve"):
            P.op(eng, lambda e: e.tensor_scalar(out=out, in0=in0, scalar1=s1, scalar2=s2, op0=op0, op1=op1), reads=reads, writes=writes)

        def RECIP(out, in_, reads, writes):
            P.op("dve", lambda e: e.reciprocal(out=out, in_=in_), reads=reads, writes=writes)

        def V(l, name, i=0, n=1):
            o = VOFF[name] + i
            return vec[:, l, o:o + n]

        def dump(name, src_ap, reads, idx=None, sl=None):
            if name in dbg_out:
                dst = dbg_out[name] if idx is None else dbg_out[name][idx]
                if sl is not None:
                    dst = dst[:, sl[0]:sl[1]]
                P.dma("sp", dst, src_ap, reads=reads)

        P.dma("sp", vec[:], vecs_d, writes=["vec"])
        P.dma("sp", csb[:], cT, writes=["csb"])
        P.dma("sp", ptab[:], ptab_d, writes=["ptab"])
        P.op("dve", lambda e: e.memset(ones_bf[:], 1.0), writes=["ones_bf"])
        P.op("dve", lambda e: e.memset(ones_f[:], 1.0), writes=["ones_f"])
        P.op("dve", lambda e: e.memset(wkr[:], 0.0), writes=["wkr"])
        ACT(scb[:], csb[:], AF.Silu, ["csb"], ["scb"])
        psM = PS[7][:, 0:144].rearrange("p (j v) -> p j v", v=3)

        def ada_slab_load(l, s_):
            i = wb_take()
            wb = load_slab(ada_w[l][:, s_ * 512:(s_ + 1) * 512].rearrange("(k p) c -> p k c", p=128), i, (KC, 512))
            return i, wb

        def ada_slab_mm(l, s_, i, wb):
            for jj in range(4):
                j = s_ * 4 + jj
                mm(psM[:, j, :], [(wb[:, k, jj * 128:(jj + 1) * 128], scb[:, k, :]) for k in range(KC)], [("WB", i), "scb"], ("ps", 7))

        def ada_finish(l):
            for v in range(3):
                TT(modt[:, l, :, v], psM[:, :, v], V(l, "adab", 0, 48), ALU.add, [("ps", 7), "vec"], [("modt", l)])
            for v in range(3):
                for sub in range(2):
                    STT(der[:, l, v, sub, :], modt[:, l, (1 + 3 * sub) * 8:(2 + 3 * sub) * 8, v], 1.0, V(l, "n1g" if sub == 0 else "n2g", 0, 8),
                        ALU.add, ALU.mult, [("modt", l), "vec"], [("der", l)])
            e_ = ltmp[:, 0, :]
            w_ = ltmp[:, 1, :]
            w2 = ltmp[:, 2, :]
            pl = ltmp[:, 3, :]
            ACT(e_, V(l, "llam", 0, 16), AF.Exp, ["vec"], ["ltmp"], scale=-1.0)
            TS(w_, e_, 2.0, None, ALU.add, ALU.bypass, ["ltmp"], ["ltmp"])
            RECIP(w_, w_, ["ltmp"], ["ltmp"])
            TT(w_, w_, e_, ALU.mult, ["ltmp"], ["ltmp"])
            TT(w2, w_, w_, ALU.mult, ["ltmp"], ["ltmp"])
            TS(pl, w2, 1.0 / 9.0, 1.0 / 7.0, ALU.mult, ALU.add, ["ltmp"], ["ltmp"])
            for cf in (1.0 / 5.0, 1.0 / 3.0, 1.0):
                TT(pl, pl, w2, ALU.mult, ["ltmp"], ["ltmp"])
                TS(pl, pl, cf, None, ALU.add, ALU.bypass, ["ltmp"], ["ltmp"])
            TT(pl, pl, w_, ALU.mult, ["ltmp"], ["ltmp"])
            TS(lder[:, l, 0, :], pl, -8.0, None, ALU.mult, ALU.bypass, ["ltmp"], [("lder", l)])
            TS(lder[:, l, 1, :], pl, -16.0, None, ALU.mult, ALU.bypass, ["ltmp"], [("lder", l)])
            TT(lder[:, l, 2, 0:4], V(l, "pb", 0, 4), V(l, "psc", 0, 4), ALU.mult, ["vec"], [("lder", l)])
            for v in range(3):
                TS(hg1[:, l, v, :], modt[:, l, 16:24, v], 0.5, None, ALU.mult, ALU.bypass, [("modt", l)], [("der", l)])
            TS(lder[:, l, 3, :], V(l, "lba", 0, 16), 0.5, None, ALU.mult, ALU.bypass, ["vec"], [("lder", l)])
            TS(lder[:, l, 4, :], V(l, "lbi", 0, 16), 0.5, None, ALU.mult, ALU.bypass, ["vec"], [("lder", l)])

        for s_ in range(12):
            i_, wb_ = ada_slab_load(0, s_)
            ada_slab_mm(0, s_, i_, wb_)
        ada_finish(0)
        dump("mods", modt[:, 0, :, :], [("modt", 0)])

        def G(l, v, sub, k):
            return der[:, l, v, sub, k:k + 1]

        def MOD(l, v, m, k):
            return modt[:, l, m * 8 + k, v:v + 1]

        WK = O_WK
        norm_ctr = [0]

        def norm_tile(l, v, sub, xt, tn, psq_i, dst, dkeys, rkeys):
            par = norm_ctr[0] % 2
            norm_ctr[0] += 1
            rs = af32(WK + par * 5120, 512)[:, :tn]
            ACT(rs, PS[psq_i][:, :tn], AF.Ln, [("ps", psq_i)], [("rs", par)], bias=EPS, scale=1.0 / D)
            ACT(rs, rs, AF.Exp, [("rs", par)], [("rs", par)], scale=-0.5)
            for k in range(KC):
                tmp = af32(WK + 1024 + (k % 2) * 1024, 512)[:, :tn]
                STT(tmp, xt[:, k, :], G(l, v, sub, k), rs, ALU.mult, ALU.mult, rkeys(k) + [("rs", par), ("der", l)], [("ntmp", k % 2)])
                ACT(dst(k), tmp, AF.Identity, [("ntmp", k % 2), ("modt", l)], [dkeys(k)], bias=MOD(l, v, 3 * sub, k))

        def layer_batch(l, b):
            last = (l == nlayer - 1)
            tiles = TILES
            skipc = last and not full_last
            tiles_e = TILES[:4] if skipc else TILES
            segs_e = SEGS[:1] if skipc else SEGS
            xsrc = xT if l == 0 else xs
            phase = [0]

            class _Stop(Exception):
                pass

            def B():
                phase[0] += 1
                if stop_after is not None and phase[0] > stop_after:
                    raise _Stop()
                P.barrier(bscr[:, 0:1])

            B()
            for ti, (t0, tn) in enumerate(tiles):
                v = 2 if ti == 4 else b
                xt = af32(O_R2 + (ti % 2) * 8192, KC, 512)[:, :, :tn]
                P.dma("sp", xt, xsrc[b][:, t0:t0 + tn].rearrange("(k p) t -> p k t", p=128), reads=([("xs", b, ti)] if l > 0 else []),
                      writes=[("xt", ti % 2)])
                sq = abf(O_R3 + (ti % 2) * 4096, KC, 512)[:, :, :tn]
                ACT(sq, xt, AF.Square, [("xt", ti % 2)], [("sq", ti % 2)])
                pi = nextps()
                mm(PS[pi][:, :tn], [(ones_bf[:], sq[:, k, :]) for k in range(KC)], [("sq", ti % 2), "ones_bf"], ("ps", pi))
                norm_tile(l, v, 0, xt, tn, pi, lambda k, t0=t0, tn=tn: R1[:, k, t0:t0 + tn], lambda k, ti=ti: ("R1", k, ti), lambda k, ti=ti: [("xt", ti % 2)])
            if l == 0 and b == 0:
                for k in range(KC):
                    dump("hx", R1[:, k, :], [("R1", k, ti) for ti in range(5)], idx=k)

            R1K = lambda ti: [("R1", k, ti) for k in range(KC)]
            BK = lambda i: [("B", i, t) for t in range(5)]

            def scan(out, a, bx, init, reads, writes):
                P.op("dve", lambda e: e.tensor_tensor_scan(out=out, data0=a, data1=bx, initial=init, op0=ALU.mult, op1=ALU.add),
                     reads=reads, writes=writes)

            gate_state = {"slab": None, "q": 0}

            def gate_block():
                q = gate_state["q"]
                gate_state["q"] += 1
                s_, jj = q // 4, q % 4
                if jj == 0:
                    gi = wb_take()
                    gwb = load_slab(w_inP[l][:, WIN_GATE0 + s_ * 512:WIN_GATE0 + (s_ + 1) * 512].rearrange("(k p) c -> p k c", p=128), gi, (KC, 512))
                    gate_state["slab"] = (gi, gwb)
                gi, gwb = gate_state["slab"]
                j, n = q // 8, q % 8
                gst = abf(O_RV + (q % 2) * T, T)
                for ti, (t0, tn) in enumerate(tiles):
                    pi = nextps()
                    mm(PS[pi][:, :tn], [(gwb[:, k, jj * 128:(jj + 1) * 128], R1[:, k, t0:t0 + tn]) for k in range(KC)], [("WB", gi)] + R1K(ti), ("ps", pi))
                    ACT(gst[:, t0:t0 + tn], PS[pi][:, :tn], AF.Tanh, [("ps", pi)], [("gst", q % 2)], scale=0.5)
                P.dma("sp", gsc_h[b][n, :, j, :], gst, reads=[("gst", q % 2)], writes=[("gsc", b, n, j)])

            B()
            i_wa = wb_pin()
            i_wi = wb_pin()
            lwa = load_slab(lru_wa[l].rearrange("d n c e -> c d n e"), i_wa, (2, 8, 128))
            lwi = load_slab(lru_wi[l].rearrange("d n c e -> c d n e"), i_wi, (2, 8, 128))
            Bf = lambda i: af32(O_R2 + i * 4608, T)
            ub = abf(WK, T)
            rst = abf(WK + T, T)
            gls = [abf(WK + 2 * T, T), abf(WK + 3 * T, T)]
            XB = af32(O_RV + 2 * T, T)
            XK_ = [("X", t) for t in range(5)]
            lru_slab = [None]

            def lru_front(n):
                s_, cc = n // 2, n % 2
                if cc == 0:
                    i_ = wb_take()
                    lru_slab[0] = (i_, load_slab(w_inP[l][:, WIN_LRU0 + s_ * 512:WIN_LRU0 + (s_ + 1) * 512].rearrange("(k p) c -> p k c", p=128), i_, (KC, 512)))
                i, wb = lru_slab[0]
                gl = gls[n % 2]
                for ti, (t0, tn) in enumerate(tiles):
                    pi = nextps()
                    mm(PS[pi][:, :tn], [(wb[:, k, 2 * cc * 128:(2 * cc + 1) * 128], R1[:, k, t0:t0 + tn]) for k in range(KC)],
                       [("WB", i)] + R1K(ti), ("ps", pi))
                    P.op("dve", lambda e, pi=pi, t0=t0, tn=tn: e.tensor_copy(out=XB[:, t0:t0 + tn], in_=PS[pi][:, :tn]), reads=[("ps", pi)], writes=[("X", ti)])
                for ti, (t0, tn) in enumerate(tiles):
                    pi = nextps()
                    mm(PS[pi][:, :tn], [(wb[:, k, (2 * cc + 1) * 128:(2 * cc + 2) * 128], R1[:, k, t0:t0 + tn]) for k in range(KC)],
                       [("WB", i)] + R1K(ti), ("ps", pi))
                    ACT(gl[:, t0:t0 + tn], PS[pi][:, :tn], AF.Gelu_apprx_tanh, [("ps", pi)], [("gl", n % 2, ti)])

            def lru_conv(n):
                for (s0, sn) in SEGS:
                    TS(Bf(1)[:, s0:s0 + sn], XB[:, s0:s0 + sn], V(l, "lcw", 2 * 8 + n), V(l, "lcb", n), ALU.mult, ALU.add,
                       XK_ + ["vec"], BK(1))
                    for k in (0, 1, 3):
                        o = k - 2
                        a = max(0, -o)
                        e_ = sn - max(0, o)
                        STT(Bf(1)[:, s0 + a:s0 + e_], XB[:, s0 + a + o:s0 + e_ + o], V(l, "lcw", k * 8 + n), Bf(1)[:, s0 + a:s0 + e_],
                            ALU.mult, ALU.add, XK_ + BK(1) + ["vec"], BK(1))
                if l == 0 and b == 0 and n == 0:
                    dump("u0", Bf(1), BK(1))
                P.op("dve", lambda e: e.tensor_copy(out=ub, in_=Bf(1)), reads=BK(1), writes=["ub"])

            def lru_gates(n):
                for d in range(2):
                    br, bi_ = 2 + 3 * d, 4 + 3 * d
                    for ti, (t0, tn) in enumerate(tiles):
                        pr = nextps()
                        mm(PS[pr][:, :tn], [(lwa[:, d, n, :], ub[:, t0:t0 + tn])], [("WB", i_wa), "ub"], ("ps", pr))
                        ACT(Bf(br)[:, t0:t0 + tn], PS[pr][:, :tn], AF.Tanh, [("ps", pr), ("lder", l)], [("B", br, ti)],
                            bias=lder[:, l, 3, d * 8 + n:d * 8 + n + 1], scale=0.5)
                        pq = nextps()
                        mm(PS[pq][:, :tn], [(lwi[:, d, n, :], ub[:, t0:t0 + tn])], [("WB", i_wi), "ub"], ("ps", pq))
                        ACT(Bf(bi_)[:, t0:t0 + tn], PS[pq][:, :tn], AF.Tanh, [("ps", pq), ("lder", l)], [("B", bi_, ti)],
                            bias=lder[:, l, 4, d * 8 + n:d * 8 + n + 1], scale=0.5)
                for d in range(2):
                    br, b2 = 2 + 3 * d, 3 + 3 * d
                    la_h = lder[:, l, 0, d * 8 + n:d * 8 + n + 1]
                    la_f = lder[:, l, 1, d * 8 + n:d * 8 + n + 1]
                    ACT(Bf(b2), Bf(br), AF.Exp, BK(br) + [("lder", l)], BK(b2), scale=la_f, bias=la_f)
                    ACT(Bf(br), Bf(br), AF.Exp, BK(br) + [("lder", l)], BK(br), scale=la_h, bias=la_h)
                for d in range(2):
                    b2 = 3 + 3 * d
                    ACT(Bf(b2), Bf(b2), AF.Sqrt, BK(b2), BK(b2), scale=-0.25, bias=0.25 + 2.5e-7)

            def lru_bx(m):
                for d in range(2):
                    b2, bi_ = 3 + 3 * d, 4 + 3 * d
                    STT(Bf(bi_), Bf(bi_), 1.0, Bf(b2), ALU.add, ALU.mult, BK(bi_) + BK(b2), BK(bi_))
                    TT(Bf(bi_), Bf(bi_), Bf(1), ALU.mult, BK(bi_) + BK(1), BK(bi_))

            def lru_scan(m):
                for d in range(2):
                    br, bi_ = 2 + 3 * d, 4 + 3 * d
                    hb = 0 if d == 0 else 3
                    H = Bf(hb)
                    rk = BK(br) + BK(bi_)
                    if d == 0:
                        scan(H[:, SEQ:T], Bf(br)[:, SEQ:T], Bf(bi_)[:, SEQ:T], 0.0, rk, BK(hb))
                        scan(H[:, 0:SEQ], Bf(br)[:, 0:SEQ], Bf(bi_)[:, 0:SEQ], H[:, T - 1:T], rk + BK(hb), BK(hb))
                    else:
                        scan(H[:, SEQ:T][:, ::-1], Bf(br)[:, SEQ:T][:, ::-1], Bf(bi_)[:, SEQ:T][:, ::-1], 0.0, rk, BK(hb))
                        scan(H[:, 0:SEQ][:, ::-1], Bf(br)[:, 0:SEQ][:, ::-1], Bf(bi_)[:, 0:SEQ][:, ::-1], H[:, SEQ:SEQ + 1], rk + BK(hb), BK(hb))

            def lru_out(m):
                TT(Bf(0), Bf(0), Bf(3), ALU.add, BK(0) + BK(3), BK(0))
                TT(rst, Bf(0), gls[m % 2], ALU.mult, BK(0) + [("gl", m % 2, t) for t in range(5)], ["rst"])
                P.dma("sp", recT_h[b][m * 128:(m + 1) * 128, :], rst, reads=["rst"], writes=[("recT", b, m)])

            for it_ in range(9):
                if it_ < 8:
                    lru_front(it_)
                    for _ in range(3):
                        gate_block()
                if it_ >= 1:
                    lru_bx(it_ - 1)
                if it_ < 8:
                    lru_conv(it_)
                if it_ >= 1:
                    lru_scan(it_ - 1)
                    lru_out(it_ - 1)
                if it_ < 8:
                    lru_gates(it_)
            wb_unpin(i_wi)
            wb_unpin(i_wa)
            if l == 0 and b == 0:
                dump("rec", recT_h[b], [("recT", b, n) for n in range(8)])

            B()
            i_pw = wb_pin()
            pw = load_slab(pool_w[l].rearrange("g c d -> c g d"), i_pw, (4, 128))
            i = wb_take()
            wb = load_slab(w_inP[l][:, WIN_POOL0:WIN_POOL0 + 512].rearrange("(k p) c -> p k c", p=128), i, (KC, 512))
            PW, LB, CB = 2368, 16, 2096
            Pa = [af32(O_R2, PW), af32(O_R2 + 2 * PW, PW)]
            Pb = af32(O_R2 + 4 * PW, PW)
            Pc = af32(O_R2 + 6 * PW, PW)
            tmpe = af32(WK + 3 * T, 16)
            for q in range(2):
                P.op("dve", lambda e, q=q: e.memset(Pa[q], 0.0), writes=[("Pa", q)])
            dbs = [abf(WK, T), abf(WK + 3 * T + 64, T)]

            def pool_front(g):
                U = Pa[g % 2]
                for ti, (t0, tn) in enumerate(tiles):
                    pi = nextps()
                    mm(PS[pi][:, :tn], [(wb[:, k, g * 128:(g + 1) * 128], R1[:, k, t0:t0 + tn]) for k in range(KC)], [("WB", i)] + R1K(ti), ("ps", pi))
                    base = (LB + t0) if ti < 4 else CB
                    ACT(U[:, base:base + tn], PS[pi][:, :tn], AF.Identity, [("ps", pi)], [("Pa", g % 2)])

            def pool_mid(g):
                w = POOL_WIN[g]
                left = w // 2
                right = w - 1 - left
                U = Pa[g % 2]
                db = dbs[g % 2]
                dbk = ("db", g % 2)
                src, skey = U, ("Pa", g % 2)
                m = 1
                lvl = 0
                while m < w:
                    dst, dkey = (Pb, "Pb") if lvl % 2 == 0 else (Pc, "Pc")
                    TT(dst[:, 0:PW - m], src[:, 0:PW - m], src[:, m:PW], ALU.add, [skey], [dkey])
                    src, skey = dst, dkey
                    m *= 2
                    lvl += 1
                Aw, akey = src, skey
                for (base, s0, sn) in ((LB, 0, SEQ), (CB, SEQ, CTXL)):
                    STT(db[:, s0:s0 + sn], Aw[:, base - left:base - left + sn], 1.0 / w, U[:, base:base + sn], ALU.mult, ALU.subtract,
                        [akey, ("Pa", g % 2)], [dbk])
                    TT(tmpe[:, 0:left], Aw[:, base - left:base], ptab[:, g, 0:left], ALU.mult, [akey, "ptab"], ["tmpe"])
                    TT(db[:, s0:s0 + left], tmpe[:, 0:left], U[:, base:base + left], ALU.subtract, ["tmpe", ("Pa", g % 2)], [dbk])
                    if right > 0:
                        TT(tmpe[:, 8:8 + right], Aw[:, base - left + sn - right:base - left + sn], ptab[:, g, 8:8 + right], ALU.mult,
                           [akey, "ptab"], ["tmpe"])
                        TT(db[:, s0 + sn - right:s0 + sn], tmpe[:, 8:8 + right], U[:, base + sn - right:base + sn], ALU.subtract,
                           ["tmpe", ("Pa", g % 2)], [dbk])

            def pool_back(g):
                db = dbs[g % 2]
                pst = abf(WK + T + (g % 2) * T, T)
                for ti, (t0, tn) in enumerate(tiles):
                    pi = nextps()
                    mm(PS[pi][:, :tn], [(pw[:, g, :], db[:, t0:t0 + tn])], [("WB", i_pw), ("db", g % 2)], ("ps", pi))
                    ACT(pst[:, t0:t0 + tn], PS[pi][:, :tn], AF.Identity, [("ps", pi), "vec", ("lder", l)], [("pst", g % 2)],
                        scale=V(l, "psc", g), bias=lder[:, l, 2, g:g + 1])
                P.dma("sp", poolT_h[b][g * 128:(g + 1) * 128, :], pst, reads=[("pst", g % 2)], writes=[("poolT", b, g)])

            pool_front(0)
            for g in range(4):
                if g + 1 < 4:
                    pool_front(g + 1)
                pool_mid(g)
                pool_back(g)
            wb_unpin(i_pw)
            if l == 0 and b == 0:
                dump("pool", poolT_h[b], [("poolT", b, g) for g in range(4)])

            def COPY(out, in_, reads, writes, which):
                if which % 2 == 0:
                    ACT(out, in_, AF.Identity, reads, writes)
                else:
                    P.op("dve", lambda e: e.tensor_copy(out=out, in_=in_), reads=reads, writes=writes)

            B()
            i_uq = wb_pin()
            wuq = load_slab(w_uq2[l].rearrange("(k p) v c -> p k v c", p=128), i_uq, (2, 2, 768))
            i_kv = wb_pin()
            wukv = load_slab(w_ukvP[l], i_kv, (1024,))
            i = wb_take()
            wb = load_slab(w_inP[l][:, WIN_MLA0:WIN_MLA0 + 384].rearrange("(k p) c -> p k c", p=128), i, (KC, 384))
            P.dma("pool", wkr[:, :, 0, 64:96], w_kr2[l][:, 0:32].rearrange("(k p) c -> p k c", p=128), writes=["wkr"], nobar=True)
            P.dma("pool", wkr[:, :, 1, 64:96], w_kr2[l][:, 32:64].rearrange("(k p) c -> p k c", p=128), writes=["wkr"], nobar=True)
            RVK = [("RV", kt) for kt in range(18)]
            P.op("dve", lambda e: e.memset(VV[:, :, :, 64:65], 1.0), writes=RVK + ["RVp"])
            ckv = af32(WK, 512)
            rs = af32(WK + 1024, 512)
            sqk = abf(WK + 2048, 512)
            ckvn = abf(WK + 2560, 512)
            cq = af32(WK + 3072, 2, 512)
            sqq = abf(WK + 5120, 2, 512)
            cqn = abf(WK + 6144, 2, 512)
            Ct = af32(WK + 7168, 512)
            St = af32(WK + 8192, 512)
            t1 = af32(WK + 9216, 512)
            rs2 = rs
            t2 = cq[:, 0, :]
            tq1 = ckv
            tq2 = cq[:, 1, :]
            cw = 0
            for ti, (t0, tn) in enumerate(tiles):
                pi = nextps()
                mm(PS[pi][:, :tn], [(wb[:, k, 0:128], R1[:, k, t0:t0 + tn]) for k in range(KC)], [("WB", i)] + R1K(ti), ("ps", pi))
                ACT(ckv[:, :tn], PS[pi][:, :tn], AF.Identity, [("ps", pi)], ["ckv"])
                ACT(sqk[:, :tn], PS[pi][:, :tn], AF.Square, [("ps", pi)], ["sqk"])
                for c2 in range(2):
                    p5 = nextps()
                    mm(PS[p5][:, :tn], [(wb[:, k, 128 + c2 * 128:256 + c2 * 128], R1[:, k, t0:t0 + tn]) for k in range(KC)],
                       [("WB", i)] + R1K(ti), ("ps", p5))
                    ACT(cq[:, c2, :tn], PS[p5][:, :tn], AF.Identity, [("ps", p5)], [("cq", c2)])
                    ACT(sqq[:, c2, :tn], PS[p5][:, :tn], AF.Square, [("ps", p5)], ["sqq"])
                pa = nextps()
                mm(PS[pa][0:96, :tn], [(wkr[:, k, 0, :], R1[:, k, t0:t0 + tn]) for k in range(KC)], ["wkr"] + R1K(ti), ("ps", pa))
                pb_ = nextps()
                mm(PS[pb_][0:96, :tn], [(wkr[:, k, 1, :], R1[:, k, t0:t0 + tn]) for k in range(KC)], ["wkr"] + R1K(ti), ("ps", pb_))
                P.dma("sp", Ct[64:96, :tn], ropeC_d[:, t0:t0 + tn], writes=["Ct"])
                P.dma("sp", St[64:96, :tn], ropeS_d[:, t0:t0 + tn], writes=["St"])
                p2 = nextps()
                mm(PS[p2][:, :tn], [(ones_bf[:], sqk[:, :tn])], ["sqk", "ones_bf"], ("ps", p2))
                ACT(rs[:, :tn], PS[p2][:, :tn], AF.Ln, [("ps", p2)], ["rs"], bias=EPS, scale=1.0 / 128)
                ACT(rs[:, :tn], rs[:, :tn], AF.Exp, ["rs"], ["rs"], scale=-0.5)
                STT(ckvn[:, :tn], ckv[:, :tn], V(l, "kvng", 0), rs[:, :tn], ALU.mult, ALU.mult, ["ckv", "rs", "vec"], ["ckvn"])
                p6 = nextps()
                mm(PS[p6][:, :tn], [(ones_bf[:], sqq[:, 0, :tn]), (ones_bf[:], sqq[:, 1, :tn])], ["sqq", "ones_bf"], ("ps", p6))
                ACT(rs2[:, :tn], PS[p6][:, :tn], AF.Ln, [("ps", p6)], ["rs"], bias=EPS, scale=1.0 / 256)
                ACT(rs2[:, :tn], rs2[:, :tn], AF.Exp, ["rs"], ["rs"], scale=-0.5, bias=LN_SM)
                for c2 in range(2):
                    STT(cqn[:, c2, :tn], cq[:, c2, :tn], V(l, "qng", c2), rs2[:, :tn], ALU.mult, ALU.mult, [("cq", c2), "rs", "vec"], ["cqn"])
                TT(t1[64:96, :tn], PS[pa][64:96, :tn], Ct[64:96, :tn], ALU.mult, [("ps", pa), "Ct"], ["t1"])
                TT(t2[64:96, :tn], PS[pb_][64:96, :tn], St[64:96, :tn], ALU.mult, [("ps", pb_), "St", "cqn"], [("cq", 0)])
                TT(t1[64:96, :tn], t1[64:96, :tn], t2[64:96, :tn], ALU.add, ["t1", ("cq", 0)], ["t1"])
                if l == 0 and b == 0:
                    dump("kr", t1[64:96, :tn], ["t1"], sl=(t0, t0 + tn))
                for h in range(8):
                    COPY(R2[64:96, h, t0:t0 + tn], t1[64:96, :tn], ["t1"], [("R2", h, ti)], cw)
                    cw += 1
                for h in range(8):
                    p3 = nextps()
                    mm(PS[p3][0:64, :tn], [(wukv[:, h * 64:(h + 1) * 64], ckvn[:, :tn])], [("WB", i_kv), "ckvn"], ("ps", p3))
                    COPY(R2[0:64, h, t0:t0 + tn], PS[p3][0:64, :tn], [("ps", p3)], [("R2", h, ti)], cw)
                    cw += 1
                for sub in range(tn // 128):
                    kt = t0 // 128 + sub
                    p4 = nextps()
                    mm(PS[p4][:, 0:512], [(ckvn[:, sub * 128:(sub + 1) * 128], wukv[:, 512:1024])], [("WB", i_kv), "ckvn"], ("ps", p4))
                    COPY(VV[:, kt, :, 0:64], PS[p4][:, 0:512].rearrange("p (h d) -> p h d", h=8), [("ps", p4)], [("RV", kt), "RVp"], cw)
                    cw += 1
                for h in range(8):
                    pA = nextps()
                    mm(PS[pA][0:96, :tn], [(wuq[:, k, 0, h * 96:(h + 1) * 96], cqn[:, k, :tn]) for k in range(2)], [("WB", i_uq), "cqn"], ("ps", pA))
                    pB = nextps()
                    mm(PS[pB][0:96, :tn], [(wuq[:, k, 1, h * 96:(h + 1) * 96], cqn[:, k, :tn]) for k in range(2)], [("WB", i_uq), "cqn"], ("ps", pB))
                    ACT(R3[0:64, h, t0:t0 + tn], PS[pA][0:64, :tn], AF.Identity, [("ps", pA)], [("R3", h, ti)])
                    TT(tq1[64:96, :tn], PS[pA][64:96, :tn], Ct[64:96, :tn], ALU.mult, [("ps", pA), "Ct"], ["ckv"])
                    TT(tq2[64:96, :tn], PS[pB][64:96, :tn], St[64:96, :tn], ALU.mult, [("ps", pB), "St"], [("cq", 1)])
                    TT(R3[64:96, h, t0:t0 + tn], tq1[64:96, :tn], tq2[64:96, :tn], ALU.add, ["ckv", ("cq", 1)], [("R3", h, ti)])
            wb_unpin(i_kv)
            wb_unpin(i_uq)

            if l == 0 and b == 0:
                for h in range(8):
                    dump("K", R2[0:96, h, :], [("R2", h, t) for t in range(5)], idx=h)
                    dump("Q", R3[0:96, h, :], [("R3", h, t) for t in range(5)], idx=h)
                dump("Vv", AR[:, O_RV:O_RV + VSZ], RVK)
            B()
            PTs = [abf(WK + q * 1024, 1024) for q in range(3)]
            rc = af32(WK + 3072, 1024)
            osb = af32(WK + 5120, 1024)
            sctr = 0
            groups = [[0, 1], [2, 3]] + ([] if skipc else [[4]])
            for h in range(8):
                for qis in groups:
                    kts = list(range(18)) if qis[0] < 4 else [16, 17]
                    nk = len(kts)
                    W = sum(tiles[qi][1] for qi in qis)
                    pend = []
                    for step in range(nk + 1):
                        if step < nk:
                            kt = kts[step]
                            sb = sctr % 3
                            sctr += 1
                            for jq, qi in enumerate(qis):
                                q0, qn = tiles[qi]
                                P.op("pe", lambda e, sb=sb, jq=jq, qn=qn, q0=q0, kt=kt, h=h: e.matmul(
                                    PSALL[:, sb * 1024 + jq * 512:sb * 1024 + jq * 512 + qn], R2[0:96, h, kt * 128:(kt + 1) * 128],
                                    R3[0:96, h, q0:q0 + qn], start=True, stop=True),
                                    reads=[("R2", h, kt // 4), ("R3", h, qi)], writes=[("ps", 2 * sb), ("ps", 2 * sb + 1)])
                            ACT(PTs[sb][:, :W], PSALL[:, sb * 1024:sb * 1024 + W], AF.Exp, [("ps", 2 * sb), ("ps", 2 * sb + 1)], [("PT", sb)])
                            pend.append((kt, sb))
                        if step > 0:
                            kt, sb = pend.pop(0)
                            for jq, qi in enumerate(qis):
                                q0, qn = tiles[qi]
                                P.op("pe", lambda e, kt=kt, sb=sb, step=step, jq=jq, h=h, qn=qn, nk=nk: e.matmul(
                                    PS[6 + jq][0:65, :qn], VV[:, kt, h, 0:65], PTs[sb][:, jq * 512:jq * 512 + qn], start=(step == 1), stop=(step == nk)),
                                    reads=[("RV", kt), ("PT", sb)], writes=[("ps", 6 + jq)])
                    for jq, qi in enumerate(qis):
                        q0, qn = tiles[qi]
                        cs = slice(jq * 512, jq * 512 + qn)
                        ACT(rc[64:65, cs], PS[6 + jq][64:65, :qn], AF.Ln, [("ps", 6 + jq)], [("rc", jq)])
                        ACT(rc[64:65, cs], rc[64:65, cs], AF.Exp, [("rc", jq)], [("rc", jq)], scale=-1.0)
                        P.op("dve", lambda e, jq=jq, qn=qn, cs=cs: e.tensor_copy(out=osb[0:64, cs], in_=PS[6 + jq][0:64, :qn]),
                             reads=[("ps", 6 + jq)], writes=[("osb", jq)])
                        sb = sctr % 3
                        sctr += 1
                        P.op("pe", lambda e, sb=sb, qn=qn, cs=cs: e.matmul(PSALL[0:64, sb * 1024:sb * 1024 + qn], ones_f[64:65, 0:64], rc[64:65, cs], start=True, stop=True),
                             reads=[("rc", jq), "ones_f"], writes=[("ps", 2 * sb), ("ps", 2 * sb + 1)])
                        TT(R1[0:64, h, q0:q0 + qn], osb[0:64, cs], PSALL[0:64, sb * 1024:sb * 1024 + qn], ALU.mult, [("osb", jq), ("ps", 2 * sb), ("ps", 2 * sb + 1)], [("R1", h, qi)])
            if l == 0 and b == 0:
                for h in range(8):
                    dump("att", R1[0:64, h, :], [("R1", h, t) for t in range(5)], idx=h)

            B()
            R3ALL = [("R3", k, t) for k in range(KC) for t in range(5)]
            P.dma("sp", R3, recT_h[b].rearrange("(k p) t -> p k t", p=128), reads=[("recT", b, n) for n in range(8)], writes=R3ALL)
            P.dma("sp", PLT, poolT_h[b].rearrange("(g p) t -> p g t", p=128), reads=[("poolT", b, g) for g in range(4)], writes=["RVp"] + RVK)

            gts = [abf(WK + q * 1536, 3, 512) for q in range(3)]
            tAs = [af32(WK + 4608 + q * 2048, 512) for q in range(2)]
            tBs = [af32(WK + 5632 + q * 2048, 512) for q in range(2)]
            gcnt = 0
            for grp in range(2):
                ia = wb_take()
                wa = load_slab(proj_mla[l][:, grp * 512:(grp + 1) * 512].rearrange("(h d) n -> d h n", d=64), ia, (8, 512), pslice=(0, 64))
                ir = wb_take()
                wr = load_slab(proj_lru[l][:, grp * 512:(grp + 1) * 512].rearrange("(k p) n -> p k n", p=128), ir, (8, 512))
                ip = wb_take()
                wp = load_slab(proj_pool[l][:, grp * 512:(grp + 1) * 512].rearrange("(g p) n -> p g n", p=128), ip, (4, 512))
                for nn in range(4):
                    n = grp * 4 + nn
                    for ti, (t0, tn) in enumerate(tiles_e):
                        gq = gcnt % 3
                        gt = gts[gq]
                        P.dma("sp", gt[:, :, :tn], gsc_h[b][n, :, :, t0:t0 + tn], reads=[("gsc", b, n, j) for j in range(3)], writes=[("gt", gq)])
                        pA = nextps()
                        mm(PS[pA][:, :tn], [(wa[0:64, h, nn * 128:(nn + 1) * 128], R1[0:64, h, t0:t0 + tn]) for h in range(8)],
                           [("WB", ia)] + R1K(ti), ("ps", pA))
                        pR = nextps()
                        mm(PS[pR][:, :tn], [(wr[:, k, nn * 128:(nn + 1) * 128], R3[:, k, t0:t0 + tn]) for k in range(KC)],
                           [("WB", ir)] + [("R3", k, ti) for k in range(KC)], ("ps", pR))
                        pP = nextps()
                        mm(PS[pP][:, :tn], [(wp[:, g, nn * 128:(nn + 1) * 128], PLT[:, g, t0:t0 + tn]) for g in range(4)],
                           [("WB", ip), "RVp"], ("ps", pP))
                        par = gcnt % 2
                        tA = tAs[par][:, :tn]
                        tB = tBs[par][:, :tn]
                        STT(tA, gt[:, 0, :tn], 1.0, PS[pA][:, :tn], ALU.add, ALU.mult, [("ps", pA), ("gt", gq)], [("tA", par)])
                        STT(tB, gt[:, 1, :tn], 1.0, PS[pR][:, :tn], ALU.add, ALU.mult, [("ps", pR), ("gt", gq)], [("tB", par)])
                        TT(tA, tA, tB, ALU.add, [("tA", par), ("tB", par)], [("tA", par)])
                        STT(tB, gt[:, 2, :tn], 1.0, PS[pP][:, :tn], ALU.add, ALU.mult, [("ps", pP), ("gt", gq)], [("tB", par)])
                        TT(R2[:, n, t0:t0 + tn], tA, tB, ALU.add, [("tA", par), ("tB", par)], [("R2", n, ti)])
                        gcnt += 1

            B()
            i0 = wb_pin()
            wo0 = load_slab(w_out[l][:, 0:512].rearrange("(k p) n -> p k n", p=128), i0, (KC, 512))
            i1 = wb_pin()
            wo1 = load_slab(w_out[l][:, 512:1024].rearrange("(k p) n -> p k n", p=128), i1, (KC, 512))
            wos = [(wo0, i0), (wo1, i1)]
            xts = [af32(O_R3 + q * 8192, KC, 512) for q in range(2)]
            sqs = [abf(WK + 3072 + q * 512, 512) for q in range(4)]
            XK = lambda q: [("xt", q, k) for k in range(KC)]
            for ti, (t0, tn) in enumerate(tiles_e):
                v = 2 if ti == 4 else b
                q = ti % 2
                xt = xts[q][:, :, :tn]
                P.dma("sp", xt, xsrc[b][:, t0:t0 + tn].rearrange("(k p) t -> p k t", p=128), reads=([("xs", b, ti)] if l > 0 else []), writes=XK(q))
                pend_sq = []

                def flush_sq(pend_sq=pend_sq, tn=tn):
                    sq_, n2_ = pend_sq.pop(0)
                    P.op("pe", lambda e: e.matmul(PS[7][:, :tn], ones_bf[:], sq_, start=(n2_ == 0), stop=(n2_ == 7)),
                         reads=[("sq", n2_ % 4), "ones_bf"], writes=[("ps", 7)])

                for n2 in range(8):
                    pi = nextps()
                    wo, iw = wos[n2 // 4]
                    mm(PS[pi][:, :tn], [(wo[:, k, (n2 % 4) * 128:(n2 % 4 + 1) * 128], R2[:, k, t0:t0 + tn]) for k in range(KC)],
                       [("WB", iw)] + [("R2", k, ti) for k in range(KC)], ("ps", pi))
                    if len(pend_sq) >= 2:
                        flush_sq()
                    STT(xt[:, n2, :], PS[pi][:, :tn], hg1[:, l, v, n2:n2 + 1], xt[:, n2, :], ALU.mult, ALU.add,
                        [("ps", pi), ("xt", q, n2), ("der", l)], [("xt", q, n2)])
                    sq = sqs[n2 % 4][:, :tn]
                    ACT(sq, xt[:, n2, :], AF.Square, [("xt", q, n2)], [("sq", n2 % 4)])
                    pend_sq.append((sq, n2))
                while pend_sq:
                    flush_sq()
                P.dma("sp", xs[b][:, t0:t0 + tn].rearrange("(k p) t -> p k t", p=128), xt, reads=XK(q), writes=[("xs", b, ti)])
                norm_tile(l, v, 1, xt, tn, 7, lambda k, t0=t0, tn=tn: R1[:, k, t0:t0 + tn], lambda k, ti=ti: ("R1", k, ti),
                          lambda k, q=q: [("xt", q, k)])
            wb_unpin(i1)
            wb_unpin(i0)
            if l == 0 and b == 0:
                dump("x1", xs[b], [("xs", b, t) for t in range(len(tiles_e))])
                for k in range(KC):
                    dump("h2", R1[:, k, :], [("R1", k, t) for t in range(len(tiles_e))], idx=k)

            B()
            G0s = [af32(O_R3 + q * 4608, T) for q in range(2)]
            C0s = [af32(O_R3 + (2 + q) * 4608, T) for q in range(2)]
            fd = abf(O_R2, NJ, 1024)

            def load_fdA():
                for hh in range(2):
                    P.dma("pool", fd[:, 0:18, hh * 512:(hh + 1) * 512], ffn_down[l][0:18 * 128, hh * 512:(hh + 1) * 512].rearrange("(j p) n -> p j n", p=128),
                          writes=[("fdA", hh)])
            asts = [abf(WK + q * T, T) for q in range(2)]
            TE = tiles_e[-1][0] + tiles_e[-1][1]
            do_ada = (b == 0 and l + 1 < nlayer)
            ada_q = []
            ada_next = [0]

            def ada_step():
                if not do_ada:
                    return
                if ada_q:
                    ada_slab_mm(l + 1, *ada_q.pop(0))
                if ada_next[0] < 12:
                    ii, wbb = ada_slab_load(l + 1, ada_next[0])
                    ada_q.append((ada_next[0], ii, wbb))
                    ada_next[0] += 1

            for s in range(11):
                i = wb_take()
                wb = load_slab(ffn_upP[l][:, s * 512:(s + 1) * 512].rearrange("(k p) c -> p k c", p=128), i, (KC, 512))
                ada_step()
                if s == 5:
                    ada_step()
                if s == 2:
                    load_fdA()
                for cc in range(2):
                    j = 2 * s + cc
                    q = j % 2
                    G0, C0, ast = G0s[q], C0s[q], asts[q]
                    for ti, (t0, tn) in enumerate(tiles_e):
                        pi = nextps()
                        mm(PS[pi][:, :tn], [(wb[:, k, (2 * cc + 1) * 128:(2 * cc + 2) * 128], R1[:, k, t0:t0 + tn]) for k in range(KC)],
                           [("WB", i)] + R1K(ti), ("ps", pi))
                        ACT(G0[:, t0:t0 + tn], PS[pi][:, :tn], AF.Identity, [("ps", pi)], [("G0", q, ti)])
                    GK = [("G0", q, t) for t in range(5)]
                    CK = [("C0", q, t) for t in range(5)]
                    for (s0, sn) in segs_e:
                        TS(C0[:, s0:s0 + sn], G0[:, s0:s0 + sn], V(l, "fcw", 22 + j), V(l, "fcb", j), ALU.mult, ALU.add, GK + ["vec"], CK)
                        for k in (0, 2):
                            o = k - 1
                            a = max(0, -o)
                            e_ = sn - max(0, o)
                            STT(C0[:, s0 + a:s0 + e_], G0[:, s0 + a + o:s0 + e_ + o], V(l, "fcw", k * 22 + j), C0[:, s0 + a:s0 + e_],
                                ALU.mult, ALU.add, GK + CK + ["vec"], CK)
                    ACT(C0[:, 0:TE], C0[:, 0:TE], AF.Silu, CK, CK)
                    for ti, (t0, tn) in enumerate(tiles_e):
                        pi = nextps()
                        mm(PS[pi][:, :tn], [(wb[:, k, 2 * cc * 128:(2 * cc + 1) * 128], R1[:, k, t0:t0 + tn]) for k in range(KC)],
                           [("WB", i)] + R1K(ti), ("ps", pi))
                        TT(ast[:, t0:t0 + tn], PS[pi][:, :tn], C0[:, t0:t0 + tn], ALU.mult, [("ps", pi)] + CK, [("ast", q)])
                    for ti, (t0, tn) in enumerate(tiles_e):
                        P.dma("sp", actT_h[b][ti][:, j, 0:tn], ast[:, t0:t0 + tn], reads=[("ast", q)], writes=[("actT", b, j, ti)])

            if do_ada:
                while ada_q:
                    ada_slab_mm(l + 1, *ada_q.pop(0))
                ada_finish(l + 1)

            B()
            for hh in range(2):
                P.dma("pool", fd[:, 18:NJ, hh * 512:(hh + 1) * 512], ffn_down[l][18 * 128:NJ * 128, hh * 512:(hh + 1) * 512].rearrange("(j p) n -> p j n", p=128),
                      writes=[("fdB", hh)])
            ats = [abf(O_R1, NJ, 512), abf(O_R2 + NJ * 1024, NJ, 512)]
            xts = [af32(O_RV, KC, 512), af32(O_WK, KC, 512)]
            sqs = [abf(WK + 8192 + q * 512, 512) for q in range(2)]
            rsf = af32(WK + 9216, 512)
            for ti, (t0, tn) in enumerate(tiles_e):
                v = 2 if ti == 4 else b
                q = ti % 2
                at = ats[q][:, :, :tn]
                xt = xts[q][:, :, :tn]
                P.dma("pool", at, actT_h[b][ti][:, :, 0:tn], reads=[("actT", b, j, ti) for j in range(NJ)], writes=[("at", q)])
                P.dma("sp", xt, xs[b][:, t0:t0 + tn].rearrange("(k p) t -> p k t", p=128), reads=[("xs", b, ti)], writes=XK(q))
                for n2 in range(8):
                    pi = nextps()
                    mm(PS[pi][:, :tn], [(fd[:, j, n2 * 128:(n2 + 1) * 128], at[:, j, :]) for j in range(NJ)], [("fdA", n2 // 4), ("fdB", n2 // 4), ("at", q)], ("ps", pi))
                    STT(xt[:, n2, :], PS[pi][:, :tn], MOD(l, v, 5, n2), xt[:, n2, :], ALU.mult, ALU.add,
                        [("ps", pi), ("xt", q, n2), ("modt", l)], [("xt", q, n2)])
                    if last and ti < 4:
                        sq = sqs[n2 % 2][:, :tn]
                        ACT(sq, xt[:, n2, :], AF.Square, [("xt", q, n2)], [("sq", n2 % 2)])
                        P.op("pe", lambda e, sq=sq, tn=tn, n2=n2: e.matmul(PS[7][:, :tn], ones_bf[:], sq, start=(n2 == 0), stop=(n2 == 7)),
                             reads=[("sq", n2 % 2), "ones_bf"], writes=[("ps", 7)])
                if not last or "x2" in dbg_out:
                    P.dma("sp" if last else "act", xs[b][:, t0:t0 + tn].rearrange("(k p) t -> p k t", p=128), xt, reads=XK(q), writes=[("xs", b, ti)])
                if last and ti < 4:
                    rs_ = rsf[:, :tn]
                    ACT(rs_, PS[7][:, :tn], AF.Ln, [("ps", 7)], ["rsf"], bias=EPS, scale=1.0 / D)
                    ACT(rs_, rs_, AF.Exp, ["rsf"], ["rsf"], scale=-0.5)
                    for k in range(KC):
                        STT(xt[:, k, :], xt[:, k, :], V(l, "fng", k), rs_, ALU.mult, ALU.mult, [("xt", q, k), "rsf", "vec"], [("xt", q, k)])
                    P.dma("sp", yT[b][:, t0:t0 + tn].rearrange("(k p) t -> p k t", p=128), xt, reads=XK(q), writes=[("yT", b, ti)])
            if l == 0 and b == 0:
                dump("x2", xs[b], [("xs", b, t) for t in range(len(tiles_e))])

        for l in range(nlayer):
            for b in range(nb):
                try:
                    layer_batch(l, b)
                except Exception as ex:
                    if type(ex).__name__ != "_Stop":
                        raise
        P.barrier(bscr[:, 0:1])
        P.emit()
    return nc


def _colify(a):
    a = np.asarray(a, np.float32)
    lead = a.shape[:-1]
    n = a.shape[-1] // 128
    a = a.reshape(*lead, n, 128)
    a = np.moveaxis(a, -1, 0)
    return a.reshape(128, -1)


def prep_shared(inp, nlayer=NLAYER):
    f = lambda k: np.asarray(inp[k], np.float32)
    vecs = np.zeros((128, nlayer, NV), np.float32)
    for l in range(nlayer):
        def put(name, arr):
            c = _colify(arr)
            vecs[:, l, VOFF[name]:VOFF[name] + c.shape[1]] = c
        put("n1g", f("norm1_g")[l]); put("n2g", f("norm2_g")[l]); put("adab", f("ada_b")[l]); put("qng", f("q_norm_g")[l])
        put("kvng", f("kv_norm_g")[l]); put("lcw", f("lru_conv_w")[l]); put("lcb", f("lru_conv_b")[l]); put("lba", f("lru_ba")[l])
        put("lbi", f("lru_bi")[l]); put("llam", f("lru_lambda")[l]); put("pb", f("pool_b")[l]); put("psc", f("pool_scale")[l])
        put("fcw", f("ffn_conv_w")[l]); put("fcb", f("ffn_conv_b")[l]); put("fng", f("final_norm_g"))
    half = 16
    inv = (10000.0 ** (-np.arange(0, half, 2, dtype=np.float32) / half)).astype(np.float32)
    t = np.arange(SEQ)
    ang_r = (t // 64).astype(np.float32)[:, None] * inv
    ang_c = (t % 64).astype(np.float32)[:, None] * inv
    cr, sr, cc, sc = np.cos(ang_r).T, np.sin(ang_r).T, np.cos(ang_c).T, np.sin(ang_c).T
    ropeC = np.ones((32, T), np.float32)
    ropeS = np.zeros((32, T), np.float32)
    ropeC[:, :SEQ] = np.concatenate([cr, cr, cc, cc], 0)
    ropeS[:, :SEQ] = np.concatenate([-sr, sr, -sc, sc], 0)
    ptab = np.ones((128, 4, 16), np.float32)
    for g, w in enumerate(POOL_WIN):
        left = w // 2
        right = w - 1 - left
        for tt in range(left):
            ptab[:, g, tt] = 1.0 / (tt + right + 1)
        for i in range(right):
            ptab[:, g, 8 + i] = 1.0 / (right - i + left)
    w_in = f("w_in")[:nlayer]
    perm = _win_perm()
    w_uq = f("w_uq")[:nlayer]
    w_uq_sw = w_uq.copy()
    for h in range(8):
        w_uq_sw[:, :, h * 96 + 64:h * 96 + 96] = w_uq[:, :, h * 96 + 64 + KR_SWAP]
    w_ukv = f("w_ukv")[:nlayer].reshape(nlayer, 128, 8, 128)
    ffn_up = f("ffn_up")[:nlayer]
    upcols = []
    for j in range(NJ):
        upcols += list(range(j * 128, (j + 1) * 128)) + list(range(DFF + j * 128, DFF + (j + 1) * 128))
    kr = w_in[:, :, COL_KV:COL_KR]
    sh = dict(
        vecs=vecs, ropeC=ropeC, ropeS=ropeS, ptab=ptab,
        ada_w=np.ascontiguousarray(f("ada_w")[:nlayer]),
        w_inP=np.ascontiguousarray(w_in[:, :, perm]),
        w_kr2=np.ascontiguousarray(np.concatenate([kr, kr[:, :, KR_SWAP]], -1)),
        w_uq2=np.ascontiguousarray(np.stack([w_uq, w_uq_sw], 2)),
        w_ukvP=np.ascontiguousarray(np.concatenate([w_ukv[..., :64].reshape(nlayer, 128, 512), w_ukv[..., 64:].reshape(nlayer, 128, 512)], -1)),
        lru_wa=f("lru_wa")[:nlayer], lru_wi=f("lru_wi")[:nlayer], pool_w=f("pool_w")[:nlayer],
        proj_mla=f("proj_mla")[:nlayer], proj_lru=f("proj_lru")[:nlayer], proj_pool=f("proj_pool")[:nlayer],
        w_out=f("w_out")[:nlayer], ffn_upP=np.ascontiguousarray(ffn_up[:, :, np.array(upcols)]), ffn_down=f("ffn_down")[:nlayer],
    )
    return sh


def prep_core(inp, batches):
    x = np.asarray(inp["x"], np.float32)
    ctx = np.asarray(inp["ctx"], np.float32)
    c = np.asarray(inp["c"], np.float32)
    xT = np.stack([np.concatenate([x[b].T, ctx[b].T], axis=1) for b in batches], 0)
    cv = [c[batches[0]], c[batches[-1]], np.asarray(inp["c_ctx"], np.float32)]
    cT = np.stack([_colify(v)[:, :] for v in cv], -1)
    return dict(xT=np.ascontiguousarray(xT), cT=np.ascontiguousarray(cT))


_CACHE = {}


def kernel(**inputs):
    if "nc" not in _CACHE:
        _CACHE["nc"] = build_program()
    nc = _CACHE["nc"]
    sh = prep_shared(inputs)
    in_maps = []
    for core in range(NCORES):
        m = dict(sh)
        m.update(prep_core(inputs, [2 * core, 2 * core + 1]))
        in_maps.append(m)
    res = run_bass_kernel_spmd(nc, in_maps, core_ids=list(range(NCORES)))
    out = np.empty((16, SEQ, D), np.float32)
    for core in range(NCORES):
        y = np.asarray(res.results[core]["yT"])
        out[2 * core] = y[0].T
        out[2 * core + 1] = y[1].T
    return out
```

```python
import contextlib
import numpy as np
import concourse.bass as bass
import concourse.mybir as mybir
from concourse.bass_utils import run_bass_kernel_spmd

F32 = mybir.dt.float32
BF16 = mybir.dt.bfloat16
ALU = mybir.AluOpType
AF = mybir.ActivationFunctionType

NDSEM = 24
SAME_ENG_SYNC = True


class Op:
    __slots__ = ("eng", "fn", "dma", "deps", "need_inc", "count", "sem_i", "sem_val")

    def __init__(self, eng, fn, dma):
        self.eng = eng
        self.fn = fn
        self.dma = dma
        self.deps = []
        self.need_inc = False
        self.count = 0
        self.sem_i = 0
        self.sem_val = 0


class Prog:
    ENGS = ("pe", "act", "dve", "pool", "sp")

    def __init__(self, nc):
        self.nc = nc
        self.ops = {e: [] for e in self.ENGS}
        self.writers = {}
        self.readers = {}
        self.ndma = {e: 0 for e in self.ENGS}
        self.bar = None
        self.pending_dma = []

    def op(self, eng, fn, reads=(), writes=(), dma=False, nobar=False):
        o = Op(eng, fn, dma)
        deps = {}
        for k in reads:
            for w in self.writers.get(k, ()):
                deps[id(w)] = w
        for k in writes:
            for w in self.writers.get(k, ()):
                deps[id(w)] = w
            for r in self.readers.get(k, ()):
                deps[id(r)] = r
        if eng == "pe":
            nobar = True
        if self.bar is not None and not nobar:
            deps[id(self.bar)] = self.bar
        for d in deps.values():
            if d is o:
                continue
            if (not d.dma) and d.eng == eng and (not dma):
                if eng == "pe" or not SAME_ENG_SYNC:
                    continue
            o.deps.append(d)
            d.need_inc = True
        for k in reads:
            lst = self.readers.setdefault(k, [])
            if not dma:
                lst[:] = [r for r in lst if r.dma or r.eng != eng]
            lst.append(o)
        for k in writes:
            self.writers[k] = [o]
            self.readers[k] = []
        if dma:
            i = self.ndma[eng]
            self.ndma[eng] += 1
            o.sem_i = i % NDSEM
            o.sem_val = 16 * (i // NDSEM + 1)
            if not nobar:
                self.pending_dma.append(o)
        self.ops[eng].append(o)
        return o

    def dma(self, eng, out, in_, reads=(), writes=(), nobar=False):
        return self.op(eng, lambda e: e.dma_start(out=out, in_=in_), reads, writes, dma=True, nobar=nobar)

    def barrier(self, scratch):
        o = Op("dve", lambda e: e.memset(scratch, 0.0), False)
        for e in ("pe", "act", "pool"):
            for d in reversed(self.ops[e]):
                if not d.dma:
                    o.deps.append(d)
                    d.need_inc = True
                    break
        for d in reversed(self.ops["dve"]):
            if not d.dma:
                if SAME_ENG_SYNC:
                    o.deps.append(d)
                    d.need_inc = True
                break
        if self.bar is not None:
            o.deps.append(self.bar)
        for d in self.pending_dma:
            o.deps.append(d)
        self.pending_dma = []
        o.need_inc = True
        self.ops["dve"].append(o)
        self.bar = o
        return o

    def emit(self):
        nc = self.nc
        with contextlib.ExitStack() as st:
            csem = {e: st.enter_context(nc.semaphore("c_" + e)) for e in ("pe", "act", "dve", "pool")}
            dsem = {e: [st.enter_context(nc.semaphore("d_%s%d" % (e, i))) for i in range(NDSEM)]
                    for e in self.ENGS if self.ndma[e] > 0}
            for e, lst in self.ops.items():
                c = 0
                for o in lst:
                    if o.dma:
                        continue
                    if o.need_inc:
                        c += 1
                        o.count = c
            block = st.enter_context(nc.Block())

            def run(ename, handle):
                waited = {}

                def wait(sem, val):
                    key = id(sem)
                    if waited.get(key, 0) >= val:
                        return
                    waited[key] = val
                    handle.wait_ge(sem, val)

                for o in self.ops[ename]:
                    for d in o.deps:
                        if d.dma:
                            wait(dsem[d.eng][d.sem_i], d.sem_val)
                        else:
                            wait(csem[d.eng], d.count)
                    if o.dma:
                        if o.sem_val > 16:
                            wait(dsem[ename][o.sem_i], o.sem_val - 16)
                        o.fn(handle).then_inc(dsem[ename][o.sem_i], 16)
                    else:
                        ins = o.fn(handle)
                        if o.need_inc:
                            ins.then_inc(csem[ename], 1)
                if ename in dsem:
                    n = self.ndma[ename]
                    for i in range(min(n, NDSEM)):
                        cnt = (n - 1 - i) // NDSEM + 1
                        wait(dsem[ename][i], 16 * cnt)

            if self.ops["pe"]:
                @block.tensor
                def _(eng):
                    run("pe", eng)
            if self.ops["act"]:
                @block.scalar
                def _(eng):
                    run("act", eng)
            if self.ops["dve"]:
                @block.vector
                def _(eng):
                    run("dve", eng)
            if self.ops["pool"]:
                @block.gpsimd
                def _(eng):
                    run("pool", eng)
            if self.ops["sp"]:
                @block.sync
                def _(eng):
                    run("sp", eng)


D = 1024
KC = 8
SEQ = 2048
CTXL = 256
T = SEQ + CTXL
NLAYER = 4
NCORES = 8
EPS = 1e-6
SM_SCALE = 96 ** -0.5
LN_SM = -0.5 * float(np.log(96.0))
TILES = [(0, 512), (512, 512), (1024, 512), (1536, 512), (2048, 256)]
SEGS = [(0, 2048), (2048, 256)]
DFF = 2816
NJ = 22
POOL_WIN = (2, 4, 8, 16)

VOFF = {}
_o = 0
for _n, _w in (("n1g", 8), ("n2g", 8), ("adab", 48), ("qng", 2), ("kvng", 1), ("lcw", 32), ("lcb", 8), ("lba", 16), ("lbi", 16),
               ("llam", 16), ("pb", 4), ("psc", 4), ("fcw", 66), ("fcb", 22), ("fng", 8)):
    VOFF[_n] = _o
    _o += _w
NV = _o

COL_KV, COL_KR, COL_UX, COL_Q, COL_UY, COL_POOL = 128, 160, 1184, 1440, 2464, 2976


def _win_perm():
    cols = []
    for n in range(8):
        cols += list(range(COL_KR + n * 128, COL_KR + (n + 1) * 128))
        cols += list(range(COL_Q + n * 128, COL_Q + (n + 1) * 128))
    cols += list(range(COL_UY, COL_POOL))
    cols += list(range(COL_POOL, COL_POOL + 3072))
    cols += list(range(0, 128))
    cols += list(range(COL_UX, COL_Q))
    return np.array(cols)


WIN_LRU0, WIN_POOL0, WIN_GATE0, WIN_MLA0, WIN_NCOL = 0, 2048, 2560, 5632, 6016
KR_SWAP = np.array(list(range(8, 16)) + list(range(0, 8)) + list(range(24, 32)) + list(range(16, 24)))


def build_program(nlayer=NLAYER, nb=2, dbg=None, full_last=False, stop_after=None):
    nc = bass.Bass("TRN2", target_bir_lowering=False)
    dt_in = lambda name, shape: nc.dram_tensor(name, shape, F32, kind="ExternalInput").ap()
    xT = dt_in("xT", [nb, D, T])
    cT = dt_in("cT", [128, KC, 3])
    vecs_d = dt_in("vecs", [128, nlayer, NV])
    ropeC_d = dt_in("ropeC", [32, T])
    ropeS_d = dt_in("ropeS", [32, T])
    ptab_d = dt_in("ptab", [128, 4, 16])
    ada_w = dt_in("ada_w", [nlayer, D, 6 * D])
    w_inP = dt_in("w_inP", [nlayer, D, WIN_NCOL])
    w_kr2 = dt_in("w_kr2", [nlayer, D, 64])
    w_uq2 = dt_in("w_uq2", [nlayer, 256, 2, 768])
    w_ukvP = dt_in("w_ukvP", [nlayer, 128, 1024])
    lru_wa = dt_in("lru_wa", [nlayer, 2, 8, 128, 128])
    lru_wi = dt_in("lru_wi", [nlayer, 2, 8, 128, 128])
    pool_w = dt_in("pool_w", [nlayer, 4, 128, 128])
    proj_mla = dt_in("proj_mla", [nlayer, 512, D])
    proj_lru = dt_in("proj_lru", [nlayer, D, D])
    proj_pool = dt_in("proj_pool", [nlayer, 512, D])
    w_out = dt_in("w_out", [nlayer, D, D])
    ffn_upP = dt_in("ffn_upP", [nlayer, D, 2 * DFF])
    ffn_down = dt_in("ffn_down", [nlayer, DFF, D])
    yT = nc.dram_tensor("yT", [nb, D, SEQ], F32, kind="ExternalOutput").ap()
    dbg_out = {}
    if dbg:
        for name, shape in dbg.items():
            dbg_out[name] = nc.dram_tensor("dbg_" + name, shape[1], shape[0], kind="ExternalOutput").ap()
    xs = nc.dram_tensor("xs", [nb, D, T], F32, kind="Internal").ap()
    recT_h = nc.dram_tensor("recT_h", [nb, D, T], BF16, kind="Internal").ap()
    poolT_h = nc.dram_tensor("poolT_h", [nb, 512, T], BF16, kind="Internal").ap()
    gsc_h = nc.dram_tensor("gsc_h", [nb, 8, 128, 3, T], BF16, kind="Internal").ap()
    actT_h = nc.dram_tensor("actT_h", [nb, 5, 128, NJ, 512], BF16, kind="Internal").ap()

    RSZ = KC * T
    VSZ = 18 * 8 * 65
    NWB = 5
    WBSZ = 4096
    WKSZ = 10240
    O_R1, O_R2, O_R3 = 0, RSZ, 2 * RSZ
    O_RV = 3 * RSZ
    O_WB = O_RV + VSZ
    O_WK = O_WB + NWB * WBSZ
    NA = O_WK + WKSZ

    with contextlib.ExitStack() as st:
        AR = st.enter_context(nc.sbuf_tensor("arena", [128, NA], BF16))
        vec = st.enter_context(nc.sbuf_tensor("vec", [128, nlayer, NV], F32))
        modt = st.enter_context(nc.sbuf_tensor("modt", [128, nlayer, 48, 3], F32))
        der = st.enter_context(nc.sbuf_tensor("der", [128, nlayer, 3, 2, 8], F32))
        lder = st.enter_context(nc.sbuf_tensor("lder", [128, nlayer, 5, 16], F32))
        hg1 = st.enter_context(nc.sbuf_tensor("hg1", [128, nlayer, 3, 8], F32))
        ltmp = st.enter_context(nc.sbuf_tensor("ltmp", [128, 4, 16], F32))
        ptab = st.enter_context(nc.sbuf_tensor("ptab_s", [128, 4, 16], F32))
        ones_bf = st.enter_context(nc.sbuf_tensor("ones_bf", [128, 128], BF16))
        ones_f = st.enter_context(nc.sbuf_tensor("ones_f", [128, 64], F32))
        wkr = st.enter_context(nc.sbuf_tensor("wkr", [128, KC, 2, 96], BF16))
        csb = st.enter_context(nc.sbuf_tensor("csb", [128, KC, 3], F32))
        scb = st.enter_context(nc.sbuf_tensor("scb", [128, KC, 3], BF16))
        bscr = st.enter_context(nc.sbuf_tensor("bscr", [128, 2], F32))
        PSALL = st.enter_context(nc.psum_tensor("psall", [128, 4096], F32))
        PS = [PSALL[:, i * 512:(i + 1) * 512] for i in range(8)]
        P = Prog(nc)

        def abf(off, *shape):
            n = int(np.prod(shape))
            ap = AR[:, off:off + n]
            if len(shape) == 2:
                return ap.rearrange("p (a b) -> p a b", a=shape[0])
            if len(shape) == 3:
                return ap.rearrange("p (a b c) -> p a b c", a=shape[0], b=shape[1])
            return ap

        def af32(off, *shape):
            n = int(np.prod(shape))
            ap = AR[:, off:off + 2 * n].bitcast(F32)
            if len(shape) == 2:
                return ap.rearrange("p (a b) -> p a b", a=shape[0])
            if len(shape) == 3:
                return ap.rearrange("p (a b c) -> p a b c", a=shape[0], b=shape[1])
            return ap

        R1 = abf(O_R1, KC, T)
        R2 = abf(O_R2, KC, T)
        R3 = abf(O_R3, KC, T)
        VV = abf(O_RV, 18, 8, 65)
        PLT = abf(O_RV, 4, T)

        psc = [0]

        def nextps():
            i = psc[0] % 7
            psc[0] += 1
            return i

        wb_rot = list(range(NWB))

        def wb_take():
            i = wb_rot.pop(0)
            wb_rot.append(i)
            return i

        def wb_pin():
            return wb_rot.pop(0)

        def wb_unpin(i):
            wb_rot.insert(0, i)

        def wb_ap(i, *shape):
            return abf(O_WB + i * WBSZ, *shape)

        def load_slab(dram_ap, i, shape, pslice=None):
            dst = wb_ap(i, *shape)
            if pslice is not None:
                dst = dst[pslice[0]:pslice[1]]
            P.dma("pool", dst, dram_ap, writes=[("WB", i)], nobar=True)
            return wb_ap(i, *shape)

        def mm(ps_ap, pairs, reads, pskey):
            n = len(pairs)
            for idx, (l, r) in enumerate(pairs):
                P.op("pe", lambda e, l=l, r=r, idx=idx: e.matmul(ps_ap, l, r, start=(idx == 0), stop=(idx == n - 1)),
                     reads=reads, writes=[pskey])

        def ACT(out, in_, func, reads, writes, bias=0.0, scale=1.0):
            P.op("act", lambda e: e.activation(out=out, in_=in_, func=func, bias=bias, scale=scale), reads=reads, writes=writes)

        def STT(out, in0, scalar, in1, op0, op1, reads, writes, eng="dve"):
            P.op(eng, lambda e: e.scalar_tensor_tensor(out=out, in0=in0, scalar=scalar, in1=in1, op0=op0, op1=op1), reads=reads, writes=writes)

        def TT(out, in0, in1, op, reads, writes, eng="dve"):
            P.op(eng, lambda e: e.tensor_tensor(out=out, in0=in0, in1=in1, op=op), reads=reads, writes=writes)

        def TS(out, in0, s1, s2, op0, op1, reads, writes, eng="dve"):
            P.op(eng, lambda e: e.tensor_scalar(out=out, in0=in0, scalar1=s1, scalar2=s2, op0=op0, op1=op1), reads=reads, writes=writes)

        def RECIP(out, in_, reads, writes):
            P.op("dve", lambda e: e.reciprocal(out=out, in_=in_), reads=reads, writes=writes)

        def V(l, name, i=0, n=1):
            o = VOFF[name] + i
            return vec[:, l, o:o + n]

        def dump(name, src_ap, reads, idx=None, sl=None):
            if name in dbg_out:
                dst = dbg_out[name] if idx is None else dbg_out[name][idx]
                if sl is not None:
                    dst = dst[:, sl[0]:sl[1]]
                P.dma("sp", dst, src_ap, reads=reads)

        P.dma("sp", vec[:], vecs_d, writes=["vec"])
        P.dma("sp", csb[:], cT, writes=["csb"])
        P.dma("sp", ptab[:], ptab_d, writes=["ptab"])
        P.op("dve", lambda e: e.memset(ones_bf[:], 1.0), writes=["ones_bf"])
        P.op("dve", lambda e: e.memset(ones_f[:], 1.0), writes=["ones_f"])
        P.op("dve", lambda e: e.memset(wkr[:], 0.0), writes=["wkr"])
        ACT(scb[:], csb[:], AF.Silu, ["csb"], ["scb"])
        psM = PS[7][:, 0:144].rearrange("p (j v) -> p j v", v=3)

        def ada_slab_load(l, s_):
            i = wb_take()
            wb = load_slab(ada_w[l][:, s_ * 512:(s_ + 1) * 512].rearrange("(k p) c -> p k c", p=128), i, (KC, 512))
            return i, wb

        def ada_slab_mm(l, s_, i, wb):
            for jj in range(4):
                j = s_ * 4 + jj
                mm(psM[:, j, :], [(wb[:, k, jj * 128:(jj + 1) * 128], scb[:, k, :]) for k in range(KC)], [("WB", i), "scb"], ("ps", 7))

        def ada_finish(l):
            for v in range(3):
                TT(modt[:, l, :, v], psM[:, :, v], V(l, "adab", 0, 48), ALU.add, [("ps", 7), "vec"], [("modt", l)])
            for v in range(3):
                for sub in range(2):
                    STT(der[:, l, v, sub, :], modt[:, l, (1 + 3 * sub) * 8:(2 + 3 * sub) * 8, v], 1.0, V(l, "n1g" if sub == 0 else "n2g", 0, 8),
                        ALU.add, ALU.mult, [("modt", l), "vec"], [("der", l)])
            e_ = ltmp[:, 0, :]
            w_ = ltmp[:, 1, :]
            w2 = ltmp[:, 2, :]
            pl = ltmp[:, 3, :]
            ACT(e_, V(l, "llam", 0, 16), AF.Exp, ["vec"], ["ltmp"], scale=-1.0)
            TS(w_, e_, 2.0, None, ALU.add, ALU.bypass, ["ltmp"], ["ltmp"])
            RECIP(w_, w_, ["ltmp"], ["ltmp"])
            TT(w_, w_, e_, ALU.mult, ["ltmp"], ["ltmp"])
            TT(w2, w_, w_, ALU.mult, ["ltmp"], ["ltmp"])
            TS(pl, w2, 1.0 / 9.0, 1.0 / 7.0, ALU.mult, ALU.add, ["ltmp"], ["ltmp"])
            for cf in (1.0 / 5.0, 1.0 / 3.0, 1.0):
                TT(pl, pl, w2, ALU.mult, ["ltmp"], ["ltmp"])
                TS(pl, pl, cf, None, ALU.add, ALU.bypass, ["ltmp"], ["ltmp"])
            TT(pl, pl, w_, ALU.mult, ["ltmp"], ["ltmp"])
            TS(lder[:, l, 0, :], pl, -8.0, None, ALU.mult, ALU.bypass, ["ltmp"], [("lder", l)])
            TS(lder[:, l, 1, :], pl, -16.0, None, ALU.mult, ALU.bypass, ["ltmp"], [("lder", l)])
            TT(lder[:, l, 2, 0:4], V(l, "pb", 0, 4), V(l, "psc", 0, 4), ALU.mult, ["vec"], [("lder", l)])
            for v in range(3):
                TS(hg1[:, l, v, :], modt[:, l, 16:24, v], 0.5, None, ALU.mult, ALU.bypass, [("modt", l)], [("der", l)])
            TS(lder[:, l, 3, :], V(l, "lba", 0, 16), 0.5, None, ALU.mult, ALU.bypass, ["vec"], [("lder", l)])
            TS(lder[:, l, 4, :], V(l, "lbi", 0, 16), 0.5, None, ALU.mult, ALU.bypass, ["vec"], [("lder", l)])

        for s_ in range(12):
            i_, wb_ = ada_slab_load(0, s_)
            ada_slab_mm(0, s_, i_, wb_)
        ada_finish(0)
        dump("mods", modt[:, 0, :, :], [("modt", 0)])

        def G(l, v, sub, k):
            return der[:, l, v, sub, k:k + 1]

        def MOD(l, v, m, k):
            return modt[:, l, m * 8 + k, v:v + 1]

        WK = O_WK
        norm_ctr = [0]

        def norm_tile(l, v, sub, xt, tn, psq_i, dst, dkeys, rkeys):
            par = norm_ctr[0] % 2
            norm_ctr[0] += 1
            rs = af32(WK + par * 5120, 512)[:, :tn]
            ACT(rs, PS[psq_i][:, :tn], AF.Ln, [("ps", psq_i)], [("rs", par)], bias=EPS, scale=1.0 / D)
            ACT(rs, rs, AF.Exp, [("rs", par)], [("rs", par)], scale=-0.5)
            for k in range(KC):
                tmp = af32(WK + 1024 + (k % 2) * 1024, 512)[:, :tn]
                STT(tmp, xt[:, k, :], G(l, v, sub, k), rs, ALU.mult, ALU.mult, rkeys(k) + [("rs", par), ("der", l)], [("ntmp", k % 2)])
                ACT(dst(k), tmp, AF.Identity, [("ntmp", k % 2), ("modt", l)], [dkeys(k)], bias=MOD(l, v, 3 * sub, k))

        def layer_batch(l, b):
            last = (l == nlayer - 1)
            tiles = TILES
            skipc = last and not full_last
            tiles_e = TILES[:4] if skipc else TILES
            segs_e = SEGS[:1] if skipc else SEGS
            xsrc = xT if l == 0 else xs
            phase = [0]

            class _Stop(Exception):
                pass

            def B():
                phase[0] += 1
                if stop_after is not None and phase[0] > stop_after:
                    raise _Stop()
                P.barrier(bscr[:, 0:1])

            B()
            for ti, (t0, tn) in enumerate(tiles):
                v = 2 if ti == 4 else b
                xt = af32(O_R2 + (ti % 2) * 8192, KC, 512)[:, :, :tn]
                P.dma("sp", xt, xsrc[b][:, t0:t0 + tn].rearrange("(k p) t -> p k t", p=128), reads=([("xs", b, ti)] if l > 0 else []),
                      writes=[("xt", ti % 2)])
                sq = abf(O_R3 + (ti % 2) * 4096, KC, 512)[:, :, :tn]
                ACT(sq, xt, AF.Square, [("xt", ti % 2)], [("sq", ti % 2)])
                pi = nextps()
                mm(PS[pi][:, :tn], [(ones_bf[:], sq[:, k, :]) for k in range(KC)], [("sq", ti % 2), "ones_bf"], ("ps", pi))
                norm_tile(l, v, 0, xt, tn, pi, lambda k, t0=t0, tn=tn: R1[:, k, t0:t0 + tn], lambda k, ti=ti: ("R1", k, ti), lambda k, ti=ti: [("xt", ti % 2)])
            if l == 0 and b == 0:
                for k in range(KC):
                    dump("hx", R1[:, k, :], [("R1", k, ti) for ti in range(5)], idx=k)

            R1K = lambda ti: [("R1", k, ti) for k in range(KC)]
            BK = lambda i: [("B", i, t) for t in range(5)]

            def scan(out, a, bx, init, reads, writes):
                P.op("dve", lambda e: e.tensor_tensor_scan(out=out, data0=a, data1=bx, initial=init, op0=ALU.mult, op1=ALU.add),
                     reads=reads, writes=writes)

            gate_state = {"slab": None, "q": 0}

            def gate_block():
                q = gate_state["q"]
                gate_state["q"] += 1
                s_, jj = q // 4, q % 4
                if jj == 0:
                    gi = wb_take()
                    gwb = load_slab(w_inP[l][:, WIN_GATE0 + s_ * 512:WIN_GATE0 + (s_ + 1) * 512].rearrange("(k p) c -> p k c", p=128), gi, (KC, 512))
                    gate_state["slab"] = (gi, gwb)
                gi, gwb = gate_state["slab"]
                j, n = q // 8, q % 8
                gst = abf(O_RV + (q % 2) * T, T)
                for ti, (t0, tn) in enumerate(tiles):
                    pi = nextps()
                    mm(PS[pi][:, :tn], [(gwb[:, k, jj * 128:(jj + 1) * 128], R1[:, k, t0:t0 + tn]) for k in range(KC)], [("WB", gi)] + R1K(ti), ("ps", pi))
                    ACT(gst[:, t0:t0 + tn], PS[pi][:, :tn], AF.Tanh, [("ps", pi)], [("gst", q % 2)], scale=0.5)
                P.dma("sp", gsc_h[b][n, :, j, :], gst, reads=[("gst", q % 2)], writes=[("gsc", b, n, j)])

            B()
            i_wa = wb_pin()
            i_wi = wb_pin()
            lwa = load_slab(lru_wa[l].rearrange("d n c e -> c d n e"), i_wa, (2, 8, 128))
            lwi = load_slab(lru_wi[l].rearrange("d n c e -> c d n e"), i_wi, (2, 8, 128))
            Bf = lambda i: af32(O_R2 + i * 4608, T)
            ub = abf(WK, T)
            rst = abf(WK + T, T)
            gls = [abf(WK + 2 * T, T), abf(WK + 3 * T, T)]
            XB = af32(O_RV + 2 * T, T)
            XK_ = [("X", t) for t in range(5)]
            lru_slab = [None]

            def lru_front(n):
                s_, cc = n // 2, n % 2
                if cc == 0:
                    i_ = wb_take()
                    lru_slab[0] = (i_, load_slab(w_inP[l][:, WIN_LRU0 + s_ * 512:WIN_LRU0 + (s_ + 1) * 512].rearrange("(k p) c -> p k c", p=128), i_, (KC, 512)))
                i, wb = lru_slab[0]
                gl = gls[n % 2]
                for ti, (t0, tn) in enumerate(tiles):
                    pi = nextps()
                    mm(PS[pi][:, :tn], [(wb[:, k, 2 * cc * 128:(2 * cc + 1) * 128], R1[:, k, t0:t0 + tn]) for k in range(KC)],
                       [("WB", i)] + R1K(ti), ("ps", pi))
                    P.op("dve", lambda e, pi=pi, t0=t0, tn=tn: e.tensor_copy(out=XB[:, t0:t0 + tn], in_=PS[pi][:, :tn]), reads=[("ps", pi)], writes=[("X", ti)])
                for ti, (t0, tn) in enumerate(tiles):
                    pi = nextps()
                    mm(PS[pi][:, :tn], [(wb[:, k, (2 * cc + 1) * 128:(2 * cc + 2) * 128], R1[:, k, t0:t0 + tn]) for k in range(KC)],
                       [("WB", i)] + R1K(ti), ("ps", pi))
                    ACT(gl[:, t0:t0 + tn], PS[pi][:, :tn], AF.Gelu_apprx_tanh, [("ps", pi)], [("gl", n % 2, ti)])

            def lru_conv(n):
                for (s0, sn) in SEGS:
                    TS(Bf(1)[:, s0:s0 + sn], XB[:, s0:s0 + sn], V(l, "lcw", 2 * 8 + n), V(l, "lcb", n), ALU.mult, ALU.add,
                       XK_ + ["vec"], BK(1))
                    for k in (0, 1, 3):
                        o = k - 2
                        a = max(0, -o)
                        e_ = sn - max(0, o)
                        STT(Bf(1)[:, s0 + a:s0 + e_], XB[:, s0 + a + o:s0 + e_ + o], V(l, "lcw", k * 8 + n), Bf(1)[:, s0 + a:s0 + e_],
                            ALU.mult, ALU.add, XK_ + BK(1) + ["vec"], BK(1))
                if l == 0 and b == 0 and n == 0:
                    dump("u0", Bf(1), BK(1))
                P.op("dve", lambda e: e.tensor_copy(out=ub, in_=Bf(1)), reads=BK(1), writes=["ub"])

            def lru_gates(n):
                for d in range(2):
                    br, bi_ = 2 + 3 * d, 4 + 3 * d
                    for ti, (t0, tn) in enumerate(tiles):
                        pr = nextps()
                        mm(PS[pr][:, :tn], [(lwa[:, d, n, :], ub[:, t0:t0 + tn])], [("WB", i_wa), "ub"], ("ps", pr))
                        ACT(Bf(br)[:, t0:t0 + tn], PS[pr][:, :tn], AF.Tanh, [("ps", pr), ("lder", l)], [("B", br, ti)],
                            bias=lder[:, l, 3, d * 8 + n:d * 8 + n + 1], scale=0.5)
                        pq = nextps()
                        mm(PS[pq][:, :tn], [(lwi[:, d, n, :], ub[:, t0:t0 + tn])], [("WB", i_wi), "ub"], ("ps", pq))
                        ACT(Bf(bi_)[:, t0:t0 + tn], PS[pq][:, :tn], AF.Tanh, [("ps", pq), ("lder", l)], [("B", bi_, ti)],
                            bias=lder[:, l, 4, d * 8 + n:d * 8 + n + 1], scale=0.5)
                for d in range(2):
                    br, b2 = 2 + 3 * d, 3 + 3 * d
                    la_h = lder[:, l, 0, d * 8 + n:d * 8 + n + 1]
                    la_f = lder[:, l, 1, d * 8 + n:d * 8 + n + 1]
                    ACT(Bf(b2), Bf(br), AF.Exp, BK(br) + [("lder", l)], BK(b2), scale=la_f, bias=la_f)
                    ACT(Bf(br), Bf(br), AF.Exp, BK(br) + [("lder", l)], BK(br), scale=la_h, bias=la_h)
                for d in range(2):
                    b2 = 3 + 3 * d
                    ACT(Bf(b2), Bf(b2), AF.Sqrt, BK(b2), BK(b2), scale=-0.25, bias=0.25 + 2.5e-7)

            def lru_bx(m):
                for d in range(2):
                    b2, bi_ = 3 + 3 * d, 4 + 3 * d
                    STT(Bf(bi_), Bf(bi_), 1.0, Bf(b2), ALU.add, ALU.mult, BK(bi_) + BK(b2), BK(bi_))
                    TT(Bf(bi_), Bf(bi_), Bf(1), ALU.mult, BK(bi_) + BK(1), BK(bi_))

            def lru_scan(m):
                for d in range(2):
                    br, bi_ = 2 + 3 * d, 4 + 3 * d
                    hb = 0 if d == 0 else 3
                    H = Bf(hb)
                    rk = BK(br) + BK(bi_)
                    if d == 0:
                        scan(H[:, SEQ:T], Bf(br)[:, SEQ:T], Bf(bi_)[:, SEQ:T], 0.0, rk, BK(hb))
                        scan(H[:, 0:SEQ], Bf(br)[:, 0:SEQ], Bf(bi_)[:, 0:SEQ], H[:, T - 1:T], rk + BK(hb), BK(hb))
                    else:
                        scan(H[:, SEQ:T][:, ::-1], Bf(br)[:, SEQ:T][:, ::-1], Bf(bi_)[:, SEQ:T][:, ::-1], 0.0, rk, BK(hb))
                        scan(H[:, 0:SEQ][:, ::-1], Bf(br)[:, 0:SEQ][:, ::-1], Bf(bi_)[:, 0:SEQ][:, ::-1], H[:, SEQ:SEQ + 1], rk + BK(hb), BK(hb))

            def lru_out(m):
                TT(Bf(0), Bf(0), Bf(3), ALU.add, BK(0) + BK(3), BK(0))
                TT(rst, Bf(0), gls[m % 2], ALU.mult, BK(0) + [("gl", m % 2, t) for t in range(5)], ["rst"])
                P.dma("sp", recT_h[b][m * 128:(m + 1) * 128, :], rst, reads=["rst"], writes=[("recT", b, m)])

            for it_ in range(9):
                if it_ < 8:
                    lru_front(it_)
                    for _ in range(2):
                        gate_block()
                if it_ >= 1:
                    lru_bx(it_ - 1)
                if it_ < 8:
                    lru_conv(it_)
                else:
                    for _ in range(8):
                        gate_block()
                if it_ >= 1:
                    lru_scan(it_ - 1)
                    lru_out(it_ - 1)
                if it_ < 8:
                    lru_gates(it_)
            wb_unpin(i_wi)
            wb_unpin(i_wa)
            if l == 0 and b == 0:
                dump("rec", recT_h[b], [("recT", b, n) for n in range(8)])

            B()
            i_pw = wb_pin()
            pw = load_slab(pool_w[l].rearrange("g c d -> c g d"), i_pw, (4, 128))
            i = wb_take()
            wb = load_slab(w_inP[l][:, WIN_POOL0:WIN_POOL0 + 512].rearrange("(k p) c -> p k c", p=128), i, (KC, 512))
            PW, LB, CB = 2368, 16, 2096
            Pa = [af32(O_R2, PW), af32(O_R2 + 2 * PW, PW)]
            Pb = af32(O_R2 + 4 * PW, PW)
            Pc = af32(O_R2 + 6 * PW, PW)
            tmpe = af32(WK + 3 * T, 16)
            for q in range(2):
                P.op("dve", lambda e, q=q: e.memset(Pa[q], 0.0), writes=[("Pa", q)])
            dbs = [abf(WK, T), abf(WK + 3 * T + 64, T)]

            def pool_front(g):
                U = Pa[g % 2]
                for ti, (t0, tn) in enumerate(tiles):
                    pi = nextps()
                    mm(PS[pi][:, :tn], [(wb[:, k, g * 128:(g + 1) * 128], R1[:, k, t0:t0 + tn]) for k in range(KC)], [("WB", i)] + R1K(ti), ("ps", pi))
                    base = (LB + t0) if ti < 4 else CB
                    ACT(U[:, base:base + tn], PS[pi][:, :tn], AF.Identity, [("ps", pi)], [("Pa", g % 2)])

            def pool_mid(g):
                w = POOL_WIN[g]
                left = w // 2
                right = w - 1 - left
                U = Pa[g % 2]
                db = dbs[g % 2]
                dbk = ("db", g % 2)
                src, skey = U, ("Pa", g % 2)
                m = 1
                lvl = 0
                while m < w:
                    dst, dkey = (Pb, "Pb") if lvl % 2 == 0 else (Pc, "Pc")
                    TT(dst[:, 0:PW - m], src[:, 0:PW - m], src[:, m:PW], ALU.add, [skey], [dkey])
                    src, skey = dst, dkey
                    m *= 2
                    lvl += 1
                Aw, akey = src, skey
                for (base, s0, sn) in ((LB, 0, SEQ), (CB, SEQ, CTXL)):
                    STT(db[:, s0:s0 + sn], Aw[:, base - left:base - left + sn], 1.0 / w, U[:, base:base + sn], ALU.mult, ALU.subtract,
                        [akey, ("Pa", g % 2)], [dbk])
                    TT(tmpe[:, 0:left], Aw[:, base - left:base], ptab[:, g, 0:left], ALU.mult, [akey, "ptab"], ["tmpe"])
                    TT(db[:, s0:s0 + left], tmpe[:, 0:left], U[:, base:base + left], ALU.subtract, ["tmpe", ("Pa", g % 2)], [dbk])
                    if right > 0:
                        TT(tmpe[:, 8:8 + right], Aw[:, base - left + sn - right:base - left + sn], ptab[:, g, 8:8 + right], ALU.mult,
                           [akey, "ptab"], ["tmpe"])
                        TT(db[:, s0 + sn - right:s0 + sn], tmpe[:, 8:8 + right], U[:, base + sn - right:base + sn], ALU.subtract,
                           ["tmpe", ("Pa", g % 2)], [dbk])

            def pool_back(g):
                db = dbs[g % 2]
                pst = abf(WK + T + (g % 2) * T, T)
                for ti, (t0, tn) in enumerate(tiles):
                    pi = nextps()
                    mm(PS[pi][:, :tn], [(pw[:, g, :], db[:, t0:t0 + tn])], [("WB", i_pw), ("db", g % 2)], ("ps", pi))
                    ACT(pst[:, t0:t0 + tn], PS[pi][:, :tn], AF.Identity, [("ps", pi), "vec", ("lder", l)], [("pst", g % 2)],
                        scale=V(l, "psc", g), bias=lder[:, l, 2, g:g + 1])
                P.dma("sp", poolT_h[b][g * 128:(g + 1) * 128, :], pst, reads=[("pst", g % 2)], writes=[("poolT", b, g)])

            pool_front(0)
            for g in range(4):
                if g + 1 < 4:
                    pool_front(g + 1)
                pool_mid(g)
                pool_back(g)
            wb_unpin(i_pw)
            if l == 0 and b == 0:
                dump("pool", poolT_h[b], [("poolT", b, g) for g in range(4)])

            def COPY(out, in_, reads, writes, which):
                if which % 2 == 0:
                    ACT(out, in_, AF.Identity, reads, writes)
                else:
                    P.op("dve", lambda e: e.tensor_copy(out=out, in_=in_), reads=reads, writes=writes)

            B()
            i_uq = wb_pin()
            wuq = load_slab(w_uq2[l].rearrange("(k p) v c -> p k v c", p=128), i_uq, (2, 2, 768))
            i_kv = wb_pin()
            wukv = load_slab(w_ukvP[l], i_kv, (1024,))
            i = wb_take()
            wb = load_slab(w_inP[l][:, WIN_MLA0:WIN_MLA0 + 384].rearrange("(k p) c -> p k c", p=128), i, (KC, 384))
            P.dma("pool", wkr[:, :, 0, 64:96], w_kr2[l][:, 0:32].rearrange("(k p) c -> p k c", p=128), writes=["wkr"], nobar=True)
            P.dma("pool", wkr[:, :, 1, 64:96], w_kr2[l][:, 32:64].rearrange("(k p) c -> p k c", p=128), writes=["wkr"], nobar=True)
            RVK = [("RV", kt) for kt in range(18)]
            P.op("dve", lambda e: e.memset(VV[:, :, :, 64:65], 1.0), writes=RVK + ["RVp"])
            ckv = af32(WK, 512)
            rs = af32(WK + 1024, 512)
            sqk = abf(WK + 2048, 512)
            ckvn = abf(WK + 2560, 512)
            cq = af32(WK + 3072, 2, 512)
            sqq = abf(WK + 5120, 2, 512)
            cqn = abf(WK + 6144, 2, 512)
            Ct = af32(WK + 7168, 512)
            St = af32(WK + 8192, 512)
            t1 = af32(WK + 9216, 512)
            rs2 = rs
            t2 = cq[:, 0, :]
            tq1 = ckv
            tq2 = cq[:, 1, :]
            cw = 0
            for ti, (t0, tn) in enumerate(tiles):
                pi = nextps()
                mm(PS[pi][:, :tn], [(wb[:, k, 0:128], R1[:, k, t0:t0 + tn]) for k in range(KC)], [("WB", i)] + R1K(ti), ("ps", pi))
                ACT(ckv[:, :tn], PS[pi][:, :tn], AF.Identity, [("ps", pi)], ["ckv"])
                ACT(sqk[:, :tn], PS[pi][:, :tn], AF.Square, [("ps", pi)], ["sqk"])
                for c2 in range(2):
                    p5 = nextps()
                    mm(PS[p5][:, :tn], [(wb[:, k, 128 + c2 * 128:256 + c2 * 128], R1[:, k, t0:t0 + tn]) for k in range(KC)],
                       [("WB", i)] + R1K(ti), ("ps", p5))
                    ACT(cq[:, c2, :tn], PS[p5][:, :tn], AF.Identity, [("ps", p5)], [("cq", c2)])
                    ACT(sqq[:, c2, :tn], PS[p5][:, :tn], AF.Square, [("ps", p5)], ["sqq"])
                pa = nextps()
                mm(PS[pa][0:96, :tn], [(wkr[:, k, 0, :], R1[:, k, t0:t0 + tn]) for k in range(KC)], ["wkr"] + R1K(ti), ("ps", pa))
                pb_ = nextps()
                mm(PS[pb_][0:96, :tn], [(wkr[:, k, 1, :], R1[:, k, t0:t0 + tn]) for k in range(KC)], ["wkr"] + R1K(ti), ("ps", pb_))
                P.dma("sp", Ct[64:96, :tn], ropeC_d[:, t0:t0 + tn], writes=["Ct"])
                P.dma("sp", St[64:96, :tn], ropeS_d[:, t0:t0 + tn], writes=["St"])
                p2 = nextps()
                mm(PS[p2][:, :tn], [(ones_bf[:], sqk[:, :tn])], ["sqk", "ones_bf"], ("ps", p2))
                ACT(rs[:, :tn], PS[p2][:, :tn], AF.Ln, [("ps", p2)], ["rs"], bias=EPS, scale=1.0 / 128)
                ACT(rs[:, :tn], rs[:, :tn], AF.Exp, ["rs"], ["rs"], scale=-0.5)
                STT(ckvn[:, :tn], ckv[:, :tn], V(l, "kvng", 0), rs[:, :tn], ALU.mult, ALU.mult, ["ckv", "rs", "vec"], ["ckvn"])
                p6 = nextps()
                mm(PS[p6][:, :tn], [(ones_bf[:], sqq[:, 0, :tn]), (ones_bf[:], sqq[:, 1, :tn])], ["sqq", "ones_bf"], ("ps", p6))
                ACT(rs2[:, :tn], PS[p6][:, :tn], AF.Ln, [("ps", p6)], ["rs"], bias=EPS, scale=1.0 / 256)
                ACT(rs2[:, :tn], rs2[:, :tn], AF.Exp, ["rs"], ["rs"], scale=-0.5, bias=LN_SM)
                for c2 in range(2):
                    STT(cqn[:, c2, :tn], cq[:, c2, :tn], V(l, "qng", c2), rs2[:, :tn], ALU.mult, ALU.mult, [("cq", c2), "rs", "vec"], ["cqn"])
                TT(t1[64:96, :tn], PS[pa][64:96, :tn], Ct[64:96, :tn], ALU.mult, [("ps", pa), "Ct"], ["t1"])
                TT(t2[64:96, :tn], PS[pb_][64:96, :tn], St[64:96, :tn], ALU.mult, [("ps", pb_), "St", "cqn"], [("cq", 0)])
                TT(t1[64:96, :tn], t1[64:96, :tn], t2[64:96, :tn], ALU.add, ["t1", ("cq", 0)], ["t1"])
                if l == 0 and b == 0:
                    dump("kr", t1[64:96, :tn], ["t1"], sl=(t0, t0 + tn))
                for h in range(8):
                    COPY(R2[64:96, h, t0:t0 + tn], t1[64:96, :tn], ["t1"], [("R2", h, ti)], cw)
                    cw += 1
                for h in range(8):
                    p3 = nextps()
                    mm(PS[p3][0:64, :tn], [(wukv[:, h * 64:(h + 1) * 64], ckvn[:, :tn])], [("WB", i_kv), "ckvn"], ("ps", p3))
                    COPY(R2[0:64, h, t0:t0 + tn], PS[p3][0:64, :tn], [("ps", p3)], [("R2", h, ti)], cw)
                    cw += 1
                for sub in range(tn // 128):
                    kt = t0 // 128 + sub
                    p4 = nextps()
                    mm(PS[p4][:, 0:512], [(ckvn[:, sub * 128:(sub + 1) * 128], wukv[:, 512:1024])], [("WB", i_kv), "ckvn"], ("ps", p4))
                    COPY(VV[:, kt, :, 0:64], PS[p4][:, 0:512].rearrange("p (h d) -> p h d", h=8), [("ps", p4)], [("RV", kt), "RVp"], cw)
                    cw += 1
                for h in range(8):
                    pA = nextps()
                    mm(PS[pA][0:96, :tn], [(wuq[:, k, 0, h * 96:(h + 1) * 96], cqn[:, k, :tn]) for k in range(2)], [("WB", i_uq), "cqn"], ("ps", pA))
                    pB = nextps()
                    mm(PS[pB][0:96, :tn], [(wuq[:, k, 1, h * 96:(h + 1) * 96], cqn[:, k, :tn]) for k in range(2)], [("WB", i_uq), "cqn"], ("ps", pB))
                    ACT(R3[0:64, h, t0:t0 + tn], PS[pA][0:64, :tn], AF.Identity, [("ps", pA)], [("R3", h, ti)])
                    TT(tq1[64:96, :tn], PS[pA][64:96, :tn], Ct[64:96, :tn], ALU.mult, [("ps", pA), "Ct"], ["ckv"])
                    TT(tq2[64:96, :tn], PS[pB][64:96, :tn], St[64:96, :tn], ALU.mult, [("ps", pB), "St"], [("cq", 1)])
                    TT(R3[64:96, h, t0:t0 + tn], tq1[64:96, :tn], tq2[64:96, :tn], ALU.add, ["ckv", ("cq", 1)], [("R3", h, ti)])
            wb_unpin(i_kv)
            wb_unpin(i_uq)

            if l == 0 and b == 0:
                for h in range(8):
                    dump("K", R2[0:96, h, :], [("R2", h, t) for t in range(5)], idx=h)
                    dump("Q", R3[0:96, h, :], [("R3", h, t) for t in range(5)], idx=h)
                dump("Vv", AR[:, O_RV:O_RV + VSZ], RVK)
            B()
            PTs = [abf(WK + q * 1024, 1024) for q in range(3)]
            rc = af32(WK + 3072, 1024)
            osb = af32(WK + 5120, 1024)
            sctr = 0
            groups = [[0, 1], [2, 3]] + ([] if skipc else [[4]])
            for h in range(8):
                for qis in groups:
                    kts = list(range(18)) if qis[0] < 4 else [16, 17]
                    nk = len(kts)
                    W = sum(tiles[qi][1] for qi in qis)
                    pend = []
                    for step in range(nk + 1):
                        if step < nk:
                            kt = kts[step]
                            sb = sctr % 3
                            sctr += 1
                            for jq, qi in enumerate(qis):
                                q0, qn = tiles[qi]
                                P.op("pe", lambda e, sb=sb, jq=jq, qn=qn, q0=q0, kt=kt, h=h: e.matmul(
                                    PSALL[:, sb * 1024 + jq * 512:sb * 1024 + jq * 512 + qn], R2[0:96, h, kt * 128:(kt + 1) * 128],
                                    R3[0:96, h, q0:q0 + qn], start=True, stop=True),
                                    reads=[("R2", h, kt // 4), ("R3", h, qi)], writes=[("ps", 2 * sb), ("ps", 2 * sb + 1)])
                            ACT(PTs[sb][:, :W], PSALL[:, sb * 1024:sb * 1024 + W], AF.Exp, [("ps", 2 * sb), ("ps", 2 * sb + 1)], [("PT", sb)])
                            pend.append((kt, sb))
                        if step > 0:
                            kt, sb = pend.pop(0)
                            for jq, qi in enumerate(qis):
                                q0, qn = tiles[qi]
                                P.op("pe", lambda e, kt=kt, sb=sb, step=step, jq=jq, h=h, qn=qn, nk=nk: e.matmul(
                                    PS[6 + jq][:, :qn], AR[:, O_RV + kt * 520 + h * 65:O_RV + kt * 520 + h * 65 + 128], PTs[sb][:, jq * 512:jq * 512 + qn], start=(step == 1), stop=(step == nk)),
                                    reads=[("RV", kt), ("PT", sb)], writes=[("ps", 6 + jq)])
                    for jq, qi in enumerate(qis):
                        q0, qn = tiles[qi]
                        cs = slice(jq * 512, jq * 512 + qn)
                        ACT(rc[64:65, cs], PS[6 + jq][64:65, :qn], AF.Ln, [("ps", 6 + jq)], [("rc", jq)])
                        ACT(rc[64:65, cs], rc[64:65, cs], AF.Exp, [("rc", jq)], [("rc", jq)], scale=-1.0)
                        P.op("dve", lambda e, jq=jq, qn=qn, cs=cs: e.tensor_copy(out=osb[0:64, cs], in_=PS[6 + jq][0:64, :qn]),
                             reads=[("ps", 6 + jq)], writes=[("osb", jq)])
                        sb = sctr % 3
                        sctr += 1
                        P.op("pe", lambda e, sb=sb, qn=qn, cs=cs: e.matmul(PSALL[0:64, sb * 1024:sb * 1024 + qn], ones_f[64:65, 0:64], rc[64:65, cs], start=True, stop=True),
                             reads=[("rc", jq), "ones_f"], writes=[("ps", 2 * sb), ("ps", 2 * sb + 1)])
                        TT(R1[0:64, h, q0:q0 + qn], osb[0:64, cs], PSALL[0:64, sb * 1024:sb * 1024 + qn], ALU.mult, [("osb", jq), ("ps", 2 * sb), ("ps", 2 * sb + 1)], [("R1", h, qi)])
            if l == 0 and b == 0:
                for h in range(8):
                    dump("att", R1[0:64, h, :], [("R1", h, t) for t in range(5)], idx=h)

            B()
            R3ALL = [("R3", k, t) for k in range(KC) for t in range(5)]
            P.dma("sp", R3, recT_h[b].rearrange("(k p) t -> p k t", p=128), reads=[("recT", b, n) for n in range(8)], writes=R3ALL)
            P.dma("sp", PLT, poolT_h[b].rearrange("(g p) t -> p g t", p=128), reads=[("poolT", b, g) for g in range(4)], writes=["RVp"] + RVK)

            gts = [abf(WK + q * 1536, 3, 512) for q in range(3)]
            tAs = [af32(WK + 4608 + q * 2048, 512) for q in range(2)]
            tBs = [af32(WK + 5632 + q * 2048, 512) for q in range(2)]
            gcnt = 0
            for grp in range(2):
                ia = wb_take()
                wa = load_slab(proj_mla[l][:, grp * 512:(grp + 1) * 512].rearrange("(h d) n -> d h n", d=64), ia, (8, 512), pslice=(0, 64))
                ir = wb_take()
                wr = load_slab(proj_lru[l][:, grp * 512:(grp + 1) * 512].rearrange("(k p) n -> p k n", p=128), ir, (8, 512))
                ip = wb_take()
                wp = load_slab(proj_pool[l][:, grp * 512:(grp + 1) * 512].rearrange("(g p) n -> p g n", p=128), ip, (4, 512))
                for nn in range(4):
                    n = grp * 4 + nn
                    for ti, (t0, tn) in enumerate(tiles_e):
                        gq = gcnt % 3
                        gt = gts[gq]
                        P.dma("sp", gt[:, :, :tn], gsc_h[b][n, :, :, t0:t0 + tn], reads=[("gsc", b, n, j) for j in range(3)], writes=[("gt", gq)])
                        pA = nextps()
                        mm(PS[pA][:, :tn], [(wa[0:64, h, nn * 128:(nn + 1) * 128], R1[0:64, h, t0:t0 + tn]) for h in range(8)],
                           [("WB", ia)] + R1K(ti), ("ps", pA))
                        pR = nextps()
                        mm(PS[pR][:, :tn], [(wr[:, k, nn * 128:(nn + 1) * 128], R3[:, k, t0:t0 + tn]) for k in range(KC)],
                           [("WB", ir)] + [("R3", k, ti) for k in range(KC)], ("ps", pR))
                        pP = nextps()
                        mm(PS[pP][:, :tn], [(wp[:, g, nn * 128:(nn + 1) * 128], PLT[:, g, t0:t0 + tn]) for g in range(4)],
                           [("WB", ip), "RVp"], ("ps", pP))
                        par = gcnt % 2
                        tA = tAs[par][:, :tn]
                        tB = tBs[par][:, :tn]
                        STT(tA, gt[:, 0, :tn], 1.0, PS[pA][:, :tn], ALU.add, ALU.mult, [("ps", pA), ("gt", gq)], [("tA", par)])
                        STT(tB, gt[:, 1, :tn], 1.0, PS[pR][:, :tn], ALU.add, ALU.mult, [("ps", pR), ("gt", gq)], [("tB", par)])
                        TT(tA, tA, tB, ALU.add, [("tA", par), ("tB", par)], [("tA", par)])
                        STT(tB, gt[:, 2, :tn], 1.0, PS[pP][:, :tn], ALU.add, ALU.mult, [("ps", pP), ("gt", gq)], [("tB", par)])
                        TT(R2[:, n, t0:t0 + tn], tA, tB, ALU.add, [("tA", par), ("tB", par)], [("R2", n, ti)])
                        gcnt += 1

            B()
            i0 = wb_pin()
            wo0 = load_slab(w_out[l][:, 0:512].rearrange("(k p) n -> p k n", p=128), i0, (KC, 512))
            i1 = wb_pin()
            wo1 = load_slab(w_out[l][:, 512:1024].rearrange("(k p) n -> p k n", p=128), i1, (KC, 512))
            wos = [(wo0, i0), (wo1, i1)]
            xts = [af32(O_R3 + q * 8192, KC, 512) for q in range(2)]
            sqs = [abf(WK + 3072 + q * 512, 512) for q in range(4)]
            XK = lambda q: [("xt", q, k) for k in range(KC)]
            for ti, (t0, tn) in enumerate(tiles_e):
                v = 2 if ti == 4 else b
                q = ti % 2
                xt = xts[q][:, :, :tn]
                P.dma("sp", xt, xsrc[b][:, t0:t0 + tn].rearrange("(k p) t -> p k t", p=128), reads=([("xs", b, ti)] if l > 0 else []), writes=XK(q))
                pend_sq = []

                def flush_sq(pend_sq=pend_sq, tn=tn):
                    sq_, n2_ = pend_sq.pop(0)
                    P.op("pe", lambda e: e.matmul(PS[7][:, :tn], ones_bf[:], sq_, start=(n2_ == 0), stop=(n2_ == 7)),
                         reads=[("sq", n2_ % 4), "ones_bf"], writes=[("ps", 7)])

                for n2 in range(8):
                    pi = nextps()
                    wo, iw = wos[n2 // 4]
                    mm(PS[pi][:, :tn], [(wo[:, k, (n2 % 4) * 128:(n2 % 4 + 1) * 128], R2[:, k, t0:t0 + tn]) for k in range(KC)],
                       [("WB", iw)] + [("R2", k, ti) for k in range(KC)], ("ps", pi))
                    if len(pend_sq) >= 2:
                        flush_sq()
                    STT(xt[:, n2, :], PS[pi][:, :tn], hg1[:, l, v, n2:n2 + 1], xt[:, n2, :], ALU.mult, ALU.add,
                        [("ps", pi), ("xt", q, n2), ("der", l)], [("xt", q, n2)])
                    sq = sqs[n2 % 4][:, :tn]
                    ACT(sq, xt[:, n2, :], AF.Square, [("xt", q, n2)], [("sq", n2 % 4)])
                    pend_sq.append((sq, n2))
                while pend_sq:
                    flush_sq()
                P.dma("sp", xs[b][:, t0:t0 + tn].rearrange("(k p) t -> p k t", p=128), xt, reads=XK(q), writes=[("xs", b, ti)])
                norm_tile(l, v, 1, xt, tn, 7, lambda k, t0=t0, tn=tn: R1[:, k, t0:t0 + tn], lambda k, ti=ti: ("R1", k, ti),
                          lambda k, q=q: [("xt", q, k)])
            wb_unpin(i1)
            wb_unpin(i0)
            if l == 0 and b == 0:
                dump("x1", xs[b], [("xs", b, t) for t in range(len(tiles_e))])
                for k in range(KC):
                    dump("h2", R1[:, k, :], [("R1", k, t) for t in range(len(tiles_e))], idx=k)

            B()
            G0s = [af32(O_R3 + q * 4608, T) for q in range(2)]
            C0s = [af32(O_R3 + (2 + q) * 4608, T) for q in range(2)]
            fd = abf(O_R2, NJ, 1024)

            def load_fdA():
                for hh in range(2):
                    P.dma("pool", fd[:, 0:18, hh * 512:(hh + 1) * 512], ffn_down[l][0:18 * 128, hh * 512:(hh + 1) * 512].rearrange("(j p) n -> p j n", p=128),
                          writes=[("fdA", hh)])
            asts = [abf(WK + q * T, T) for q in range(2)]
            TE = tiles_e[-1][0] + tiles_e[-1][1]
            do_ada = (b == 0 and l + 1 < nlayer)
            ada_q = []
            ada_next = [0]

            def ada_step():
                if not do_ada:
                    return
                if ada_q:
                    ada_slab_mm(l + 1, *ada_q.pop(0))
                if ada_next[0] < 12:
                    ii, wbb = ada_slab_load(l + 1, ada_next[0])
                    ada_q.append((ada_next[0], ii, wbb))
                    ada_next[0] += 1

            for s in range(11):
                i = wb_take()
                wb = load_slab(ffn_upP[l][:, s * 512:(s + 1) * 512].rearrange("(k p) c -> p k c", p=128), i, (KC, 512))
                ada_step()
                if s == 5:
                    ada_step()
                if s == 2:
                    load_fdA()
                for cc in range(2):
                    j = 2 * s + cc
                    q = j % 2
                    G0, C0, ast = G0s[q], C0s[q], asts[q]
                    for ti, (t0, tn) in enumerate(tiles_e):
                        pi = nextps()
                        mm(PS[pi][:, :tn], [(wb[:, k, (2 * cc + 1) * 128:(2 * cc + 2) * 128], R1[:, k, t0:t0 + tn]) for k in range(KC)],
                           [("WB", i)] + R1K(ti), ("ps", pi))
                        ACT(G0[:, t0:t0 + tn], PS[pi][:, :tn], AF.Identity, [("ps", pi)], [("G0", q, ti)])
                    GK = [("G0", q, t) for t in range(5)]
                    CK = [("C0", q, t) for t in range(5)]
                    for (s0, sn) in segs_e:
                        TS(C0[:, s0:s0 + sn], G0[:, s0:s0 + sn], V(l, "fcw", 22 + j), V(l, "fcb", j), ALU.mult, ALU.add, GK + ["vec"], CK)
                        for k in (0, 2):
                            o = k - 1
                            a = max(0, -o)
                            e_ = sn - max(0, o)
                            STT(C0[:, s0 + a:s0 + e_], G0[:, s0 + a + o:s0 + e_ + o], V(l, "fcw", k * 22 + j), C0[:, s0 + a:s0 + e_],
                                ALU.mult, ALU.add, GK + CK + ["vec"], CK)
                    ACT(C0[:, 0:TE], C0[:, 0:TE], AF.Silu, CK, CK)
                    for ti, (t0, tn) in enumerate(tiles_e):
                        pi = nextps()
                        mm(PS[pi][:, :tn], [(wb[:, k, 2 * cc * 128:(2 * cc + 1) * 128], R1[:, k, t0:t0 + tn]) for k in range(KC)],
                           [("WB", i)] + R1K(ti), ("ps", pi))
                        TT(ast[:, t0:t0 + tn], PS[pi][:, :tn], C0[:, t0:t0 + tn], ALU.mult, [("ps", pi)] + CK, [("ast", q)])
                    for ti, (t0, tn) in enumerate(tiles_e):
                        P.dma("sp", actT_h[b][ti][:, j, 0:tn], ast[:, t0:t0 + tn], reads=[("ast", q)], writes=[("actT", b, j, ti)])

            if do_ada:
                while ada_q:
                    ada_slab_mm(l + 1, *ada_q.pop(0))
                ada_finish(l + 1)

            B()
            for hh in range(2):
                P.dma("pool", fd[:, 18:NJ, hh * 512:(hh + 1) * 512], ffn_down[l][18 * 128:NJ * 128, hh * 512:(hh + 1) * 512].rearrange("(j p) n -> p j n", p=128),
                      writes=[("fdB", hh)])
            ats = [abf(O_R1, NJ, 512), abf(O_R2 + NJ * 1024, NJ, 512)]
            xts = [af32(O_RV, KC, 512), af32(O_WK, KC, 512)]
            sqs = [abf(WK + 8192 + q * 512, 512) for q in range(2)]
            rsf = af32(WK + 9216, 512)
            pend11 = []
            for ti, (t0, tn) in enumerate(tiles_e):
                v = 2 if ti == 4 else b
                q = ti % 2
                at = ats[q][:, :, :tn]
                xt = xts[q][:, :, :tn]
                P.dma("pool", at, actT_h[b][ti][:, :, 0:tn], reads=[("actT", b, j, ti) for j in range(NJ)], writes=[("at", q)])
                P.dma("sp", xt, xs[b][:, t0:t0 + tn].rearrange("(k p) t -> p k t", p=128), reads=[("xs", b, ti)], writes=XK(q))
                for n2 in range(8):
                    pi = nextps()
                    mm(PS[pi][:, :tn], [(fd[:, j, n2 * 128:(n2 + 1) * 128], at[:, j, :]) for j in range(NJ)], [("fdA", n2 // 4), ("fdB", n2 // 4), ("at", q)], ("ps", pi))
                    STT(xt[:, n2, :], PS[pi][:, :tn], MOD(l, v, 5, n2), xt[:, n2, :], ALU.mult, ALU.add,
                        [("ps", pi), ("xt", q, n2), ("modt", l)], [("xt", q, n2)])
                    if last and ti < 4:
                        sq = sqs[n2 % 2][:, :tn]
                        ACT(sq, xt[:, n2, :], AF.Square, [("xt", q, n2)], [("sq", n2 % 2)])
                        pend11.append((sq, n2, tn))
                    if len(pend11) >= 2 or (pend11 and n2 == 7):
                        while pend11 and (len(pend11) >= 2 or n2 == 7):
                            sq_, n2_, tn_ = pend11.pop(0)
                            P.op("pe", lambda e, sq_=sq_, tn_=tn_, n2_=n2_: e.matmul(PS[7][:, :tn_], ones_bf[:], sq_, start=(n2_ == 0), stop=(n2_ == 7)),
                                 reads=[("sq", n2_ % 2), "ones_bf"], writes=[("ps", 7)])
                if not last or "x2" in dbg_out:
                    P.dma("sp" if last else "act", xs[b][:, t0:t0 + tn].rearrange("(k p) t -> p k t", p=128), xt, reads=XK(q), writes=[("xs", b, ti)])
                if last and ti < 4:
                    rs_ = rsf[:, :tn]
                    ACT(rs_, PS[7][:, :tn], AF.Ln, [("ps", 7)], ["rsf"], bias=EPS, scale=1.0 / D)
                    ACT(rs_, rs_, AF.Exp, ["rsf"], ["rsf"], scale=-0.5)
                    for k in range(KC):
                        STT(xt[:, k, :], xt[:, k, :], V(l, "fng", k), rs_, ALU.mult, ALU.mult, [("xt", q, k), "rsf", "vec"], [("xt", q, k)])
                    P.dma("sp", yT[b][:, t0:t0 + tn].rearrange("(k p) t -> p k t", p=128), xt, reads=XK(q), writes=[("yT", b, ti)])
            if l == 0 and b == 0:
                dump("x2", xs[b], [("xs", b, t) for t in range(len(tiles_e))])

        for l in range(nlayer):
            for b in range(nb):
                try:
                    layer_batch(l, b)
                except Exception as ex:
                    if type(ex).__name__ != "_Stop":
                        raise
        P.barrier(bscr[:, 0:1])
        P.emit()
    return nc


def _colify(a):
    a = np.asarray(a, np.float32)
    lead = a.shape[:-1]
    n = a.shape[-1] // 128
    a = a.reshape(*lead, n, 128)
    a = np.moveaxis(a, -1, 0)
    return a.reshape(128, -1)


def prep_shared(inp, nlayer=NLAYER):
    f = lambda k: np.asarray(inp[k], np.float32)
    vecs = np.zeros((128, nlayer, NV), np.float32)
    for l in range(nlayer):
        def put(name, arr):
            c = _colify(arr)
            vecs[:, l, VOFF[name]:VOFF[name] + c.shape[1]] = c
        put("n1g", f("norm1_g")[l]); put("n2g", f("norm2_g")[l]); put("adab", f("ada_b")[l]); put("qng", f("q_norm_g")[l])
        put("kvng", f("kv_norm_g")[l]); put("lcw", f("lru_conv_w")[l]); put("lcb", f("lru_conv_b")[l]); put("lba", f("lru_ba")[l])
        put("lbi", f("lru_bi")[l]); put("llam", f("lru_lambda")[l]); put("pb", f("pool_b")[l]); put("psc", f("pool_scale")[l])
        put("fcw", f("ffn_conv_w")[l]); put("fcb", f("ffn_conv_b")[l]); put("fng", f("final_norm_g"))
    half = 16
    inv = (10000.0 ** (-np.arange(0, half, 2, dtype=np.float32) / half)).astype(np.float32)
    t = np.arange(SEQ)
    ang_r = (t // 64).astype(np.float32)[:, None] * inv
    ang_c = (t % 64).astype(np.float32)[:, None] * inv
    cr, sr, cc, sc = np.cos(ang_r).T, np.sin(ang_r).T, np.cos(ang_c).T, np.sin(ang_c).T
    ropeC = np.ones((32, T), np.float32)
    ropeS = np.zeros((32, T), np.float32)
    ropeC[:, :SEQ] = np.concatenate([cr, cr, cc, cc], 0)
    ropeS[:, :SEQ] = np.concatenate([-sr, sr, -sc, sc], 0)
    ptab = np.ones((128, 4, 16), np.float32)
    for g, w in enumerate(POOL_WIN):
        left = w // 2
        right = w - 1 - left
        for tt in range(left):
            ptab[:, g, tt] = 1.0 / (tt + right + 1)
        for i in range(right):
            ptab[:, g, 8 + i] = 1.0 / (right - i + left)
    w_in = f("w_in")[:nlayer]
    perm = _win_perm()
    w_uq = f("w_uq")[:nlayer]
    w_uq_sw = w_uq.copy()
    for h in range(8):
        w_uq_sw[:, :, h * 96 + 64:h * 96 + 96] = w_uq[:, :, h * 96 + 64 + KR_SWAP]
    w_ukv = f("w_ukv")[:nlayer].reshape(nlayer, 128, 8, 128)
    ffn_up = f("ffn_up")[:nlayer]
    upcols = []
    for j in range(NJ):
        upcols += list(range(j * 128, (j + 1) * 128)) + list(range(DFF + j * 128, DFF + (j + 1) * 128))
    kr = w_in[:, :, COL_KV:COL_KR]
    sh = dict(
        vecs=vecs, ropeC=ropeC, ropeS=ropeS, ptab=ptab,
        ada_w=np.ascontiguousarray(f("ada_w")[:nlayer]),
        w_inP=np.ascontiguousarray(w_in[:, :, perm]),
        w_kr2=np.ascontiguousarray(np.concatenate([kr, kr[:, :, KR_SWAP]], -1)),
        w_uq2=np.ascontiguousarray(np.stack([w_uq, w_uq_sw], 2)),
        w_ukvP=np.ascontiguousarray(np.concatenate([w_ukv[..., :64].reshape(nlayer, 128, 512), w_ukv[..., 64:].reshape(nlayer, 128, 512)], -1)),
        lru_wa=f("lru_wa")[:nlayer], lru_wi=f("lru_wi")[:nlayer], pool_w=f("pool_w")[:nlayer],
        proj_mla=f("proj_mla")[:nlayer], proj_lru=f("proj_lru")[:nlayer], proj_pool=f("proj_pool")[:nlayer],
        w_out=f("w_out")[:nlayer], ffn_upP=np.ascontiguousarray(ffn_up[:, :, np.array(upcols)]), ffn_down=f("ffn_down")[:nlayer],
    )
    return sh


def prep_core(inp, batches):
    x = np.asarray(inp["x"], np.float32)
    ctx = np.asarray(inp["ctx"], np.float32)
    c = np.asarray(inp["c"], np.float32)
    xT = np.stack([np.concatenate([x[b].T, ctx[b].T], axis=1) for b in batches], 0)
    cv = [c[batches[0]], c[batches[-1]], np.asarray(inp["c_ctx"], np.float32)]
    cT = np.stack([_colify(v)[:, :] for v in cv], -1)
    return dict(xT=np.ascontiguousarray(xT), cT=np.ascontiguousarray(cT))


_CACHE = {}


def kernel(**inputs):
    if "nc" not in _CACHE:
        _CACHE["nc"] = build_program()
    nc = _CACHE["nc"]
    sh = prep_shared(inputs)
    in_maps = []
    for core in range(NCORES):
        m = dict(sh)
        m.update(prep_core(inputs, [2 * core, 2 * core + 1]))
        in_maps.append(m)
    res = run_bass_kernel_spmd(nc, in_maps, core_ids=list(range(NCORES)))
    out = np.empty((16, SEQ, D), np.float32)
    for core in range(NCORES):
        y = np.asarray(res.results[core]["yT"])
        out[2 * core] = y[0].T
        out[2 * core + 1] = y[1].T
    return out
```

```python
import contextlib
import numpy as np
import concourse.bass as bass
import concourse.mybir as mybir
from concourse.bass_utils import run_bass_kernel_spmd

F32 = mybir.dt.float32
BF16 = mybir.dt.bfloat16
ALU = mybir.AluOpType
AF = mybir.ActivationFunctionType

NDSEM = 24
SAME_ENG_SYNC = True


class Op:
    __slots__ = ("eng", "fn", "dma", "deps", "need_inc", "count", "sem_i", "sem_val")

    def __init__(self, eng, fn, dma):
        self.eng = eng
        self.fn = fn
        self.dma = dma
        self.deps = []
        self.need_inc = False
        self.count = 0
        self.sem_i = 0
        self.sem_val = 0


class Prog:
    ENGS = ("pe", "act", "dve", "pool", "sp")

    def __init__(self, nc):
        self.nc = nc
        self.ops = {e: [] for e in self.ENGS}
        self.writers = {}
        self.readers = {}
        self.ndma = {e: 0 for e in self.ENGS}
        self.bar = None
        self.pending_dma = []

    def op(self, eng, fn, reads=(), writes=(), dma=False, nobar=False):
        o = Op(eng, fn, dma)
        deps = {}
        for k in reads:
            for w in self.writers.get(k, ()):
                deps[id(w)] = w
        for k in writes:
            for w in self.writers.get(k, ()):
                deps[id(w)] = w
            for r in self.readers.get(k, ()):
                deps[id(r)] = r
        if eng == "pe":
            nobar = True
        if self.bar is not None and not nobar:
            deps[id(self.bar)] = self.bar
        for d in deps.values():
            if d is o:
                continue
            if (not d.dma) and d.eng == eng and (not dma):
                if eng == "pe" or not SAME_ENG_SYNC:
                    continue
            o.deps.append(d)
            d.need_inc = True
        for k in reads:
            lst = self.readers.setdefault(k, [])
            if not dma:
                lst[:] = [r for r in lst if r.dma or r.eng != eng]
            lst.append(o)
        for k in writes:
            self.writers[k] = [o]
            self.readers[k] = []
        if dma:
            i = self.ndma[eng]
            self.ndma[eng] += 1
            o.sem_i = i % NDSEM
            o.sem_val = 16 * (i // NDSEM + 1)
            if not nobar:
                self.pending_dma.append(o)
        self.ops[eng].append(o)
        return o

    def dma(self, eng, out, in_, reads=(), writes=(), nobar=False):
        return self.op(eng, lambda e: e.dma_start(out=out, in_=in_), reads, writes, dma=True, nobar=nobar)

    def barrier(self, scratch):
        o = Op("dve", lambda e: e.memset(scratch, 0.0), False)
        for e in ("pe", "act", "pool"):
            for d in reversed(self.ops[e]):
                if not d.dma:
                    o.deps.append(d)
                    d.need_inc = True
                    break
        for d in reversed(self.ops["dve"]):
            if not d.dma:
                if SAME_ENG_SYNC:
                    o.deps.append(d)
                    d.need_inc = True
                break
        if self.bar is not None:
            o.deps.append(self.bar)
        for d in self.pending_dma:
            o.deps.append(d)
        self.pending_dma = []
        o.need_inc = True
        self.ops["dve"].append(o)
        self.bar = o
        return o

    def emit(self):
        nc = self.nc
        with contextlib.ExitStack() as st:
            csem = {e: st.enter_context(nc.semaphore("c_" + e)) for e in ("pe", "act", "dve", "pool")}
            dsem = {e: [st.enter_context(nc.semaphore("d_%s%d" % (e, i))) for i in range(NDSEM)]
                    for e in self.ENGS if self.ndma[e] > 0}
            for e, lst in self.ops.items():
                c = 0
                for o in lst:
                    if o.dma:
                        continue
                    if o.need_inc:
                        c += 1
                        o.count = c
            block = st.enter_context(nc.Block())

            def run(ename, handle):
                waited = {}

                def wait(sem, val):
                    key = id(sem)
                    if waited.get(key, 0) >= val:
                        return
                    waited[key] = val
                    handle.wait_ge(sem, val)

                for o in self.ops[ename]:
                    for d in o.deps:
                        if d.dma:
                            wait(dsem[d.eng][d.sem_i], d.sem_val)
                        else:
                            wait(csem[d.eng], d.count)
                    if o.dma:
                        if o.sem_val > 16:
                            wait(dsem[ename][o.sem_i], o.sem_val - 16)
                        o.fn(handle).then_inc(dsem[ename][o.sem_i], 16)
                    else:
                        ins = o.fn(handle)
                        if o.need_inc:
                            ins.then_inc(csem[ename], 1)
                if ename in dsem:
                    n = self.ndma[ename]
                    for i in range(min(n, NDSEM)):
                        cnt = (n - 1 - i) // NDSEM + 1
                        wait(dsem[ename][i], 16 * cnt)

            if self.ops["pe"]:
                @block.tensor
                def _(eng):
                    run("pe", eng)
            if self.ops["act"]:
                @block.scalar
                def _(eng):
                    run("act", eng)
            if self.ops["dve"]:
                @block.vector
                def _(eng):
                    run("dve", eng)
            if self.ops["pool"]:
                @block.gpsimd
                def _(eng):
                    run("pool", eng)
            if self.ops["sp"]:
                @block.sync
                def _(eng):
                    run("sp", eng)


D = 1024
KC = 8
SEQ = 2048
CTXL = 256
T = SEQ + CTXL
NLAYER = 4
NCORES = 8
EPS = 1e-6
SM_SCALE = 96 ** -0.5
LN_SM = -0.5 * float(np.log(96.0))
TILES = [(0, 512), (512, 512), (1024, 512), (1536, 512), (2048, 256)]
SEGS = [(0, 2048), (2048, 256)]
DFF = 2816
NJ = 22
POOL_WIN = (2, 4, 8, 16)

VOFF = {}
_o = 0
for _n, _w in (("n1g", 8), ("n2g", 8), ("adab", 48), ("qng", 2), ("kvng", 1), ("lcw", 32), ("lcb", 8), ("lba", 16), ("lbi", 16),
               ("llam", 16), ("pb", 4), ("psc", 4), ("fcw", 66), ("fcb", 22), ("fng", 8)):
    VOFF[_n] = _o
    _o += _w
NV = _o

COL_KV, COL_KR, COL_UX, COL_Q, COL_UY, COL_POOL = 128, 160, 1184, 1440, 2464, 2976


def _win_perm():
    cols = []
    for n in range(8):
        cols += list(range(COL_KR + n * 128, COL_KR + (n + 1) * 128))
        cols += list(range(COL_Q + n * 128, COL_Q + (n + 1) * 128))
    cols += list(range(COL_UY, COL_POOL))
    cols += list(range(COL_POOL, COL_POOL + 3072))
    cols += list(range(0, 128))
    cols += list(range(COL_UX, COL_Q))
    return np.array(cols)


WIN_LRU0, WIN_POOL0, WIN_GATE0, WIN_MLA0, WIN_NCOL = 0, 2048, 2560, 5632, 6016
KR_SWAP = np.array(list(range(8, 16)) + list(range(0, 8)) + list(range(24, 32)) + list(range(16, 24)))


def build_program(nlayer=NLAYER, nb=2, dbg=None, full_last=False, stop_after=None):
    nc = bass.Bass("TRN2", target_bir_lowering=False)
    dt_in = lambda name, shape: nc.dram_tensor(name, shape, F32, kind="ExternalInput").ap()
    xT = dt_in("xT", [nb, D, T])
    cT = dt_in("cT", [128, KC, 3])
    vecs_d = dt_in("vecs", [128, nlayer, NV])
    ropeC_d = dt_in("ropeC", [32, T])
    ropeS_d = dt_in("ropeS", [32, T])
    ptab_d = dt_in("ptab", [128, 4, 16])
    ada_w = dt_in("ada_w", [nlayer, D, 6 * D])
    w_inP = dt_in("w_inP", [nlayer, D, WIN_NCOL])
    w_kr2 = dt_in("w_kr2", [nlayer, D, 64])
    w_uq2 = dt_in("w_uq2", [nlayer, 256, 2, 768])
    w_ukvP = dt_in("w_ukvP", [nlayer, 128, 1024])
    lru_wa = dt_in("lru_wa", [nlayer, 2, 8, 128, 128])
    lru_wi = dt_in("lru_wi", [nlayer, 2, 8, 128, 128])
    pool_w = dt_in("pool_w", [nlayer, 4, 128, 128])
    proj_mla = dt_in("proj_mla", [nlayer, 512, D])
    proj_lru = dt_in("proj_lru", [nlayer, D, D])
    proj_pool = dt_in("proj_pool", [nlayer, 512, D])
    w_out = dt_in("w_out", [nlayer, D, D])
    ffn_upP = dt_in("ffn_upP", [nlayer, D, 2 * DFF])
    ffn_down = dt_in("ffn_down", [nlayer, DFF, D])
    yT = nc.dram_tensor("yT", [nb, D, SEQ], F32, kind="ExternalOutput").ap()
    dbg_out = {}
    if dbg:
        for name, shape in dbg.items():
            dbg_out[name] = nc.dram_tensor("dbg_" + name, shape[1], shape[0], kind="ExternalOutput").ap()
    xs = nc.dram_tensor("xs", [nb, D, T], F32, kind="Internal").ap()
    recT_h = nc.dram_tensor("recT_h", [nb, D, T], BF16, kind="Internal").ap()
    poolT_h = nc.dram_tensor("poolT_h", [nb, 512, T], BF16, kind="Internal").ap()
    gsc_h = nc.dram_tensor("gsc_h", [nb, 8, 128, 3, T], BF16, kind="Internal").ap()
    actT_h = nc.dram_tensor("actT_h", [nb, 5, 128, NJ, 512], BF16, kind="Internal").ap()

    RSZ = KC * T
    VSZ = 18 * 8 * 65
    NWB = 5
    WBSZ = 4096
    WKSZ = 10240
    O_R1, O_R2, O_R3 = 0, RSZ, 2 * RSZ
    O_RV = 3 * RSZ
    O_WB = O_RV + VSZ
    O_WK = O_WB + NWB * WBSZ
    NA = O_WK + WKSZ

    with contextlib.ExitStack() as st:
        AR = st.enter_context(nc.sbuf_tensor("arena", [128, NA], BF16))
        vec = st.enter_context(nc.sbuf_tensor("vec", [128, nlayer, NV], F32))
        modt = st.enter_context(nc.sbuf_tensor("modt", [128, nlayer, 48, 3], F32))
        der = st.enter_context(nc.sbuf_tensor("der", [128, nlayer, 3, 2, 8], F32))
        lder = st.enter_context(nc.sbuf_tensor("lder", [128, nlayer, 5, 16], F32))
        hg1 = st.enter_context(nc.sbuf_tensor("hg1", [128, nlayer, 3, 8], F32))
        ltmp = st.enter_context(nc.sbuf_tensor("ltmp", [128, 4, 16], F32))
        ptab = st.enter_context(nc.sbuf_tensor("ptab_s", [128, 4, 16], F32))
        ones_bf = st.enter_context(nc.sbuf_tensor("ones_bf", [128, 128], BF16))
        ones_f = st.enter_context(nc.sbuf_tensor("ones_f", [128, 64], F32))
        wkr = st.enter_context(nc.sbuf_tensor("wkr", [128, KC, 2, 96], BF16))
        csb = st.enter_context(nc.sbuf_tensor("csb", [128, KC, 3], F32))
        scb = st.enter_context(nc.sbuf_tensor("scb", [128, KC, 3], BF16))
        bscr = st.enter_context(nc.sbuf_tensor("bscr", [128, 2], F32))
        PSALL = st.enter_context(nc.psum_tensor("psall", [128, 4096], F32))
        PS = [PSALL[:, i * 512:(i + 1) * 512] for i in range(8)]
        P = Prog(nc)

        def abf(off, *shape):
            n = int(np.prod(shape))
            ap = AR[:, off:off + n]
            if len(shape) == 2:
                return ap.rearrange("p (a b) -> p a b", a=shape[0])
            if len(shape) == 3:
                return ap.rearrange("p (a b c) -> p a b c", a=shape[0], b=shape[1])
            return ap

        def af32(off, *shape):
            n = int(np.prod(shape))
            ap = AR[:, off:off + 2 * n].bitcast(F32)
            if len(shape) == 2:
                return ap.rearrange("p (a b) -> p a b", a=shape[0])
            if len(shape) == 3:
                return ap.rearrange("p (a b c) -> p a b c", a=shape[0], b=shape[1])
            return ap

        R1 = abf(O_R1, KC, T)
        R2 = abf(O_R2, KC, T)
        R3 = abf(O_R3, KC, T)
        VV = abf(O_RV, 18, 8, 65)
        PLT = abf(O_RV, 4, T)

        psc = [0]

        def nextps():
            i = psc[0] % 7
            psc[0] += 1
            return i

        wb_rot = list(range(NWB))

        def wb_take():
            i = wb_rot.pop(0)
            wb_rot.append(i)
            return i

        def wb_pin():
            return wb_rot.pop(0)

        def wb_unpin(i):
            wb_rot.insert(0, i)

        def wb_ap(i, *shape):
            return abf(O_WB + i * WBSZ, *shape)

        def load_slab(dram_ap, i, shape, pslice=None):
            dst = wb_ap(i, *shape)
            if pslice is not None:
                dst = dst[pslice[0]:pslice[1]]
            P.dma("pool", dst, dram_ap, writes=[("WB", i)], nobar=True)
            return wb_ap(i, *shape)

        def mm(ps_ap, pairs, reads, pskey):
            n = len(pairs)
            for idx, (l, r) in enumerate(pairs):
                P.op("pe", lambda e, l=l, r=r, idx=idx: e.matmul(ps_ap, l, r, start=(idx == 0), stop=(idx == n - 1)),
                     reads=reads, writes=[pskey])

        def ACT(out, in_, func, reads, writes, bias=0.0, scale=1.0):
            P.op("act", lambda e: e.activation(out=out, in_=in_, func=func, bias=bias, scale=scale), reads=reads, writes=writes)

        def STT(out, in0, scalar, in1, op0, op1, reads, writes, eng="dve"):
            P.op(eng, lambda e: e.scalar_tensor_tensor(out=out, in0=in0, scalar=scalar, in1=in1, op0=op0, op1=op1), reads=reads, writes=writes)

        def TT(out, in0, in1, op, reads, writes, eng="dve"):
            P.op(eng, lambda e: e.tensor_tensor(out=out, in0=in0, in1=in1, op=op), reads=reads, writes=writes)

        def TS(out, in0, s1, s2, op0, op1, reads, writes, eng="dve"):
            P.op(eng, lambda e: e.tensor_scalar(out=out, in0=in0, scalar1=s1, scalar2=s2, op0=op0, op1=op1), reads=reads, writes=writes)

        def RECIP(out, in_, reads, writes):
            P.op("dve", lambda e: e.reciprocal(out=out, in_=in_), reads=reads, writes=writes)

        def V(l, name, i=0, n=1):
            o = VOFF[name] + i
            return vec[:, l, o:o + n]

        def dump(name, src_ap, reads, idx=None, sl=None):
            if name in dbg_out:
                dst = dbg_out[name] if idx is None else dbg_out[name][idx]
                if sl is not None:
                    dst = dst[:, sl[0]:sl[1]]
                P.dma("sp", dst, src_ap, reads=reads)

        P.dma("sp", vec[:], vecs_d, writes=["vec"])
        P.dma("sp", csb[:], cT, writes=["csb"])
        P.dma("sp", ptab[:], ptab_d, writes=["ptab"])
        P.op("dve", lambda e: e.memset(ones_bf[:], 1.0), writes=["ones_bf"])
        P.op("dve", lambda e: e.memset(ones_f[:], 1.0), writes=["ones_f"])
        P.op("dve", lambda e: e.memset(wkr[:], 0.0), writes=["wkr"])
        ACT(scb[:], csb[:], AF.Silu, ["csb"], ["scb"])
        psM = PS[7][:, 0:144].rearrange("p (j v) -> p j v", v=3)

        def ada_slab_load(l, s_):
            i = wb_take()
            wb = load_slab(ada_w[l][:, s_ * 512:(s_ + 1) * 512].rearrange("(k p) c -> p k c", p=128), i, (KC, 512))
            return i, wb

        def ada_slab_mm(l, s_, i, wb):
            for jj in range(4):
                j = s_ * 4 + jj
                mm(psM[:, j, :], [(wb[:, k, jj * 128:(jj + 1) * 128], scb[:, k, :]) for k in range(KC)], [("WB", i), "scb"], ("ps", 7))

        def ada_finish(l):
            for v in range(3):
                TT(modt[:, l, :, v], psM[:, :, v], V(l, "adab", 0, 48), ALU.add, [("ps", 7), "vec"], [("modt", l)])
            for v in range(3):
                for sub in range(2):
                    STT(der[:, l, v, sub, :], modt[:, l, (1 + 3 * sub) * 8:(2 + 3 * sub) * 8, v], 1.0, V(l, "n1g" if sub == 0 else "n2g", 0, 8),
                        ALU.add, ALU.mult, [("modt", l), "vec"], [("der", l)])
            e_ = ltmp[:, 0, :]
            w_ = ltmp[:, 1, :]
            w2 = ltmp[:, 2, :]
            pl = ltmp[:, 3, :]
            ACT(e_, V(l, "llam", 0, 16), AF.Exp, ["vec"], ["ltmp"], scale=-1.0)
            TS(w_, e_, 2.0, None, ALU.add, ALU.bypass, ["ltmp"], ["ltmp"])
            RECIP(w_, w_, ["ltmp"], ["ltmp"])
            TT(w_, w_, e_, ALU.mult, ["ltmp"], ["ltmp"])
            TT(w2, w_, w_, ALU.mult, ["ltmp"], ["ltmp"])
            TS(pl, w2, 1.0 / 9.0, 1.0 / 7.0, ALU.mult, ALU.add, ["ltmp"], ["ltmp"])
            for cf in (1.0 / 5.0, 1.0 / 3.0, 1.0):
                TT(pl, pl, w2, ALU.mult, ["ltmp"], ["ltmp"])
                TS(pl, pl, cf, None, ALU.add, ALU.bypass, ["ltmp"], ["ltmp"])
            TT(pl, pl, w_, ALU.mult, ["ltmp"], ["ltmp"])
            TS(lder[:, l, 0, :], pl, -8.0, None, ALU.mult, ALU.bypass, ["ltmp"], [("lder", l)])
            TS(lder[:, l, 1, :], pl, -16.0, None, ALU.mult, ALU.bypass, ["ltmp"], [("lder", l)])
            TT(lder[:, l, 2, 0:4], V(l, "pb", 0, 4), V(l, "psc", 0, 4), ALU.mult, ["vec"], [("lder", l)])
            for v in range(3):
                TS(hg1[:, l, v, :], modt[:, l, 16:24, v], 0.5, None, ALU.mult, ALU.bypass, [("modt", l)], [("der", l)])
            TS(lder[:, l, 3, :], V(l, "lba", 0, 16), 0.5, None, ALU.mult, ALU.bypass, ["vec"], [("lder", l)])
            TS(lder[:, l, 4, :], V(l, "lbi", 0, 16), 0.5, None, ALU.mult, ALU.bypass, ["vec"], [("lder", l)])

        for s_ in range(12):
            i_, wb_ = ada_slab_load(0, s_)
            ada_slab_mm(0, s_, i_, wb_)
        ada_finish(0)
        dump("mods", modt[:, 0, :, :], [("modt", 0)])

        def G(l, v, sub, k):
            return der[:, l, v, sub, k:k + 1]

        def MOD(l, v, m, k):
            return modt[:, l, m * 8 + k, v:v + 1]

        WK = O_WK
        norm_ctr = [0]

        def norm_tile(l, v, sub, xt, tn, psq_i, dst, dkeys, rkeys):
            par = norm_ctr[0] % 2
            norm_ctr[0] += 1
            rs = af32(WK + par * 5120, 512)[:, :tn]
            ACT(rs, PS[psq_i][:, :tn], AF.Ln, [("ps", psq_i)], [("rs", par)], bias=EPS, scale=1.0 / D)
            ACT(rs, rs, AF.Exp, [("rs", par)], [("rs", par)], scale=-0.5)
            for k in range(KC):
                tmp = af32(WK + 1024 + (k % 2) * 1024, 512)[:, :tn]
                STT(tmp, xt[:, k, :], G(l, v, sub, k), rs, ALU.mult, ALU.mult, rkeys(k) + [("rs", par), ("der", l)], [("ntmp", k % 2)])
                ACT(dst(k), tmp, AF.Identity, [("ntmp", k % 2), ("modt", l)], [dkeys(k)], bias=MOD(l, v, 3 * sub, k))

        def layer_batch(l, b):
            last = (l == nlayer - 1)
            tiles = TILES
            skipc = last and not full_last
            tiles_e = TILES[:4] if skipc else TILES
            segs_e = SEGS[:1] if skipc else SEGS
            xsrc = xT if l == 0 else xs
            phase = [0]

            class _Stop(Exception):
                pass

            def B():
                phase[0] += 1
                if stop_after is not None and phase[0] > stop_after:
                    raise _Stop()
                P.barrier(bscr[:, 0:1])

            B()
            for ti, (t0, tn) in enumerate(tiles):
                v = 2 if ti == 4 else b
                xt = af32(O_R2 + (ti % 2) * 8192, KC, 512)[:, :, :tn]
                P.dma("sp", xt, xsrc[b][:, t0:t0 + tn].rearrange("(k p) t -> p k t", p=128), reads=([("xs", b, ti)] if l > 0 else []),
                      writes=[("xt", ti % 2)])
                sq = abf(O_R3 + (ti % 2) * 4096, KC, 512)[:, :, :tn]
                ACT(sq, xt, AF.Square, [("xt", ti % 2)], [("sq", ti % 2)])
                pi = nextps()
                mm(PS[pi][:, :tn], [(ones_bf[:], sq[:, k, :]) for k in range(KC)], [("sq", ti % 2), "ones_bf"], ("ps", pi))
                norm_tile(l, v, 0, xt, tn, pi, lambda k, t0=t0, tn=tn: R1[:, k, t0:t0 + tn], lambda k, ti=ti: ("R1", k, ti), lambda k, ti=ti: [("xt", ti % 2)])
            if l == 0 and b == 0:
                for k in range(KC):
                    dump("hx", R1[:, k, :], [("R1", k, ti) for ti in range(5)], idx=k)

            R1K = lambda ti: [("R1", k, ti) for k in range(KC)]
            BK = lambda i: [("B", i, t) for t in range(5)]

            def scan(out, a, bx, init, reads, writes):
                P.op("dve", lambda e: e.tensor_tensor_scan(out=out, data0=a, data1=bx, initial=init, op0=ALU.mult, op1=ALU.add),
                     reads=reads, writes=writes)

            gate_state = {"slab": None, "q": 0}

            def gate_block():
                q = gate_state["q"]
                gate_state["q"] += 1
                s_, jj = q // 4, q % 4
                if jj == 0:
                    gi = wb_take()
                    gwb = load_slab(w_inP[l][:, WIN_GATE0 + s_ * 512:WIN_GATE0 + (s_ + 1) * 512].rearrange("(k p) c -> p k c", p=128), gi, (KC, 512))
                    gate_state["slab"] = (gi, gwb)
                gi, gwb = gate_state["slab"]
                j, n = q // 8, q % 8
                gst = abf(O_RV + (q % 2) * T, T)
                for ti, (t0, tn) in enumerate(tiles):
                    pi = nextps()
                    mm(PS[pi][:, :tn], [(gwb[:, k, jj * 128:(jj + 1) * 128], R1[:, k, t0:t0 + tn]) for k in range(KC)], [("WB", gi)] + R1K(ti), ("ps", pi))
                    ACT(gst[:, t0:t0 + tn], PS[pi][:, :tn], AF.Tanh, [("ps", pi)], [("gst", q % 2)], scale=0.5)
                P.dma("sp", gsc_h[b][n, :, j, :], gst, reads=[("gst", q % 2)], writes=[("gsc", b, n, j)])

            B()
            i_wa = wb_pin()
            i_wi = wb_pin()
            lwa = load_slab(lru_wa[l].rearrange("d n c e -> c d n e"), i_wa, (2, 8, 128))
            lwi = load_slab(lru_wi[l].rearrange("d n c e -> c d n e"), i_wi, (2, 8, 128))
            Bf = lambda i: af32(O_R2 + i * 4608, T)
            ub = abf(WK, T)
            rst = abf(WK + T, T)
            gls = [abf(WK + 2 * T, T), abf(WK + 3 * T, T)]
            XB = af32(O_RV + 2 * T, T)
            XK_ = [("X", t) for t in range(5)]
            lru_slab = [None]

            def lru_front(n):
                s_, cc = n // 2, n % 2
                if cc == 0:
                    i_ = wb_take()
                    lru_slab[0] = (i_, load_slab(w_inP[l][:, WIN_LRU0 + s_ * 512:WIN_LRU0 + (s_ + 1) * 512].rearrange("(k p) c -> p k c", p=128), i_, (KC, 512)))
                i, wb = lru_slab[0]
                gl = gls[n % 2]
                for ti, (t0, tn) in enumerate(tiles):
                    pi = nextps()
                    mm(PS[pi][:, :tn], [(wb[:, k, 2 * cc * 128:(2 * cc + 1) * 128], R1[:, k, t0:t0 + tn]) for k in range(KC)],
                       [("WB", i)] + R1K(ti), ("ps", pi))
                    P.op("dve", lambda e, pi=pi, t0=t0, tn=tn: e.tensor_copy(out=XB[:, t0:t0 + tn], in_=PS[pi][:, :tn]), reads=[("ps", pi)], writes=[("X", ti)])
                for ti, (t0, tn) in enumerate(tiles):
                    pi = nextps()
                    mm(PS[pi][:, :tn], [(wb[:, k, (2 * cc + 1) * 128:(2 * cc + 2) * 128], R1[:, k, t0:t0 + tn]) for k in range(KC)],
                       [("WB", i)] + R1K(ti), ("ps", pi))
                    ACT(gl[:, t0:t0 + tn], PS[pi][:, :tn], AF.Gelu_apprx_tanh, [("ps", pi)], [("gl", n % 2, ti)])

            def lru_conv(n):
                for (s0, sn) in SEGS:
                    TS(Bf(1)[:, s0:s0 + sn], XB[:, s0:s0 + sn], V(l, "lcw", 2 * 8 + n), V(l, "lcb", n), ALU.mult, ALU.add,
                       XK_ + ["vec"], BK(1))
                    for k in (0, 1, 3):
                        o = k - 2
                        a = max(0, -o)
                        e_ = sn - max(0, o)
                        STT(Bf(1)[:, s0 + a:s0 + e_], XB[:, s0 + a + o:s0 + e_ + o], V(l, "lcw", k * 8 + n), Bf(1)[:, s0 + a:s0 + e_],
                            ALU.mult, ALU.add, XK_ + BK(1) + ["vec"], BK(1))
                if l == 0 and b == 0 and n == 0:
                    dump("u0", Bf(1), BK(1))
                P.op("dve", lambda e: e.tensor_copy(out=ub, in_=Bf(1)), reads=BK(1), writes=["ub"])

            def lru_gates(n):
                for d in range(2):
                    br, bi_ = 2 + 3 * d, 4 + 3 * d
                    for ti, (t0, tn) in enumerate(tiles):
                        pr = nextps()
                        mm(PS[pr][:, :tn], [(lwa[:, d, n, :], ub[:, t0:t0 + tn])], [("WB", i_wa), "ub"], ("ps", pr))
                        ACT(Bf(br)[:, t0:t0 + tn], PS[pr][:, :tn], AF.Tanh, [("ps", pr), ("lder", l)], [("B", br, ti)],
                            bias=lder[:, l, 3, d * 8 + n:d * 8 + n + 1], scale=0.5)
                        pq = nextps()
                        mm(PS[pq][:, :tn], [(lwi[:, d, n, :], ub[:, t0:t0 + tn])], [("WB", i_wi), "ub"], ("ps", pq))
                        ACT(Bf(bi_)[:, t0:t0 + tn], PS[pq][:, :tn], AF.Tanh, [("ps", pq), ("lder", l)], [("B", bi_, ti)],
                            bias=lder[:, l, 4, d * 8 + n:d * 8 + n + 1], scale=0.5)
                for d in range(2):
                    br, b2 = 2 + 3 * d, 3 + 3 * d
                    la_h = lder[:, l, 0, d * 8 + n:d * 8 + n + 1]
                    la_f = lder[:, l, 1, d * 8 + n:d * 8 + n + 1]
                    ACT(Bf(b2), Bf(br), AF.Exp, BK(br) + [("lder", l)], BK(b2), scale=la_f, bias=la_f)
                    ACT(Bf(br), Bf(br), AF.Exp, BK(br) + [("lder", l)], BK(br), scale=la_h, bias=la_h)
                for d in range(2):
                    b2 = 3 + 3 * d
                    ACT(Bf(b2), Bf(b2), AF.Sqrt, BK(b2), BK(b2), scale=-0.25, bias=0.25 + 2.5e-7)

            def lru_bx(m):
                for d in range(2):
                    b2, bi_ = 3 + 3 * d, 4 + 3 * d
                    STT(Bf(bi_), Bf(bi_), 1.0, Bf(b2), ALU.add, ALU.mult, BK(bi_) + BK(b2), BK(bi_))
                    TT(Bf(bi_), Bf(bi_), Bf(1), ALU.mult, BK(bi_) + BK(1), BK(bi_))

            def lru_scan(m):
                for d in range(2):
                    br, bi_ = 2 + 3 * d, 4 + 3 * d
                    hb = 0 if d == 0 else 3
                    H = Bf(hb)
                    rk = BK(br) + BK(bi_)
                    if d == 0:
                        scan(H[:, SEQ:T], Bf(br)[:, SEQ:T], Bf(bi_)[:, SEQ:T], 0.0, rk, BK(hb))
                        scan(H[:, 0:SEQ], Bf(br)[:, 0:SEQ], Bf(bi_)[:, 0:SEQ], H[:, T - 1:T], rk + BK(hb), BK(hb))
                    else:
                        scan(H[:, SEQ:T][:, ::-1], Bf(br)[:, SEQ:T][:, ::-1], Bf(bi_)[:, SEQ:T][:, ::-1], 0.0, rk, BK(hb))
                        scan(H[:, 0:SEQ][:, ::-1], Bf(br)[:, 0:SEQ][:, ::-1], Bf(bi_)[:, 0:SEQ][:, ::-1], H[:, SEQ:SEQ + 1], rk + BK(hb), BK(hb))

            def lru_out(m):
                TT(Bf(0), Bf(0), Bf(3), ALU.add, BK(0) + BK(3), BK(0))
                TT(rst, Bf(0), gls[m % 2], ALU.mult, BK(0) + [("gl", m % 2, t) for t in range(5)], ["rst"])
                P.dma("sp", recT_h[b][m * 128:(m + 1) * 128, :], rst, reads=["rst"], writes=[("recT", b, m)])

            for it_ in range(9):
                if it_ < 8:
                    lru_front(it_)
                    for _ in range(2):
                        gate_block()
                if it_ >= 1:
                    lru_bx(it_ - 1)
                if it_ < 8:
                    lru_conv(it_)
                else:
                    for _ in range(8):
                        gate_block()
                if it_ >= 1:
                    lru_scan(it_ - 1)
                    lru_out(it_ - 1)
                if it_ < 8:
                    lru_gates(it_)
            wb_unpin(i_wi)
            wb_unpin(i_wa)
            if l == 0 and b == 0:
                dump("rec", recT_h[b], [("recT", b, n) for n in range(8)])

            B()
            i_pw = wb_pin()
            pw = load_slab(pool_w[l].rearrange("g c d -> c g d"), i_pw, (4, 128))
            i = wb_take()
            wb = load_slab(w_inP[l][:, WIN_POOL0:WIN_POOL0 + 512].rearrange("(k p) c -> p k c", p=128), i, (KC, 512))
            PW, LB, CB = 2368, 16, 2096
            Pa = [af32(O_R2, PW), af32(O_R2 + 2 * PW, PW)]
            Pb = af32(O_R2 + 4 * PW, PW)
            Pc = af32(O_R2 + 6 * PW, PW)
            tmpe = af32(WK + 3 * T, 16)
            for q in range(2):
                P.op("dve", lambda e, q=q: e.memset(Pa[q], 0.0), writes=[("Pa", q)])
            dbs = [abf(WK, T), abf(WK + 3 * T + 64, T)]

            def pool_front(g):
                U = Pa[g % 2]
                for ti, (t0, tn) in enumerate(tiles):
                    pi = nextps()
                    mm(PS[pi][:, :tn], [(wb[:, k, g * 128:(g + 1) * 128], R1[:, k, t0:t0 + tn]) for k in range(KC)], [("WB", i)] + R1K(ti), ("ps", pi))
                    base = (LB + t0) if ti < 4 else CB
                    ACT(U[:, base:base + tn], PS[pi][:, :tn], AF.Identity, [("ps", pi)], [("Pa", g % 2)])

            def pool_mid(g):
                w = POOL_WIN[g]
                left = w // 2
                right = w - 1 - left
                U = Pa[g % 2]
                db = dbs[g % 2]
                dbk = ("db", g % 2)
                src, skey = U, ("Pa", g % 2)
                m = 1
                lvl = 0
                while m < w:
                    dst, dkey = (Pb, "Pb") if lvl % 2 == 0 else (Pc, "Pc")
                    TT(dst[:, 0:PW - m], src[:, 0:PW - m], src[:, m:PW], ALU.add, [skey], [dkey])
                    src, skey = dst, dkey
                    m *= 2
                    lvl += 1
                Aw, akey = src, skey
                for (base, s0, sn) in ((LB, 0, SEQ), (CB, SEQ, CTXL)):
                    STT(db[:, s0:s0 + sn], Aw[:, base - left:base - left + sn], 1.0 / w, U[:, base:base + sn], ALU.mult, ALU.subtract,
                        [akey, ("Pa", g % 2)], [dbk])
                    TT(tmpe[:, 0:left], Aw[:, base - left:base], ptab[:, g, 0:left], ALU.mult, [akey, "ptab"], ["tmpe"])
                    TT(db[:, s0:s0 + left], tmpe[:, 0:left], U[:, base:base + left], ALU.subtract, ["tmpe", ("Pa", g % 2)], [dbk])
                    if right > 0:
                        TT(tmpe[:, 8:8 + right], Aw[:, base - left + sn - right:base - left + sn], ptab[:, g, 8:8 + right], ALU.mult,
                           [akey, "ptab"], ["tmpe"])
                        TT(db[:, s0 + sn - right:s0 + sn], tmpe[:, 8:8 + right], U[:, base + sn - right:base + sn], ALU.subtract,
                           ["tmpe", ("Pa", g % 2)], [dbk])

            def pool_back(g):
                db = dbs[g % 2]
                pst = abf(WK + T + (g % 2) * T, T)
                for ti, (t0, tn) in enumerate(tiles):
                    pi = nextps()
                    mm(PS[pi][:, :tn], [(pw[:, g, :], db[:, t0:t0 + tn])], [("WB", i_pw), ("db", g % 2)], ("ps", pi))
                    ACT(pst[:, t0:t0 + tn], PS[pi][:, :tn], AF.Identity, [("ps", pi), "vec", ("lder", l)], [("pst", g % 2)],
                        scale=V(l, "psc", g), bias=lder[:, l, 2, g:g + 1])
                P.dma("sp", poolT_h[b][g * 128:(g + 1) * 128, :], pst, reads=[("pst", g % 2)], writes=[("poolT", b, g)])

            pool_front(0)
            for g in range(4):
                if g + 1 < 4:
                    pool_front(g + 1)
                pool_mid(g)
                pool_back(g)
            wb_unpin(i_pw)
            if l == 0 and b == 0:
                dump("pool", poolT_h[b], [("poolT", b, g) for g in range(4)])

            def COPY(out, in_, reads, writes, which):
                if which % 2 == 0:
                    ACT(out, in_, AF.Identity, reads, writes)
                else:
                    P.op("dve", lambda e: e.tensor_copy(out=out, in_=in_), reads=reads, writes=writes)

            B()
            i_uq = wb_pin()
            wuq = load_slab(w_uq2[l].rearrange("(k p) v c -> p k v c", p=128), i_uq, (2, 2, 768))
            i_kv = wb_pin()
            wukv = load_slab(w_ukvP[l], i_kv, (1024,))
            i = wb_take()
            wb = load_slab(w_inP[l][:, WIN_MLA0:WIN_MLA0 + 384].rearrange("(k p) c -> p k c", p=128), i, (KC, 384))
            P.dma("pool", wkr[:, :, 0, 64:96], w_kr2[l][:, 0:32].rearrange("(k p) c -> p k c", p=128), writes=["wkr"], nobar=True)
            P.dma("pool", wkr[:, :, 1, 64:96], w_kr2[l][:, 32:64].rearrange("(k p) c -> p k c", p=128), writes=["wkr"], nobar=True)
            RVK = [("RV", kt) for kt in range(18)]
            P.op("dve", lambda e: e.memset(VV[:, :, :, 64:65], 1.0), writes=RVK + ["RVp"])
            ckv = af32(WK, 512)
            rs = af32(WK + 1024, 512)
            sqk = abf(WK + 2048, 512)
            ckvn = abf(WK + 2560, 512)
            cq = af32(WK + 3072, 2, 512)
            sqq = abf(WK + 5120, 2, 512)
            cqn = abf(WK + 6144, 2, 512)
            Ct = af32(WK + 7168, 512)
            St = af32(WK + 8192, 512)
            t1 = af32(WK + 9216, 512)
            rs2 = rs
            t2 = cq[:, 0, :]
            tq1 = ckv
            tq2 = cq[:, 1, :]
            cw = 0
            for ti, (t0, tn) in enumerate(tiles):
                pi = nextps()
                mm(PS[pi][:, :tn], [(wb[:, k, 0:128], R1[:, k, t0:t0 + tn]) for k in range(KC)], [("WB", i)] + R1K(ti), ("ps", pi))
                ACT(ckv[:, :tn], PS[pi][:, :tn], AF.Identity, [("ps", pi)], ["ckv"])
                ACT(sqk[:, :tn], PS[pi][:, :tn], AF.Square, [("ps", pi)], ["sqk"])
                for c2 in range(2):
                    p5 = nextps()
                    mm(PS[p5][:, :tn], [(wb[:, k, 128 + c2 * 128:256 + c2 * 128], R1[:, k, t0:t0 + tn]) for k in range(KC)],
                       [("WB", i)] + R1K(ti), ("ps", p5))
                    ACT(cq[:, c2, :tn], PS[p5][:, :tn], AF.Identity, [("ps", p5)], [("cq", c2)])
                    ACT(sqq[:, c2, :tn], PS[p5][:, :tn], AF.Square, [("ps", p5)], ["sqq"])
                pa = nextps()
                mm(PS[pa][0:96, :tn], [(wkr[:, k, 0, :], R1[:, k, t0:t0 + tn]) for k in range(KC)], ["wkr"] + R1K(ti), ("ps", pa))
                pb_ = nextps()
                mm(PS[pb_][0:96, :tn], [(wkr[:, k, 1, :], R1[:, k, t0:t0 + tn]) for k in range(KC)], ["wkr"] + R1K(ti), ("ps", pb_))
                P.dma("sp", Ct[64:96, :tn], ropeC_d[:, t0:t0 + tn], writes=["Ct"])
                P.dma("sp", St[64:96, :tn], ropeS_d[:, t0:t0 + tn], writes=["St"])
                p2 = nextps()
                mm(PS[p2][:, :tn], [(ones_bf[:], sqk[:, :tn])], ["sqk", "ones_bf"], ("ps", p2))
                ACT(rs[:, :tn], PS[p2][:, :tn], AF.Ln, [("ps", p2)], ["rs"], bias=EPS, scale=1.0 / 128)
                ACT(rs[:, :tn], rs[:, :tn], AF.Exp, ["rs"], ["rs"], scale=-0.5)
                STT(ckvn[:, :tn], ckv[:, :tn], V(l, "kvng", 0), rs[:, :tn], ALU.mult, ALU.mult, ["ckv", "rs", "vec"], ["ckvn"])
                p6 = nextps()
                mm(PS[p6][:, :tn], [(ones_bf[:], sqq[:, 0, :tn]), (ones_bf[:], sqq[:, 1, :tn])], ["sqq", "ones_bf"], ("ps", p6))
                ACT(rs2[:, :tn], PS[p6][:, :tn], AF.Ln, [("ps", p6)], ["rs"], bias=EPS, scale=1.0 / 256)
                ACT(rs2[:, :tn], rs2[:, :tn], AF.Exp, ["rs"], ["rs"], scale=-0.5, bias=LN_SM)
                for c2 in range(2):
                    STT(cqn[:, c2, :tn], cq[:, c2, :tn], V(l, "qng", c2), rs2[:, :tn], ALU.mult, ALU.mult, [("cq", c2), "rs", "vec"], ["cqn"])
                TT(t1[64:96, :tn], PS[pa][64:96, :tn], Ct[64:96, :tn], ALU.mult, [("ps", pa), "Ct"], ["t1"])
                TT(t2[64:96, :tn], PS[pb_][64:96, :tn], St[64:96, :tn], ALU.mult, [("ps", pb_), "St", "cqn"], [("cq", 0)])
                TT(t1[64:96, :tn], t1[64:96, :tn], t2[64:96, :tn], ALU.add, ["t1", ("cq", 0)], ["t1"])
                if l == 0 and b == 0:
                    dump("kr", t1[64:96, :tn], ["t1"], sl=(t0, t0 + tn))
                for h in range(8):
                    COPY(R2[64:96, h, t0:t0 + tn], t1[64:96, :tn], ["t1"], [("R2", h, ti)], cw)
                    cw += 1
                for h in range(8):
                    p3 = nextps()
                    mm(PS[p3][0:64, :tn], [(wukv[:, h * 64:(h + 1) * 64], ckvn[:, :tn])], [("WB", i_kv), "ckvn"], ("ps", p3))
                    COPY(R2[0:64, h, t0:t0 + tn], PS[p3][0:64, :tn], [("ps", p3)], [("R2", h, ti)], cw)
                    cw += 1
                for sub in range(tn // 128):
                    kt = t0 // 128 + sub
                    p4 = nextps()
                    mm(PS[p4][:, 0:512], [(ckvn[:, sub * 128:(sub + 1) * 128], wukv[:, 512:1024])], [("WB", i_kv), "ckvn"], ("ps", p4))
                    COPY(VV[:, kt, :, 0:64], PS[p4][:, 0:512].rearrange("p (h d) -> p h d", h=8), [("ps", p4)], [("RV", kt), "RVp"], cw)
                    cw += 1
                for h in range(8):
                    pA = nextps()
                    mm(PS[pA][0:96, :tn], [(wuq[:, k, 0, h * 96:(h + 1) * 96], cqn[:, k, :tn]) for k in range(2)], [("WB", i_uq), "cqn"], ("ps", pA))
                    pB = nextps()
                    mm(PS[pB][0:96, :tn], [(wuq[:, k, 1, h * 96:(h + 1) * 96], cqn[:, k, :tn]) for k in range(2)], [("WB", i_uq), "cqn"], ("ps", pB))
                    ACT(R3[0:64, h, t0:t0 + tn], PS[pA][0:64, :tn], AF.Identity, [("ps", pA)], [("R3", h, ti)])
                    TT(tq1[64:96, :tn], PS[pA][64:96, :tn], Ct[64:96, :tn], ALU.mult, [("ps", pA), "Ct"], ["ckv"])
                    TT(tq2[64:96, :tn], PS[pB][64:96, :tn], St[64:96, :tn], ALU.mult, [("ps", pB), "St"], [("cq", 1)])
                    TT(R3[64:96, h, t0:t0 + tn], tq1[64:96, :tn], tq2[64:96, :tn], ALU.add, ["ckv", ("cq", 1)], [("R3", h, ti)])
            wb_unpin(i_kv)
            wb_unpin(i_uq)

            if l == 0 and b == 0:
                for h in range(8):
                    dump("K", R2[0:96, h, :], [("R2", h, t) for t in range(5)], idx=h)
                    dump("Q", R3[0:96, h, :], [("R3", h, t) for t in range(5)], idx=h)
                dump("Vv", AR[:, O_RV:O_RV + VSZ], RVK)
            B()
            PTs = [abf(WK + q * 1024, 1024) for q in range(3)]
            rc = af32(WK + 3072, 1024)
            osb = af32(WK + 5120, 1024)
            sctr = 0
            groups = [[0, 1], [2, 3]] + ([] if skipc else [[4]])
            for h in range(8):
                for qis in groups:
                    kts = list(range(18)) if qis[0] < 4 else [16, 17]
                    nk = len(kts)
                    W = sum(tiles[qi][1] for qi in qis)
                    pend = []
                    for step in range(nk + 1):
                        if step < nk:
                            kt = kts[step]
                            sb = sctr % 3
                            sctr += 1
                            for jq, qi in enumerate(qis):
                                q0, qn = tiles[qi]
                                P.op("pe", lambda e, sb=sb, jq=jq, qn=qn, q0=q0, kt=kt, h=h: e.matmul(
                                    PSALL[:, sb * 1024 + jq * 512:sb * 1024 + jq * 512 + qn], R2[0:96, h, kt * 128:(kt + 1) * 128],
                                    R3[0:96, h, q0:q0 + qn], start=True, stop=True),
                                    reads=[("R2", h, kt // 4), ("R3", h, qi)], writes=[("ps", 2 * sb), ("ps", 2 * sb + 1)])
                            ACT(PTs[sb][:, :W], PSALL[:, sb * 1024:sb * 1024 + W], AF.Exp, [("ps", 2 * sb), ("ps", 2 * sb + 1)], [("PT", sb)])
                            pend.append((kt, sb))
                        if step > 0:
                            kt, sb = pend.pop(0)
                            for jq, qi in enumerate(qis):
                                q0, qn = tiles[qi]
                                P.op("pe", lambda e, kt=kt, sb=sb, step=step, jq=jq, h=h, qn=qn, nk=nk: e.matmul(
                                    PS[6 + jq][0:65, :qn], VV[:, kt, h, 0:65], PTs[sb][:, jq * 512:jq * 512 + qn], start=(step == 1), stop=(step == nk)),
                                    reads=[("RV", kt), ("PT", sb)], writes=[("ps", 6 + jq)])
                    for jq, qi in enumerate(qis):
                        q0, qn = tiles[qi]
                        cs = slice(jq * 512, jq * 512 + qn)
                        ACT(rc[64:65, cs], PS[6 + jq][64:65, :qn], AF.Ln, [("ps", 6 + jq)], [("rc", jq)])
                        ACT(rc[64:65, cs], rc[64:65, cs], AF.Exp, [("rc", jq)], [("rc", jq)], scale=-1.0)
                        P.op("dve", lambda e, jq=jq, qn=qn, cs=cs: e.tensor_copy(out=osb[0:64, cs], in_=PS[6 + jq][0:64, :qn]),
                             reads=[("ps", 6 + jq)], writes=[("osb", jq)])
                        sb = sctr % 3
                        sctr += 1
                        P.op("pe", lambda e, sb=sb, qn=qn, cs=cs: e.matmul(PSALL[0:64, sb * 1024:sb * 1024 + qn], ones_f[64:65, 0:64], rc[64:65, cs], start=True, stop=True),
                             reads=[("rc", jq), "ones_f"], writes=[("ps", 2 * sb), ("ps", 2 * sb + 1)])
                        TT(R1[0:64, h, q0:q0 + qn], osb[0:64, cs], PSALL[0:64, sb * 1024:sb * 1024 + qn], ALU.mult, [("osb", jq), ("ps", 2 * sb), ("ps", 2 * sb + 1)], [("R1", h, qi)])
            if l == 0 and b == 0:
                for h in range(8):
                    dump("att", R1[0:64, h, :], [("R1", h, t) for t in range(5)], idx=h)

            B()
            R3ALL = [("R3", k, t) for k in range(KC) for t in range(5)]
            P.dma("sp", R3, recT_h[b].rearrange("(k p) t -> p k t", p=128), reads=[("recT", b, n) for n in range(8)], writes=R3ALL)
            P.dma("sp", PLT, poolT_h[b].rearrange("(g p) t -> p g t", p=128), reads=[("poolT", b, g) for g in range(4)], writes=["RVp"] + RVK)

            gts = [abf(WK + q * 1536, 3, 512) for q in range(3)]
            tAs = [af32(WK + 4608 + q * 2048, 512) for q in range(2)]
            tBs = [af32(WK + 5632 + q * 2048, 512) for q in range(2)]
            gcnt = 0
            for grp in range(2):
                ia = wb_take()
                wa = load_slab(proj_mla[l][:, grp * 512:(grp + 1) * 512].rearrange("(h d) n -> d h n", d=64), ia, (8, 512), pslice=(0, 64))
                ir = wb_take()
                wr = load_slab(proj_lru[l][:, grp * 512:(grp + 1) * 512].rearrange("(k p) n -> p k n", p=128), ir, (8, 512))
                ip = wb_take()
                wp = load_slab(proj_pool[l][:, grp * 512:(grp + 1) * 512].rearrange("(g p) n -> p g n", p=128), ip, (4, 512))
                for nn in range(4):
                    n = grp * 4 + nn
                    for ti, (t0, tn) in enumerate(tiles_e):
                        gq = gcnt % 3
                        gt = gts[gq]
                        P.dma("sp", gt[:, :, :tn], gsc_h[b][n, :, :, t0:t0 + tn], reads=[("gsc", b, n, j) for j in range(3)], writes=[("gt", gq)])
                        pA = nextps()
                        mm(PS[pA][:, :tn], [(wa[0:64, h, nn * 128:(nn + 1) * 128], R1[0:64, h, t0:t0 + tn]) for h in range(8)],
                           [("WB", ia)] + R1K(ti), ("ps", pA))
                        pR = nextps()
                        mm(PS[pR][:, :tn], [(wr[:, k, nn * 128:(nn + 1) * 128], R3[:, k, t0:t0 + tn]) for k in range(KC)],
                           [("WB", ir)] + [("R3", k, ti) for k in range(KC)], ("ps", pR))
                        pP = nextps()
                        mm(PS[pP][:, :tn], [(wp[:, g, nn * 128:(nn + 1) * 128], PLT[:, g, t0:t0 + tn]) for g in range(4)],
                           [("WB", ip), "RVp"], ("ps", pP))
                        par = gcnt % 2
                        tA = tAs[par][:, :tn]
                        tB = tBs[par][:, :tn]
                        STT(tA, gt[:, 0, :tn], 1.0, PS[pA][:, :tn], ALU.add, ALU.mult, [("ps", pA), ("gt", gq)], [("tA", par)])
                        STT(tB, gt[:, 1, :tn], 1.0, PS[pR][:, :tn], ALU.add, ALU.mult, [("ps", pR), ("gt", gq)], [("tB", par)])
                        TT(tA, tA, tB, ALU.add, [("tA", par), ("tB", par)], [("tA", par)])
                        STT(tB, gt[:, 2, :tn], 1.0, PS[pP][:, :tn], ALU.add, ALU.mult, [("ps", pP), ("gt", gq)], [("tB", par)])
                        TT(R2[:, n, t0:t0 + tn], tA, tB, ALU.add, [("tA", par), ("tB", par)], [("R2", n, ti)])
                        gcnt += 1

            B()
            i0 = wb_pin()
            wo0 = load_slab(w_out[l][:, 0:512].rearrange("(k p) n -> p k n", p=128), i0, (KC, 512))
            i1 = wb_pin()
            wo1 = load_slab(w_out[l][:, 512:1024].rearrange("(k p) n -> p k n", p=128), i1, (KC, 512))
            wos = [(wo0, i0), (wo1, i1)]
            xts = [af32(O_R3 + q * 8192, KC, 512) for q in range(2)]
            sqs = [abf(WK + 3072 + q * 512, 512) for q in range(4)]
            XK = lambda q: [("xt", q, k) for k in range(KC)]
            def s9_load(ti_):
                t0_, tn_ = tiles_e[ti_]
                P.dma("sp", xts[ti_ % 2][:, :, :tn_], xsrc[b][:, t0_:t0_ + tn_].rearrange("(k p) t -> p k t", p=128),
                      reads=([("xs", b, ti_)] if l > 0 else []), writes=XK(ti_ % 2))

            s9_load(0)
            for ti, (t0, tn) in enumerate(tiles_e):
                v = 2 if ti == 4 else b
                q = ti % 2
                xt = xts[q][:, :, :tn]
                if ti + 1 < len(tiles_e):
                    s9_load(ti + 1)
                pend_sq = []

                def flush_sq(pend_sq=pend_sq, tn=tn):
                    sq_, n2_ = pend_sq.pop(0)
                    P.op("pe", lambda e: e.matmul(PS[7][:, :tn], ones_bf[:], sq_, start=(n2_ == 0), stop=(n2_ == 7)),
                         reads=[("sq", n2_ % 4), "ones_bf"], writes=[("ps", 7)])

                for n2 in range(8):
                    pi = nextps()
                    wo, iw = wos[n2 // 4]
                    mm(PS[pi][:, :tn], [(wo[:, k, (n2 % 4) * 128:(n2 % 4 + 1) * 128], R2[:, k, t0:t0 + tn]) for k in range(KC)],
                       [("WB", iw)] + [("R2", k, ti) for k in range(KC)], ("ps", pi))
                    if len(pend_sq) >= 2:
                        flush_sq()
                    STT(xt[:, n2, :], PS[pi][:, :tn], hg1[:, l, v, n2:n2 + 1], xt[:, n2, :], ALU.mult, ALU.add,
                        [("ps", pi), ("xt", q, n2), ("der", l)], [("xt", q, n2)])
                    sq = sqs[n2 % 4][:, :tn]
                    ACT(sq, xt[:, n2, :], AF.Square, [("xt", q, n2)], [("sq", n2 % 4)])
                    pend_sq.append((sq, n2))
                while pend_sq:
                    flush_sq()
                P.dma("sp", xs[b][:, t0:t0 + tn].rearrange("(k p) t -> p k t", p=128), xt, reads=XK(q), writes=[("xs", b, ti)])
                norm_tile(l, v, 1, xt, tn, 7, lambda k, t0=t0, tn=tn: R1[:, k, t0:t0 + tn], lambda k, ti=ti: ("R1", k, ti),
                          lambda k, q=q: [("xt", q, k)])
            wb_unpin(i1)
            wb_unpin(i0)
            if l == 0 and b == 0:
                dump("x1", xs[b], [("xs", b, t) for t in range(len(tiles_e))])
                for k in range(KC):
                    dump("h2", R1[:, k, :], [("R1", k, t) for t in range(len(tiles_e))], idx=k)

            B()
            G0s = [af32(O_R3 + q * 4608, T) for q in range(2)]
            C0s = [af32(O_R3 + (2 + q) * 4608, T) for q in range(2)]
            fd = abf(O_R2, NJ, 1024)

            def load_fdA():
                for hh in range(2):
                    P.dma("pool", fd[:, 0:18, hh * 512:(hh + 1) * 512], ffn_down[l][0:18 * 128, hh * 512:(hh + 1) * 512].rearrange("(j p) n -> p j n", p=128),
                          writes=[("fdA", hh)])
            asts = [abf(WK + q * T, T) for q in range(2)]
            TE = tiles_e[-1][0] + tiles_e[-1][1]
            do_ada = (b == 0 and l + 1 < nlayer)
            ada_q = []
            ada_next = [0]

            def ada_step():
                if not do_ada:
                    return
                if ada_q:
                    ada_slab_mm(l + 1, *ada_q.pop(0))
                if ada_next[0] < 12:
                    ii, wbb = ada_slab_load(l + 1, ada_next[0])
                    ada_q.append((ada_next[0], ii, wbb))
                    ada_next[0] += 1

            for s in range(11):
                i = wb_take()
                wb = load_slab(ffn_upP[l][:, s * 512:(s + 1) * 512].rearrange("(k p) c -> p k c", p=128), i, (KC, 512))
                ada_step()
                if s == 5:
                    ada_step()
                if s == 2:
                    load_fdA()
                for cc in range(2):
                    j = 2 * s + cc
                    q = j % 2
                    G0, C0, ast = G0s[q], C0s[q], asts[q]
                    for ti, (t0, tn) in enumerate(tiles_e):
                        pi = nextps()
                        mm(PS[pi][:, :tn], [(wb[:, k, (2 * cc + 1) * 128:(2 * cc + 2) * 128], R1[:, k, t0:t0 + tn]) for k in range(KC)],
                           [("WB", i)] + R1K(ti), ("ps", pi))
                        ACT(G0[:, t0:t0 + tn], PS[pi][:, :tn], AF.Identity, [("ps", pi)], [("G0", q, ti)])
                    GK = [("G0", q, t) for t in range(5)]
                    CK = [("C0", q, t) for t in range(5)]
                    for (s0, sn) in segs_e:
                        TS(C0[:, s0:s0 + sn], G0[:, s0:s0 + sn], V(l, "fcw", 22 + j), V(l, "fcb", j), ALU.mult, ALU.add, GK + ["vec"], CK)
                        for k in (0, 2):
                            o = k - 1
                            a = max(0, -o)
                            e_ = sn - max(0, o)
                            STT(C0[:, s0 + a:s0 + e_], G0[:, s0 + a + o:s0 + e_ + o], V(l, "fcw", k * 22 + j), C0[:, s0 + a:s0 + e_],
                                ALU.mult, ALU.add, GK + CK + ["vec"], CK)
                    ACT(C0[:, 0:TE], C0[:, 0:TE], AF.Silu, CK, CK)
                    for ti, (t0, tn) in enumerate(tiles_e):
                        pi = nextps()
                        mm(PS[pi][:, :tn], [(wb[:, k, 2 * cc * 128:(2 * cc + 1) * 128], R1[:, k, t0:t0 + tn]) for k in range(KC)],
                           [("WB", i)] + R1K(ti), ("ps", pi))
                        TT(ast[:, t0:t0 + tn], PS[pi][:, :tn], C0[:, t0:t0 + tn], ALU.mult, [("ps", pi)] + CK, [("ast", q)])
                    for ti, (t0, tn) in enumerate(tiles_e):
                        P.dma("sp", actT_h[b][ti][:, j, 0:tn], ast[:, t0:t0 + tn], reads=[("ast", q)], writes=[("actT", b, j, ti)])

            if do_ada:
                while ada_q:
                    ada_slab_mm(l + 1, *ada_q.pop(0))
                ada_finish(l + 1)

            B()
            for hh in range(2):
                P.dma("pool", fd[:, 18:NJ, hh * 512:(hh + 1) * 512], ffn_down[l][18 * 128:NJ * 128, hh * 512:(hh + 1) * 512].rearrange("(j p) n -> p j n", p=128),
                      writes=[("fdB", hh)])
            ats = [abf(O_R1, NJ, 512), abf(O_R2 + NJ * 1024, NJ, 512)]
            xts = [af32(O_RV, KC, 512), af32(O_WK, KC, 512)]
            sqs = [abf(WK + 8192 + q * 512, 512) for q in range(2)]
            rsf = af32(WK + 9216, 512)
            pend11 = []

            def s11_load(ti_):
                t0_, tn_ = tiles_e[ti_]
                P.dma("sp", xts[ti_ % 2][:, :, :tn_], xs[b][:, t0_:t0_ + tn_].rearrange("(k p) t -> p k t", p=128),
                      reads=[("xs", b, ti_)], writes=XK(ti_ % 2))
            for ti, (t0, tn) in enumerate(tiles_e):
                v = 2 if ti == 4 else b
                q = ti % 2
                at = ats[q][:, :, :tn]
                xt = xts[q][:, :, :tn]
                P.dma("pool", at, actT_h[b][ti][:, :, 0:tn], reads=[("actT", b, j, ti) for j in range(NJ)], writes=[("at", q)])
                if ti == 0:
                    s11_load(0)
                if ti + 1 < len(tiles_e):
                    s11_load(ti + 1)
                for n2 in range(8):
                    pi = nextps()
                    mm(PS[pi][:, :tn], [(fd[:, j, n2 * 128:(n2 + 1) * 128], at[:, j, :]) for j in range(NJ)], [("fdA", n2 // 4), ("fdB", n2 // 4), ("at", q)], ("ps", pi))
                    STT(xt[:, n2, :], PS[pi][:, :tn], MOD(l, v, 5, n2), xt[:, n2, :], ALU.mult, ALU.add,
                        [("ps", pi), ("xt", q, n2), ("modt", l)], [("xt", q, n2)])
                    if last and ti < 4:
                        sq = sqs[n2 % 2][:, :tn]
                        ACT(sq, xt[:, n2, :], AF.Square, [("xt", q, n2)], [("sq", n2 % 2)])
                        pend11.append((sq, n2, tn))
                    if len(pend11) >= 2 or (pend11 and n2 == 7):
                        while pend11 and (len(pend11) >= 2 or n2 == 7):
                            sq_, n2_, tn_ = pend11.pop(0)
                            P.op("pe", lambda e, sq_=sq_, tn_=tn_, n2_=n2_: e.matmul(PS[7][:, :tn_], ones_bf[:], sq_, start=(n2_ == 0), stop=(n2_ == 7)),
                                 reads=[("sq", n2_ % 2), "ones_bf"], writes=[("ps", 7)])
                if not last or "x2" in dbg_out:
                    P.dma("sp" if last else "act", xs[b][:, t0:t0 + tn].rearrange("(k p) t -> p k t", p=128), xt, reads=XK(q), writes=[("xs", b, ti)])
                if last and ti < 4:
                    rs_ = rsf[:, :tn]
                    ACT(rs_, PS[7][:, :tn], AF.Ln, [("ps", 7)], ["rsf"], bias=EPS, scale=1.0 / D)
                    ACT(rs_, rs_, AF.Exp, ["rsf"], ["rsf"], scale=-0.5)
                    for k in range(KC):
                        STT(xt[:, k, :], xt[:, k, :], V(l, "fng", k), rs_, ALU.mult, ALU.mult, [("xt", q, k), "rsf", "vec"], [("xt", q, k)])
                    P.dma("sp", yT[b][:, t0:t0 + tn].rearrange("(k p) t -> p k t", p=128), xt, reads=XK(q), writes=[("yT", b, ti)])
            if l == 0 and b == 0:
                dump("x2", xs[b], [("xs", b, t) for t in range(len(tiles_e))])

        for l in range(nlayer):
            for b in range(nb):
                try:
                    layer_batch(l, b)
                except Exception as ex:
                    if type(ex).__name__ != "_Stop":
                        raise
        P.barrier(bscr[:, 0:1])
        P.emit()
    return nc


def _colify(a):
    a = np.asarray(a, np.float32)
    lead = a.shape[:-1]
    n = a.shape[-1] // 128
    a = a.reshape(*lead, n, 128)
    a = np.moveaxis(a, -1, 0)
    return a.reshape(128, -1)


def prep_shared(inp, nlayer=NLAYER):
    f = lambda k: np.asarray(inp[k], np.float32)
    vecs = np.zeros((128, nlayer, NV), np.float32)
    for l in range(nlayer):
        def put(name, arr):
            c = _colify(arr)
            vecs[:, l, VOFF[name]:VOFF[name] + c.shape[1]] = c
        put("n1g", f("norm1_g")[l]); put("n2g", f("norm2_g")[l]); put("adab", f("ada_b")[l]); put("qng", f("q_norm_g")[l])
        put("kvng", f("kv_norm_g")[l]); put("lcw", f("lru_conv_w")[l]); put("lcb", f("lru_conv_b")[l]); put("lba", f("lru_ba")[l])
        put("lbi", f("lru_bi")[l]); put("llam", f("lru_lambda")[l]); put("pb", f("pool_b")[l]); put("psc", f("pool_scale")[l])
        put("fcw", f("ffn_conv_w")[l]); put("fcb", f("ffn_conv_b")[l]); put("fng", f("final_norm_g"))
    half = 16
    inv = (10000.0 ** (-np.arange(0, half, 2, dtype=np.float32) / half)).astype(np.float32)
    t = np.arange(SEQ)
    ang_r = (t // 64).astype(np.float32)[:, None] * inv
    ang_c = (t % 64).astype(np.float32)[:, None] * inv
    cr, sr, cc, sc = np.cos(ang_r).T, np.sin(ang_r).T, np.cos(ang_c).T, np.sin(ang_c).T
    ropeC = np.ones((32, T), np.float32)
    ropeS = np.zeros((32, T), np.float32)
    ropeC[:, :SEQ] = np.concatenate([cr, cr, cc, cc], 0)
    ropeS[:, :SEQ] = np.concatenate([-sr, sr, -sc, sc], 0)
    ptab = np.ones((128, 4, 16), np.float32)
    for g, w in enumerate(POOL_WIN):
        left = w // 2
        right = w - 1 - left
        for tt in range(left):
            ptab[:, g, tt] = 1.0 / (tt + right + 1)
        for i in range(right):
            ptab[:, g, 8 + i] = 1.0 / (right - i + left)
    w_in = f("w_in")[:nlayer]
    perm = _win_perm()
    w_uq = f("w_uq")[:nlayer]
    w_uq_sw = w_uq.copy()
    for h in range(8):
        w_uq_sw[:, :, h * 96 + 64:h * 96 + 96] = w_uq[:, :, h * 96 + 64 + KR_SWAP]
    w_ukv = f("w_ukv")[:nlayer].reshape(nlayer, 128, 8, 128)
    ffn_up = f("ffn_up")[:nlayer]
    upcols = []
    for j in range(NJ):
        upcols += list(range(j * 128, (j + 1) * 128)) + list(range(DFF + j * 128, DFF + (j + 1) * 128))
    kr = w_in[:, :, COL_KV:COL_KR]
    sh = dict(
        vecs=vecs, ropeC=ropeC, ropeS=ropeS, ptab=ptab,
        ada_w=np.ascontiguousarray(f("ada_w")[:nlayer]),
        w_inP=np.ascontiguousarray(w_in[:, :, perm]),
        w_kr2=np.ascontiguousarray(np.concatenate([kr, kr[:, :, KR_SWAP]], -1)),
        w_uq2=np.ascontiguousarray(np.stack([w_uq, w_uq_sw], 2)),
        w_ukvP=np.ascontiguousarray(np.concatenate([w_ukv[..., :64].reshape(nlayer, 128, 512), w_ukv[..., 64:].reshape(nlayer, 128, 512)], -1)),
        lru_wa=f("lru_wa")[:nlayer], lru_wi=f("lru_wi")[:nlayer], pool_w=f("pool_w")[:nlayer],
        proj_mla=f("proj_mla")[:nlayer], proj_lru=f("proj_lru")[:nlayer], proj_pool=f("proj_pool")[:nlayer],
        w_out=f("w_out")[:nlayer], ffn_upP=np.ascontiguousarray(ffn_up[:, :, np.array(upcols)]), ffn_down=f("ffn_down")[:nlayer],
    )
    return sh


def prep_core(inp, batches):
    x = np.asarray(inp["x"], np.float32)
    ctx = np.asarray(inp["ctx"], np.float32)
    c = np.asarray(inp["c"], np.float32)
    xT = np.stack([np.concatenate([x[b].T, ctx[b].T], axis=1) for b in batches], 0)
    cv = [c[batches[0]], c[batches[-1]], np.asarray(inp["c_ctx"], np.float32)]
    cT = np.stack([_colify(v)[:, :] for v in cv], -1)
    return dict(xT=np.ascontiguousarray(xT), cT=np.ascontiguousarray(cT))


_CACHE = {}


def kernel(**inputs):
    if "nc" not in _CACHE:
        _CACHE["nc"] = build_program()
    nc = _CACHE["nc"]
    sh = prep_shared(inputs)
    in_maps = []
    for core in range(NCORES):
        m = dict(sh)
        m.update(prep_core(inputs, [2 * core, 2 * core + 1]))
        in_maps.append(m)
    res = run_bass_kernel_spmd(nc, in_maps, core_ids=list(range(NCORES)))
    out = np.empty((16, SEQ, D), np.float32)
    for core in range(NCORES):
        y = np.asarray(res.results[core]["yT"])
        out[2 * core] = y[0].T
        out[2 * core + 1] = y[1].T
    return out
```

```python
import contextlib
import numpy as np
import concourse.bass as bass
import concourse.mybir as mybir
from concourse.bass_utils import run_bass_kernel_spmd

F32 = mybir.dt.float32
BF16 = mybir.dt.bfloat16
ALU = mybir.AluOpType
AF = mybir.ActivationFunctionType

NDSEM = 24
SAME_ENG_SYNC = True


class Op:
    __slots__ = ("eng", "fn", "dma", "deps", "need_inc", "count", "sem_i", "sem_val")

    def __init__(self, eng, fn, dma):
        self.eng = eng
        self.fn = fn
        self.dma = dma
        self.deps = []
        self.need_inc = False
        self.count = 0
        self.sem_i = 0
        self.sem_val = 0


class Prog:
    ENGS = ("pe", "act", "dve", "pool", "sp")

    def __init__(self, nc):
        self.nc = nc
        self.ops = {e: [] for e in self.ENGS}
        self.writers = {}
        self.readers = {}
        self.ndma = {e: 0 for e in self.ENGS}
        self.bar = None
        self.pending_dma = []

    def op(self, eng, fn, reads=(), writes=(), dma=False, nobar=False):
        o = Op(eng, fn, dma)
        deps = {}
        for k in reads:
            for w in self.writers.get(k, ()):
                deps[id(w)] = w
        for k in writes:
            for w in self.writers.get(k, ()):
                deps[id(w)] = w
            for r in self.readers.get(k, ()):
                deps[id(r)] = r
        if eng == "pe":
            nobar = True
        if self.bar is not None and not nobar:
            deps[id(self.bar)] = self.bar
        for d in deps.values():
            if d is o:
                continue
            if (not d.dma) and d.eng == eng and (not dma):
                if eng == "pe" or not SAME_ENG_SYNC:
                    continue
            o.deps.append(d)
            d.need_inc = True
        for k in reads:
            lst = self.readers.setdefault(k, [])
            if not dma:
                lst[:] = [r for r in lst if r.dma or r.eng != eng]
            lst.append(o)
        for k in writes:
            self.writers[k] = [o]
            self.readers[k] = []
        if dma:
            i = self.ndma[eng]
            self.ndma[eng] += 1
            o.sem_i = i % NDSEM
            o.sem_val = 16 * (i // NDSEM + 1)
            if not nobar:
                self.pending_dma.append(o)
        self.ops[eng].append(o)
        return o

    def dma(self, eng, out, in_, reads=(), writes=(), nobar=False):
        return self.op(eng, lambda e: e.dma_start(out=out, in_=in_), reads, writes, dma=True, nobar=nobar)

    def barrier(self, scratch):
        o = Op("dve", lambda e: e.memset(scratch, 0.0), False)
        for e in ("pe", "act", "pool"):
            for d in reversed(self.ops[e]):
                if not d.dma:
                    o.deps.append(d)
                    d.need_inc = True
                    break
        for d in reversed(self.ops["dve"]):
            if not d.dma:
                if SAME_ENG_SYNC:
                    o.deps.append(d)
                    d.need_inc = True
                break
        if self.bar is not None:
            o.deps.append(self.bar)
        for d in self.pending_dma:
            o.deps.append(d)
        self.pending_dma = []
        o.need_inc = True
        self.ops["dve"].append(o)
        self.bar = o
        return o

    def emit(self):
        nc = self.nc
        with contextlib.ExitStack() as st:
            csem = {e: st.enter_context(nc.semaphore("c_" + e)) for e in ("pe", "act", "dve", "pool")}
            dsem = {e: [st.enter_context(nc.semaphore("d_%s%d" % (e, i))) for i in range(NDSEM)]
                    for e in self.ENGS if self.ndma[e] > 0}
            for e, lst in self.ops.items():
                c = 0
                for o in lst:
                    if o.dma:
                        continue
                    if o.need_inc:
                        c += 1
                        o.count = c
            block = st.enter_context(nc.Block())

            def run(ename, handle):
                waited = {}

                def wait(sem, val):
                    key = id(sem)
                    if waited.get(key, 0) >= val:
                        return
                    waited[key] = val
                    handle.wait_ge(sem, val)

                for o in self.ops[ename]:
                    for d in o.deps:
                        if d.dma:
                            wait(dsem[d.eng][d.sem_i], d.sem_val)
                        else:
                            wait(csem[d.eng], d.count)
                    if o.dma:
                        if o.sem_val > 16:
                            wait(dsem[ename][o.sem_i], o.sem_val - 16)
                        o.fn(handle).then_inc(dsem[ename][o.sem_i], 16)
                    else:
                        ins = o.fn(handle)
                        if o.need_inc:
                            ins.then_inc(csem[ename], 1)
                if ename in dsem:
                    n = self.ndma[ename]
                    for i in range(min(n, NDSEM)):
                        cnt = (n - 1 - i) // NDSEM + 1
                        wait(dsem[ename][i], 16 * cnt)

            if self.ops["pe"]:
                @block.tensor
                def _(eng):
                    run("pe", eng)
            if self.ops["act"]:
                @block.scalar
                def _(eng):
                    run("act", eng)
            if self.ops["dve"]:
                @block.vector
                def _(eng):
                    run("dve", eng)
            if self.ops["pool"]:
                @block.gpsimd
                def _(eng):
                    run("pool", eng)
            if self.ops["sp"]:
                @block.sync
                def _(eng):
                    run("sp", eng)


D = 1024
KC = 8
SEQ = 2048
CTXL = 256
T = SEQ + CTXL
NLAYER = 4
NCORES = 8
EPS = 1e-6
SM_SCALE = 96 ** -0.5
LN_SM = -0.5 * float(np.log(96.0))
TILES = [(0, 512), (512, 512), (1024, 512), (1536, 512), (2048, 256)]
SEGS = [(0, 2048), (2048, 256)]
DFF = 2816
NJ = 22
POOL_WIN = (2, 4, 8, 16)

VOFF = {}
_o = 0
for _n, _w in (("n1g", 8), ("n2g", 8), ("adab", 48), ("qng", 2), ("kvng", 1), ("lcw", 32), ("lcb", 8), ("lba", 16), ("lbi", 16),
               ("llam", 16), ("pb", 4), ("psc", 4), ("fcw", 66), ("fcb", 22), ("fng", 8)):
    VOFF[_n] = _o
    _o += _w
NV = _o

COL_KV, COL_KR, COL_UX, COL_Q, COL_UY, COL_POOL = 128, 160, 1184, 1440, 2464, 2976


def _win_perm():
    cols = []
    for n in range(8):
        cols += list(range(COL_KR + n * 128, COL_KR + (n + 1) * 128))
        cols += list(range(COL_Q + n * 128, COL_Q + (n + 1) * 128))
    cols += list(range(COL_UY, COL_POOL))
    cols += list(range(COL_POOL, COL_POOL + 3072))
    cols += list(range(0, 128))
    cols += list(range(COL_UX, COL_Q))
    return np.array(cols)


WIN_LRU0, WIN_POOL0, WIN_GATE0, WIN_MLA0, WIN_NCOL = 0, 2048, 2560, 5632, 6016
KR_SWAP = np.array(list(range(8, 16)) + list(range(0, 8)) + list(range(24, 32)) + list(range(16, 24)))


def build_program(nlayer=NLAYER, nb=2, dbg=None, full_last=False, stop_after=None):
    nc = bass.Bass("TRN2", target_bir_lowering=False)
    dt_in = lambda name, shape: nc.dram_tensor(name, shape, F32, kind="ExternalInput").ap()
    xT = dt_in("xT", [nb, D, T])
    cT = dt_in("cT", [128, KC, 3])
    vecs_d = dt_in("vecs", [128, nlayer, NV])
    ropeC_d = dt_in("ropeC", [32, T])
    ropeS_d = dt_in("ropeS", [32, T])
    ptab_d = dt_in("ptab", [128, 4, 16])
    ada_w = dt_in("ada_w", [nlayer, D, 6 * D])
    w_inP = dt_in("w_inP", [nlayer, D, WIN_NCOL])
    w_kr2 = dt_in("w_kr2", [nlayer, D, 64])
    w_uq2 = dt_in("w_uq2", [nlayer, 256, 2, 768])
    w_ukvP = dt_in("w_ukvP", [nlayer, 128, 1024])
    lru_wa = dt_in("lru_wa", [nlayer, 2, 8, 128, 128])
    lru_wi = dt_in("lru_wi", [nlayer, 2, 8, 128, 128])
    pool_w = dt_in("pool_w", [nlayer, 4, 128, 128])
    proj_mla = dt_in("proj_mla", [nlayer, 512, D])
    proj_lru = dt_in("proj_lru", [nlayer, D, D])
    proj_pool = dt_in("proj_pool", [nlayer, 512, D])
    w_out = dt_in("w_out", [nlayer, D, D])
    ffn_upP = dt_in("ffn_upP", [nlayer, D, 2 * DFF])
    ffn_down = dt_in("ffn_down", [nlayer, DFF, D])
    yT = nc.dram_tensor("yT", [nb, D, SEQ], F32, kind="ExternalOutput").ap()
    dbg_out = {}
    if dbg:
        for name, shape in dbg.items():
            dbg_out[name] = nc.dram_tensor("dbg_" + name, shape[1], shape[0], kind="ExternalOutput").ap()
    xs = nc.dram_tensor("xs", [nb, D, T], F32, kind="Internal").ap()
    recT_h = nc.dram_tensor("recT_h", [nb, D, T], BF16, kind="Internal").ap()
    poolT_h = nc.dram_tensor("poolT_h", [nb, 512, T], BF16, kind="Internal").ap()
    gsc_h = nc.dram_tensor("gsc_h", [nb, 8, 128, 3, T], BF16, kind="Internal").ap()
    actT_h = nc.dram_tensor("actT_h", [nb, 5, 128, NJ, 512], BF16, kind="Internal").ap()

    RSZ = KC * T
    VSZ = 18 * 8 * 65
    NWB = 5
    WBSZ = 4096
    WKSZ = 10240
    O_R1, O_R2, O_R3 = 0, RSZ, 2 * RSZ
    O_RV = 3 * RSZ
    O_WB = O_RV + VSZ
    O_WK = O_WB + NWB * WBSZ
    NA = O_WK + WKSZ

    with contextlib.ExitStack() as st:
        AR = st.enter_context(nc.sbuf_tensor("arena", [128, NA], BF16))
        vec = st.enter_context(nc.sbuf_tensor("vec", [128, nlayer, NV], F32))
        modt = st.enter_context(nc.sbuf_tensor("modt", [128, nlayer, 48, 3], F32))
        der = st.enter_context(nc.sbuf_tensor("der", [128, nlayer, 3, 2, 8], F32))
        lder = st.enter_context(nc.sbuf_tensor("lder", [128, nlayer, 5, 16], F32))
        hg1 = st.enter_context(nc.sbuf_tensor("hg1", [128, nlayer, 3, 8], F32))
        ltmp = st.enter_context(nc.sbuf_tensor("ltmp", [128, 4, 16], F32))
        ptab = st.enter_context(nc.sbuf_tensor("ptab_s", [128, 4, 16], F32))
        ones_bf = st.enter_context(nc.sbuf_tensor("ones_bf", [128, 128], BF16))
        ones_f = st.enter_context(nc.sbuf_tensor("ones_f", [128, 64], F32))
        wkr = st.enter_context(nc.sbuf_tensor("wkr", [128, KC, 2, 96], BF16))
        csb = st.enter_context(nc.sbuf_tensor("csb", [128, KC, 3], F32))
        scb = st.enter_context(nc.sbuf_tensor("scb", [128, KC, 3], BF16))
        bscr = st.enter_context(nc.sbuf_tensor("bscr", [128, 2], F32))
        PSALL = st.enter_context(nc.psum_tensor("psall", [128, 4096], F32))
        PS = [PSALL[:, i * 512:(i + 1) * 512] for i in range(8)]
        P = Prog(nc)

        def abf(off, *shape):
            n = int(np.prod(shape))
            ap = AR[:, off:off + n]
            if len(shape) == 2:
                return ap.rearrange("p (a b) -> p a b", a=shape[0])
            if len(shape) == 3:
                return ap.rearrange("p (a b c) -> p a b c", a=shape[0], b=shape[1])
            return ap

        def af32(off, *shape):
            n = int(np.prod(shape))
            ap = AR[:, off:off + 2 * n].bitcast(F32)
            if len(shape) == 2:
                return ap.rearrange("p (a b) -> p a b", a=shape[0])
            if len(shape) == 3:
                return ap.rearrange("p (a b c) -> p a b c", a=shape[0], b=shape[1])
            return ap

        R1 = abf(O_R1, KC, T)
        R2 = abf(O_R2, KC, T)
        R3 = abf(O_R3, KC, T)
        VV = abf(O_RV, 18, 8, 65)
        PLT = abf(O_RV, 4, T)

        psc = [0]

        def nextps():
            i = psc[0] % 7
            psc[0] += 1
            return i

        wb_rot = list(range(NWB))

        def wb_take():
            i = wb_rot.pop(0)
            wb_rot.append(i)
            return i

        def wb_pin():
            return wb_rot.pop(0)

        def wb_unpin(i):
            wb_rot.insert(0, i)

        def wb_ap(i, *shape):
            return abf(O_WB + i * WBSZ, *shape)

        def load_slab(dram_ap, i, shape, pslice=None):
            dst = wb_ap(i, *shape)
            if pslice is not None:
                dst = dst[pslice[0]:pslice[1]]
            P.dma("pool", dst, dram_ap, writes=[("WB", i)], nobar=True)
            return wb_ap(i, *shape)

        def mm(ps_ap, pairs, reads, pskey):
            n = len(pairs)
            for idx, (l, r) in enumerate(pairs):
                P.op("pe", lambda e, l=l, r=r, idx=idx: e.matmul(ps_ap, l, r, start=(idx == 0), stop=(idx == n - 1)),
                     reads=reads, writes=[pskey])

        def ACT(out, in_, func, reads, writes, bias=0.0, scale=1.0):
            P.op("act", lambda e: e.activation(out=out, in_=in_, func=func, bias=bias, scale=scale), reads=reads, writes=writes)

        def STT(out, in0, scalar, in1, op0, op1, reads, writes, eng="dve"):
            P.op(eng, lambda e: e.scalar_tensor_tensor(out=out, in0=in0, scalar=scalar, in1=in1, op0=op0, op1=op1), reads=reads, writes=writes)

        def TT(out, in0, in1, op, reads, writes, eng="dve"):
            P.op(eng, lambda e: e.tensor_tensor(out=out, in0=in0, in1=in1, op=op), reads=reads, writes=writes)

        def TS(out, in0, s1, s2, op0, op1, reads, writes, eng="dve"):
            P.op(eng, lambda e: e.tensor_scalar(out=out, in0=in0, scalar1=s1, scalar2=s2, op0=op0, op1=op1), reads=reads, writes=writes)

        def RECIP(out, in_, reads, writes):
            P.op("dve", lambda e: e.reciprocal(out=out, in_=in_), reads=reads, writes=writes)

        def V(l, name, i=0, n=1):
            o = VOFF[name] + i
            return vec[:, l, o:o + n]

        def dump(name, src_ap, reads, idx=None, sl=None):
            if name in dbg_out:
                dst = dbg_out[name] if idx is None else dbg_out[name][idx]
                if sl is not None:
                    dst = dst[:, sl[0]:sl[1]]
                P.dma("sp", dst, src_ap, reads=reads)

        P.dma("sp", vec[:], vecs_d, writes=["vec"])
        P.dma("sp", csb[:], cT, writes=["csb"])
        P.dma("sp", ptab[:], ptab_d, writes=["ptab"])
        P.op("dve", lambda e: e.memset(ones_bf[:], 1.0), writes=["ones_bf"])
        P.op("dve", lambda e: e.memset(ones_f[:], 1.0), writes=["ones_f"])
        P.op("dve", lambda e: e.memset(wkr[:], 0.0), writes=["wkr"])
        ACT(scb[:], csb[:], AF.Silu, ["csb"], ["scb"])
        psM = PS[7][:, 0:144].rearrange("p (j v) -> p j v", v=3)

        def ada_slab_load(l, s_):
            i = wb_take()
            wb = load_slab(ada_w[l][:, s_ * 512:(s_ + 1) * 512].rearrange("(k p) c -> p k c", p=128), i, (KC, 512))
            return i, wb

        def ada_slab_mm(l, s_, i, wb):
            for jj in range(4):
                j = s_ * 4 + jj
                mm(psM[:, j, :], [(wb[:, k, jj * 128:(jj + 1) * 128], scb[:, k, :]) for k in range(KC)], [("WB", i), "scb"], ("ps", 7))

        def ada_finish(l):
            for v in range(3):
                TT(modt[:, l, :, v], psM[:, :, v], V(l, "adab", 0, 48), ALU.add, [("ps", 7), "vec"], [("modt", l)])
            for v in range(3):
                for sub in range(2):
                    STT(der[:, l, v, sub, :], modt[:, l, (1 + 3 * sub) * 8:(2 + 3 * sub) * 8, v], 1.0, V(l, "n1g" if sub == 0 else "n2g", 0, 8),
                        ALU.add, ALU.mult, [("modt", l), "vec"], [("der", l)])
            e_ = ltmp[:, 0, :]
            w_ = ltmp[:, 1, :]
            w2 = ltmp[:, 2, :]
            pl = ltmp[:, 3, :]
            ACT(e_, V(l, "llam", 0, 16), AF.Exp, ["vec"], ["ltmp"], scale=-1.0)
            TS(w_, e_, 2.0, None, ALU.add, ALU.bypass, ["ltmp"], ["ltmp"])
            RECIP(w_, w_, ["ltmp"], ["ltmp"])
            TT(w_, w_, e_, ALU.mult, ["ltmp"], ["ltmp"])
            TT(w2, w_, w_, ALU.mult, ["ltmp"], ["ltmp"])
            TS(pl, w2, 1.0 / 9.0, 1.0 / 7.0, ALU.mult, ALU.add, ["ltmp"], ["ltmp"])
            for cf in (1.0 / 5.0, 1.0 / 3.0, 1.0):
                TT(pl, pl, w2, ALU.mult, ["ltmp"], ["ltmp"])
                TS(pl, pl, cf, None, ALU.add, ALU.bypass, ["ltmp"], ["ltmp"])
            TT(pl, pl, w_, ALU.mult, ["ltmp"], ["ltmp"])
            TS(lder[:, l, 0, :], pl, -8.0, None, ALU.mult, ALU.bypass, ["ltmp"], [("lder", l)])
            TS(lder[:, l, 1, :], pl, -16.0, None, ALU.mult, ALU.bypass, ["ltmp"], [("lder", l)])
            TT(lder[:, l, 2, 0:4], V(l, "pb", 0, 4), V(l, "psc", 0, 4), ALU.mult, ["vec"], [("lder", l)])
            for v in range(3):
                TS(hg1[:, l, v, :], modt[:, l, 16:24, v], 0.5, None, ALU.mult, ALU.bypass, [("modt", l)], [("der", l)])
            TS(lder[:, l, 3, :], V(l, "lba", 0, 16), 0.5, None, ALU.mult, ALU.bypass, ["vec"], [("lder", l)])
            TS(lder[:, l, 4, :], V(l, "lbi", 0, 16), 0.5, None, ALU.mult, ALU.bypass, ["vec"], [("lder", l)])

        for s_ in range(12):
            i_, wb_ = ada_slab_load(0, s_)
            ada_slab_mm(0, s_, i_, wb_)
        ada_finish(0)
        dump("mods", modt[:, 0, :, :], [("modt", 0)])

        def G(l, v, sub, k):
            return der[:, l, v, sub, k:k + 1]

        def MOD(l, v, m, k):
            return modt[:, l, m * 8 + k, v:v + 1]

        WK = O_WK
        norm_ctr = [0]

        def norm_tile(l, v, sub, xt, tn, psq_i, dst, dkeys, rkeys):
            par = norm_ctr[0] % 2
            norm_ctr[0] += 1
            rs = af32(WK + par * 5120, 512)[:, :tn]
            ACT(rs, PS[psq_i][:, :tn], AF.Ln, [("ps", psq_i)], [("rs", par)], bias=EPS, scale=1.0 / D)
            ACT(rs, rs, AF.Exp, [("rs", par)], [("rs", par)], scale=-0.5)
            for k in range(KC):
                tmp = af32(WK + 1024 + (k % 2) * 1024, 512)[:, :tn]
                STT(tmp, xt[:, k, :], G(l, v, sub, k), rs, ALU.mult, ALU.mult, rkeys(k) + [("rs", par), ("der", l)], [("ntmp", k % 2)])
                ACT(dst(k), tmp, AF.Identity, [("ntmp", k % 2), ("modt", l)], [dkeys(k)], bias=MOD(l, v, 3 * sub, k))

        def layer_batch(l, b):
            last = (l == nlayer - 1)
            tiles = TILES
            skipc = last and not full_last
            tiles_e = TILES[:4] if skipc else TILES
            segs_e = SEGS[:1] if skipc else SEGS
            xsrc = xT if l == 0 else xs
            phase = [0]

            class _Stop(Exception):
                pass

            def B():
                phase[0] += 1
                if stop_after is not None and phase[0] > stop_after:
                    raise _Stop()
                P.barrier(bscr[:, 0:1])

            B()
            for ti, (t0, tn) in enumerate(tiles):
                v = 2 if ti == 4 else b
                xt = af32(O_R2 + (ti % 2) * 8192, KC, 512)[:, :, :tn]
                P.dma("sp", xt, xsrc[b][:, t0:t0 + tn].rearrange("(k p) t -> p k t", p=128), reads=([("xs", b, ti)] if l > 0 else []),
                      writes=[("xt", ti % 2)])
                sq = abf(O_R3 + (ti % 2) * 4096, KC, 512)[:, :, :tn]
                ACT(sq, xt, AF.Square, [("xt", ti % 2)], [("sq", ti % 2)])
                pi = nextps()
                mm(PS[pi][:, :tn], [(ones_bf[:], sq[:, k, :]) for k in range(KC)], [("sq", ti % 2), "ones_bf"], ("ps", pi))
                norm_tile(l, v, 0, xt, tn, pi, lambda k, t0=t0, tn=tn: R1[:, k, t0:t0 + tn], lambda k, ti=ti: ("R1", k, ti), lambda k, ti=ti: [("xt", ti % 2)])
            if l == 0 and b == 0:
                for k in range(KC):
                    dump("hx", R1[:, k, :], [("R1", k, ti) for ti in range(5)], idx=k)

            R1K = lambda ti: [("R1", k, ti) for k in range(KC)]
            BK = lambda i: [("B", i, t) for t in range(5)]

            def scan(out, a, bx, init, reads, writes):
                P.op("dve", lambda e: e.tensor_tensor_scan(out=out, data0=a, data1=bx, initial=init, op0=ALU.mult, op1=ALU.add),
                     reads=reads, writes=writes)

            gate_state = {"slab": None, "q": 0}

            def gate_block():
                q = gate_state["q"]
                gate_state["q"] += 1
                s_, jj = q // 4, q % 4
                if jj == 0:
                    gi = wb_take()
                    gwb = load_slab(w_inP[l][:, WIN_GATE0 + s_ * 512:WIN_GATE0 + (s_ + 1) * 512].rearrange("(k p) c -> p k c", p=128), gi, (KC, 512))
                    gate_state["slab"] = (gi, gwb)
                gi, gwb = gate_state["slab"]
                j, n = q // 8, q % 8
                gst = abf(O_RV + (q % 2) * T, T)
                for ti, (t0, tn) in enumerate(tiles):
                    pi = nextps()
                    mm(PS[pi][:, :tn], [(gwb[:, k, jj * 128:(jj + 1) * 128], R1[:, k, t0:t0 + tn]) for k in range(KC)], [("WB", gi)] + R1K(ti), ("ps", pi))
                    ACT(gst[:, t0:t0 + tn], PS[pi][:, :tn], AF.Tanh, [("ps", pi)], [("gst", q % 2)], scale=0.5)
                P.dma("sp", gsc_h[b][n, :, j, :], gst, reads=[("gst", q % 2)], writes=[("gsc", b, n, j)])

            B()
            i_wa = wb_pin()
            i_wi = wb_pin()
            lwa = load_slab(lru_wa[l].rearrange("d n c e -> c d n e"), i_wa, (2, 8, 128))
            lwi = load_slab(lru_wi[l].rearrange("d n c e -> c d n e"), i_wi, (2, 8, 128))
            Bf = lambda i: af32(O_R2 + i * 4608, T)
            ub = abf(WK, T)
            rst = abf(WK + T, T)
            gls = [abf(WK + 2 * T, T), abf(WK + 3 * T, T)]
            XB = af32(O_RV + 2 * T, T)
            XK_ = [("X", t) for t in range(5)]
            lru_slab = [None]

            def lru_front(n):
                s_, cc = n // 2, n % 2
                if cc == 0:
                    i_ = wb_take()
                    lru_slab[0] = (i_, load_slab(w_inP[l][:, WIN_LRU0 + s_ * 512:WIN_LRU0 + (s_ + 1) * 512].rearrange("(k p) c -> p k c", p=128), i_, (KC, 512)))
                i, wb = lru_slab[0]
                gl = gls[n % 2]
                for ti, (t0, tn) in enumerate(tiles):
                    pi = nextps()
                    mm(PS[pi][:, :tn], [(wb[:, k, 2 * cc * 128:(2 * cc + 1) * 128], R1[:, k, t0:t0 + tn]) for k in range(KC)],
                       [("WB", i)] + R1K(ti), ("ps", pi))
                    P.op("dve", lambda e, pi=pi, t0=t0, tn=tn: e.tensor_copy(out=XB[:, t0:t0 + tn], in_=PS[pi][:, :tn]), reads=[("ps", pi)], writes=[("X", ti)])
                for ti, (t0, tn) in enumerate(tiles):
                    pi = nextps()
                    mm(PS[pi][:, :tn], [(wb[:, k, (2 * cc + 1) * 128:(2 * cc + 2) * 128], R1[:, k, t0:t0 + tn]) for k in range(KC)],
                       [("WB", i)] + R1K(ti), ("ps", pi))
                    ACT(gl[:, t0:t0 + tn], PS[pi][:, :tn], AF.Gelu_apprx_tanh, [("ps", pi)], [("gl", n % 2, ti)])

            def lru_conv(n):
                for (s0, sn) in SEGS:
                    TS(Bf(1)[:, s0:s0 + sn], XB[:, s0:s0 + sn], V(l, "lcw", 2 * 8 + n), V(l, "lcb", n), ALU.mult, ALU.add,
                       XK_ + ["vec"], BK(1))
                    for k in (0, 1, 3):
                        o = k - 2
                        a = max(0, -o)
                        e_ = sn - max(0, o)
                        STT(Bf(1)[:, s0 + a:s0 + e_], XB[:, s0 + a + o:s0 + e_ + o], V(l, "lcw", k * 8 + n), Bf(1)[:, s0 + a:s0 + e_],
                            ALU.mult, ALU.add, XK_ + BK(1) + ["vec"], BK(1))
                if l == 0 and b == 0 and n == 0:
                    dump("u0", Bf(1), BK(1))
                P.op("dve", lambda e: e.tensor_copy(out=ub, in_=Bf(1)), reads=BK(1), writes=["ub"])

            def lru_gates(n):
                for d in range(2):
                    br, bi_ = 2 + 3 * d, 4 + 3 * d
                    for ti, (t0, tn) in enumerate(tiles):
                        pr = nextps()
                        mm(PS[pr][:, :tn], [(lwa[:, d, n, :], ub[:, t0:t0 + tn])], [("WB", i_wa), "ub"], ("ps", pr))
                        ACT(Bf(br)[:, t0:t0 + tn], PS[pr][:, :tn], AF.Tanh, [("ps", pr), ("lder", l)], [("B", br, ti)],
                            bias=lder[:, l, 3, d * 8 + n:d * 8 + n + 1], scale=0.5)
                        pq = nextps()
                        mm(PS[pq][:, :tn], [(lwi[:, d, n, :], ub[:, t0:t0 + tn])], [("WB", i_wi), "ub"], ("ps", pq))
                        ACT(Bf(bi_)[:, t0:t0 + tn], PS[pq][:, :tn], AF.Tanh, [("ps", pq), ("lder", l)], [("B", bi_, ti)],
                            bias=lder[:, l, 4, d * 8 + n:d * 8 + n + 1], scale=0.5)
                for d in range(2):
                    br, b2 = 2 + 3 * d, 3 + 3 * d
                    la_h = lder[:, l, 0, d * 8 + n:d * 8 + n + 1]
                    la_f = lder[:, l, 1, d * 8 + n:d * 8 + n + 1]
                    ACT(Bf(b2), Bf(br), AF.Exp, BK(br) + [("lder", l)], BK(b2), scale=la_f, bias=la_f)
                    ACT(Bf(br), Bf(br), AF.Exp, BK(br) + [("lder", l)], BK(br), scale=la_h, bias=la_h)
                for d in range(2):
                    b2 = 3 + 3 * d
                    ACT(Bf(b2), Bf(b2), AF.Sqrt, BK(b2), BK(b2), scale=-0.25, bias=0.25 + 2.5e-7)

            def lru_bx(m):
                for d in range(2):
                    b2, bi_ = 3 + 3 * d, 4 + 3 * d
                    STT(Bf(bi_), Bf(bi_), 1.0, Bf(b2), ALU.add, ALU.mult, BK(bi_) + BK(b2), BK(bi_))
                    TT(Bf(bi_), Bf(bi_), Bf(1), ALU.mult, BK(bi_) + BK(1), BK(bi_))

            def lru_scan(m):
                for d in range(2):
                    br, bi_ = 2 + 3 * d, 4 + 3 * d
                    hb = 0 if d == 0 else 3
                    H = Bf(hb)
                    rk = BK(br) + BK(bi_)
                    if d == 0:
                        scan(H[:, SEQ:T], Bf(br)[:, SEQ:T], Bf(bi_)[:, SEQ:T], 0.0, rk, BK(hb))
                        scan(H[:, 0:SEQ], Bf(br)[:, 0:SEQ], Bf(bi_)[:, 0:SEQ], H[:, T - 1:T], rk + BK(hb), BK(hb))
                    else:
                        scan(H[:, SEQ:T][:, ::-1], Bf(br)[:, SEQ:T][:, ::-1], Bf(bi_)[:, SEQ:T][:, ::-1], 0.0, rk, BK(hb))
                        scan(H[:, 0:SEQ][:, ::-1], Bf(br)[:, 0:SEQ][:, ::-1], Bf(bi_)[:, 0:SEQ][:, ::-1], H[:, SEQ:SEQ + 1], rk + BK(hb), BK(hb))

            def lru_out(m):
                TT(Bf(0), Bf(0), Bf(3), ALU.add, BK(0) + BK(3), BK(0))
                TT(rst, Bf(0), gls[m % 2], ALU.mult, BK(0) + [("gl", m % 2, t) for t in range(5)], ["rst"])
                P.dma("sp", recT_h[b][m * 128:(m + 1) * 128, :], rst, reads=["rst"], writes=[("recT", b, m)])

            for it_ in range(9):
                if it_ < 8:
                    lru_front(it_)
                    for _ in range(2):
                        gate_block()
                if it_ >= 1:
                    lru_bx(it_ - 1)
                if it_ < 8:
                    lru_conv(it_)
                else:
                    for _ in range(8):
                        gate_block()
                if it_ >= 1:
                    lru_scan(it_ - 1)
                    lru_out(it_ - 1)
                if it_ < 8:
                    lru_gates(it_)
            wb_unpin(i_wi)
            wb_unpin(i_wa)
            if l == 0 and b == 0:
                dump("rec", recT_h[b], [("recT", b, n) for n in range(8)])

            B()
            i_pw = wb_pin()
            pw = load_slab(pool_w[l].rearrange("g c d -> c g d"), i_pw, (4, 128))
            i = wb_take()
            wb = load_slab(w_inP[l][:, WIN_POOL0:WIN_POOL0 + 512].rearrange("(k p) c -> p k c", p=128), i, (KC, 512))
            PW, LB, CB = 2368, 16, 2096
            Pa = [af32(O_R2, PW), af32(O_R2 + 2 * PW, PW)]
            Pb = af32(O_R2 + 4 * PW, PW)
            Pc = af32(O_R2 + 6 * PW, PW)
            tmpe = af32(WK + 3 * T, 16)
            for q in range(2):
                P.op("dve", lambda e, q=q: e.memset(Pa[q], 0.0), writes=[("Pa", q)])
            dbs = [abf(WK, T), abf(WK + 3 * T + 64, T)]

            def pool_front(g):
                U = Pa[g % 2]
                for ti, (t0, tn) in enumerate(tiles):
                    pi = nextps()
                    mm(PS[pi][:, :tn], [(wb[:, k, g * 128:(g + 1) * 128], R1[:, k, t0:t0 + tn]) for k in range(KC)], [("WB", i)] + R1K(ti), ("ps", pi))
                    base = (LB + t0) if ti < 4 else CB
                    ACT(U[:, base:base + tn], PS[pi][:, :tn], AF.Identity, [("ps", pi)], [("Pa", g % 2)])

            def pool_mid(g):
                w = POOL_WIN[g]
                left = w // 2
                right = w - 1 - left
                U = Pa[g % 2]
                db = dbs[g % 2]
                dbk = ("db", g % 2)
                src, skey = U, ("Pa", g % 2)
                m = 1
                lvl = 0
                while m < w:
                    dst, dkey = (Pb, "Pb") if lvl % 2 == 0 else (Pc, "Pc")
                    TT(dst[:, 0:PW - m], src[:, 0:PW - m], src[:, m:PW], ALU.add, [skey], [dkey])
                    src, skey = dst, dkey
                    m *= 2
                    lvl += 1
                Aw, akey = src, skey
                for (base, s0, sn) in ((LB, 0, SEQ), (CB, SEQ, CTXL)):
                    STT(db[:, s0:s0 + sn], Aw[:, base - left:base - left + sn], 1.0 / w, U[:, base:base + sn], ALU.mult, ALU.subtract,
                        [akey, ("Pa", g % 2)], [dbk])
                    TT(tmpe[:, 0:left], Aw[:, base - left:base], ptab[:, g, 0:left], ALU.mult, [akey, "ptab"], ["tmpe"])
                    TT(db[:, s0:s0 + left], tmpe[:, 0:left], U[:, base:base + left], ALU.subtract, ["tmpe", ("Pa", g % 2)], [dbk])
                    if right > 0:
                        TT(tmpe[:, 8:8 + right], Aw[:, base - left + sn - right:base - left + sn], ptab[:, g, 8:8 + right], ALU.mult,
                           [akey, "ptab"], ["tmpe"])
                        TT(db[:, s0 + sn - right:s0 + sn], tmpe[:, 8:8 + right], U[:, base + sn - right:base + sn], ALU.subtract,
                           ["tmpe", ("Pa", g % 2)], [dbk])

            def pool_back(g):
                db = dbs[g % 2]
                pst = abf(WK + T + (g % 2) * T, T)
                for ti, (t0, tn) in enumerate(tiles):
                    pi = nextps()
                    mm(PS[pi][:, :tn], [(pw[:, g, :], db[:, t0:t0 + tn])], [("WB", i_pw), ("db", g % 2)], ("ps", pi))
                    ACT(pst[:, t0:t0 + tn], PS[pi][:, :tn], AF.Identity, [("ps", pi), "vec", ("lder", l)], [("pst", g % 2)],
                        scale=V(l, "psc", g), bias=lder[:, l, 2, g:g + 1])
                P.dma("sp", poolT_h[b][g * 128:(g + 1) * 128, :], pst, reads=[("pst", g % 2)], writes=[("poolT", b, g)])

            pool_front(0)
            for g in range(4):
                if g + 1 < 4:
                    pool_front(g + 1)
                pool_mid(g)
                pool_back(g)
            wb_unpin(i_pw)
            if l == 0 and b == 0:
                dump("pool", poolT_h[b], [("poolT", b, g) for g in range(4)])

            def COPY(out, in_, reads, writes, which):
                if which % 2 == 0:
                    ACT(out, in_, AF.Identity, reads, writes)
                else:
                    P.op("dve", lambda e: e.tensor_copy(out=out, in_=in_), reads=reads, writes=writes)

            B()
            i_uq = wb_pin()
            wuq = load_slab(w_uq2[l].rearrange("(k p) v c -> p k v c", p=128), i_uq, (2, 2, 768))
            i_kv = wb_pin()
            wukv = load_slab(w_ukvP[l], i_kv, (1024,))
            i = wb_take()
            wb = load_slab(w_inP[l][:, WIN_MLA0:WIN_MLA0 + 384].rearrange("(k p) c -> p k c", p=128), i, (KC, 384))
            P.dma("pool", wkr[:, :, 0, 64:96], w_kr2[l][:, 0:32].rearrange("(k p) c -> p k c", p=128), writes=["wkr"], nobar=True)
            P.dma("pool", wkr[:, :, 1, 64:96], w_kr2[l][:, 32:64].rearrange("(k p) c -> p k c", p=128), writes=["wkr"], nobar=True)
            RVK = [("RV", kt) for kt in range(18)]
            RVP = [("RVp", t) for t in range(5)]
            P.op("dve", lambda e: e.memset(VV[:, :, :, 64:65], 1.0), writes=RVK + RVP)
            ckv = af32(WK, 512)
            rs = af32(WK + 1024, 512)
            sqk = abf(WK + 2048, 512)
            ckvn = abf(WK + 2560, 512)
            cq = af32(WK + 3072, 2, 512)
            sqq = abf(WK + 5120, 2, 512)
            cqn = abf(WK + 6144, 2, 512)
            Ct = af32(WK + 7168, 512)
            St = af32(WK + 8192, 512)
            t1 = af32(WK + 9216, 512)
            rs2 = rs
            t2 = cq[:, 0, :]
            tq1 = ckv
            tq2 = cq[:, 1, :]
            cw = 0
            for ti, (t0, tn) in enumerate(tiles):
                pi = nextps()
                mm(PS[pi][:, :tn], [(wb[:, k, 0:128], R1[:, k, t0:t0 + tn]) for k in range(KC)], [("WB", i)] + R1K(ti), ("ps", pi))
                ACT(ckv[:, :tn], PS[pi][:, :tn], AF.Identity, [("ps", pi)], ["ckv"])
                ACT(sqk[:, :tn], PS[pi][:, :tn], AF.Square, [("ps", pi)], ["sqk"])
                for c2 in range(2):
                    p5 = nextps()
                    mm(PS[p5][:, :tn], [(wb[:, k, 128 + c2 * 128:256 + c2 * 128], R1[:, k, t0:t0 + tn]) for k in range(KC)],
                       [("WB", i)] + R1K(ti), ("ps", p5))
                    ACT(cq[:, c2, :tn], PS[p5][:, :tn], AF.Identity, [("ps", p5)], [("cq", c2)])
                    ACT(sqq[:, c2, :tn], PS[p5][:, :tn], AF.Square, [("ps", p5)], ["sqq"])
                pa = nextps()
                mm(PS[pa][0:96, :tn], [(wkr[:, k, 0, :], R1[:, k, t0:t0 + tn]) for k in range(KC)], ["wkr"] + R1K(ti), ("ps", pa))
                pb_ = nextps()
                mm(PS[pb_][0:96, :tn], [(wkr[:, k, 1, :], R1[:, k, t0:t0 + tn]) for k in range(KC)], ["wkr"] + R1K(ti), ("ps", pb_))
                P.dma("sp", Ct[64:96, :tn], ropeC_d[:, t0:t0 + tn], writes=["Ct"])
                P.dma("sp", St[64:96, :tn], ropeS_d[:, t0:t0 + tn], writes=["St"])
                p2 = nextps()
                mm(PS[p2][:, :tn], [(ones_bf[:], sqk[:, :tn])], ["sqk", "ones_bf"], ("ps", p2))
                ACT(rs[:, :tn], PS[p2][:, :tn], AF.Ln, [("ps", p2)], ["rs"], bias=EPS, scale=1.0 / 128)
                ACT(rs[:, :tn], rs[:, :tn], AF.Exp, ["rs"], ["rs"], scale=-0.5)
                STT(ckvn[:, :tn], ckv[:, :tn], V(l, "kvng", 0), rs[:, :tn], ALU.mult, ALU.mult, ["ckv", "rs", "vec"], ["ckvn"])
                p6 = nextps()
                mm(PS[p6][:, :tn], [(ones_bf[:], sqq[:, 0, :tn]), (ones_bf[:], sqq[:, 1, :tn])], ["sqq", "ones_bf"], ("ps", p6))
                ACT(rs2[:, :tn], PS[p6][:, :tn], AF.Ln, [("ps", p6)], ["rs"], bias=EPS, scale=1.0 / 256)
                ACT(rs2[:, :tn], rs2[:, :tn], AF.Exp, ["rs"], ["rs"], scale=-0.5, bias=LN_SM)
                for c2 in range(2):
                    STT(cqn[:, c2, :tn], cq[:, c2, :tn], V(l, "qng", c2), rs2[:, :tn], ALU.mult, ALU.mult, [("cq", c2), "rs", "vec"], ["cqn"])
                TT(t1[64:96, :tn], PS[pa][64:96, :tn], Ct[64:96, :tn], ALU.mult, [("ps", pa), "Ct"], ["t1"])
                TT(t2[64:96, :tn], PS[pb_][64:96, :tn], St[64:96, :tn], ALU.mult, [("ps", pb_), "St", "cqn"], [("cq", 0)])
                TT(t1[64:96, :tn], t1[64:96, :tn], t2[64:96, :tn], ALU.add, ["t1", ("cq", 0)], ["t1"])
                if l == 0 and b == 0:
                    dump("kr", t1[64:96, :tn], ["t1"], sl=(t0, t0 + tn))
                for h in range(8):
                    COPY(R2[64:96, h, t0:t0 + tn], t1[64:96, :tn], ["t1"], [("R2", h, ti)], cw)
                    cw += 1
                for h in range(8):
                    p3 = nextps()
                    mm(PS[p3][0:64, :tn], [(wukv[:, h * 64:(h + 1) * 64], ckvn[:, :tn])], [("WB", i_kv), "ckvn"], ("ps", p3))
                    COPY(R2[0:64, h, t0:t0 + tn], PS[p3][0:64, :tn], [("ps", p3)], [("R2", h, ti)], cw)
                    cw += 1
                for sub in range(tn // 128):
                    kt = t0 // 128 + sub
                    p4 = nextps()
                    mm(PS[p4][:, 0:512], [(ckvn[:, sub * 128:(sub + 1) * 128], wukv[:, 512:1024])], [("WB", i_kv), "ckvn"], ("ps", p4))
                    COPY(VV[:, kt, :, 0:64], PS[p4][:, 0:512].rearrange("p (h d) -> p h d", h=8), [("ps", p4)], [("RV", kt)] + RVP, cw)
                    cw += 1
                for h in range(8):
                    pA = nextps()
                    mm(PS[pA][0:96, :tn], [(wuq[:, k, 0, h * 96:(h + 1) * 96], cqn[:, k, :tn]) for k in range(2)], [("WB", i_uq), "cqn"], ("ps", pA))
                    pB = nextps()
                    mm(PS[pB][0:96, :tn], [(wuq[:, k, 1, h * 96:(h + 1) * 96], cqn[:, k, :tn]) for k in range(2)], [("WB", i_uq), "cqn"], ("ps", pB))
                    ACT(R3[0:64, h, t0:t0 + tn], PS[pA][0:64, :tn], AF.Identity, [("ps", pA)], [("R3", h, ti)])
                    TT(tq1[64:96, :tn], PS[pA][64:96, :tn], Ct[64:96, :tn], ALU.mult, [("ps", pA), "Ct"], ["ckv"])
                    TT(tq2[64:96, :tn], PS[pB][64:96, :tn], St[64:96, :tn], ALU.mult, [("ps", pB), "St"], [("cq", 1)])
                    TT(R3[64:96, h, t0:t0 + tn], tq1[64:96, :tn], tq2[64:96, :tn], ALU.add, ["ckv", ("cq", 1)], [("R3", h, ti)])
            wb_unpin(i_kv)
            wb_unpin(i_uq)

            if l == 0 and b == 0:
                for h in range(8):
                    dump("K", R2[0:96, h, :], [("R2", h, t) for t in range(5)], idx=h)
                    dump("Q", R3[0:96, h, :], [("R3", h, t) for t in range(5)], idx=h)
                dump("Vv", AR[:, O_RV:O_RV + VSZ], RVK)
            B()
            PTs = [abf(WK + q * 1024, 1024) for q in range(3)]
            rc = af32(WK + 3072, 1024)
            osb = af32(WK + 5120, 1024)
            sctr = 0
            groups = [[0, 1], [2, 3]] + ([] if skipc else [[4]])
            for h in range(8):
                for qis in groups:
                    kts = list(range(18)) if qis[0] < 4 else [16, 17]
                    nk = len(kts)
                    W = sum(tiles[qi][1] for qi in qis)
                    pend = []
                    for step in range(nk + 1):
                        if step < nk:
                            kt = kts[step]
                            sb = sctr % 3
                            sctr += 1
                            for jq, qi in enumerate(qis):
                                q0, qn = tiles[qi]
                                P.op("pe", lambda e, sb=sb, jq=jq, qn=qn, q0=q0, kt=kt, h=h: e.matmul(
                                    PSALL[:, sb * 1024 + jq * 512:sb * 1024 + jq * 512 + qn], R2[0:96, h, kt * 128:(kt + 1) * 128],
                                    R3[0:96, h, q0:q0 + qn], start=True, stop=True),
                                    reads=[("R2", h, kt // 4), ("R3", h, qi)], writes=[("ps", 2 * sb), ("ps", 2 * sb + 1)])
                            ACT(PTs[sb][:, :W], PSALL[:, sb * 1024:sb * 1024 + W], AF.Exp, [("ps", 2 * sb), ("ps", 2 * sb + 1)], [("PT", sb)])
                            pend.append((kt, sb))
                        if step > 0:
                            kt, sb = pend.pop(0)
                            for jq, qi in enumerate(qis):
                                q0, qn = tiles[qi]
                                P.op("pe", lambda e, kt=kt, sb=sb, step=step, jq=jq, h=h, qn=qn, nk=nk: e.matmul(
                                    PS[6 + jq][0:65, :qn], VV[:, kt, h, 0:65], PTs[sb][:, jq * 512:jq * 512 + qn], start=(step == 1), stop=(step == nk)),
                                    reads=[("RV", kt), ("PT", sb)], writes=[("ps", 6 + jq)])
                    for jq, qi in enumerate(qis):
                        q0, qn = tiles[qi]
                        cs = slice(jq * 512, jq * 512 + qn)
                        ACT(rc[64:65, cs], PS[6 + jq][64:65, :qn], AF.Ln, [("ps", 6 + jq)], [("rc", jq)])
                        ACT(rc[64:65, cs], rc[64:65, cs], AF.Exp, [("rc", jq)], [("rc", jq)], scale=-1.0)
                        P.op("dve", lambda e, jq=jq, qn=qn, cs=cs: e.tensor_copy(out=osb[0:64, cs], in_=PS[6 + jq][0:64, :qn]),
                             reads=[("ps", 6 + jq)], writes=[("osb", jq)])
                        sb = sctr % 3
                        sctr += 1
                        P.op("pe", lambda e, sb=sb, qn=qn, cs=cs: e.matmul(PSALL[0:64, sb * 1024:sb * 1024 + qn], ones_f[64:65, 0:64], rc[64:65, cs], start=True, stop=True),
                             reads=[("rc", jq), "ones_f"], writes=[("ps", 2 * sb), ("ps", 2 * sb + 1)])
                        TT(R1[0:64, h, q0:q0 + qn], osb[0:64, cs], PSALL[0:64, sb * 1024:sb * 1024 + qn], ALU.mult, [("osb", jq), ("ps", 2 * sb), ("ps", 2 * sb + 1)], [("R1", h, qi)])
            if l == 0 and b == 0:
                for h in range(8):
                    dump("att", R1[0:64, h, :], [("R1", h, t) for t in range(5)], idx=h)

            B()
            for ti, (t0, tn) in enumerate(tiles_e):
                P.dma("sp", R3[:, :, t0:t0 + tn], recT_h[b][:, t0:t0 + tn].rearrange("(k p) t -> p k t", p=128),
                      reads=[("recT", b, n) for n in range(8)], writes=[("R3", k, ti) for k in range(KC)])
                P.dma("sp", PLT[:, :, t0:t0 + tn], poolT_h[b][:, t0:t0 + tn].rearrange("(g p) t -> p g t", p=128),
                      reads=[("poolT", b, g) for g in range(4)], writes=[("RVp", ti)] + ([("RV", kt) for kt in range(18)] if ti == 0 else []))

            gts = [abf(WK + q * 1536, 3, 512) for q in range(3)]
            tAs = [af32(WK + 4608 + q * 2048, 512) for q in range(2)]
            tBs = [af32(WK + 5632 + q * 2048, 512) for q in range(2)]
            gcnt = 0
            for grp in range(2):
                ia = wb_take()
                wa = load_slab(proj_mla[l][:, grp * 512:(grp + 1) * 512].rearrange("(h d) n -> d h n", d=64), ia, (8, 512), pslice=(0, 64))
                ir = wb_take()
                wr = load_slab(proj_lru[l][:, grp * 512:(grp + 1) * 512].rearrange("(k p) n -> p k n", p=128), ir, (8, 512))
                ip = wb_take()
                wp = load_slab(proj_pool[l][:, grp * 512:(grp + 1) * 512].rearrange("(g p) n -> p g n", p=128), ip, (4, 512))
                for nn in range(4):
                    n = grp * 4 + nn
                    for ti, (t0, tn) in enumerate(tiles_e):
                        gq = gcnt % 3
                        gt = gts[gq]
                        P.dma("sp", gt[:, :, :tn], gsc_h[b][n, :, :, t0:t0 + tn], reads=[("gsc", b, n, j) for j in range(3)], writes=[("gt", gq)])
                        pA = nextps()
                        mm(PS[pA][:, :tn], [(wa[0:64, h, nn * 128:(nn + 1) * 128], R1[0:64, h, t0:t0 + tn]) for h in range(8)],
                           [("WB", ia)] + R1K(ti), ("ps", pA))
                        pR = nextps()
                        mm(PS[pR][:, :tn], [(wr[:, k, nn * 128:(nn + 1) * 128], R3[:, k, t0:t0 + tn]) for k in range(KC)],
                           [("WB", ir)] + [("R3", k, ti) for k in range(KC)], ("ps", pR))
                        pP = nextps()
                        mm(PS[pP][:, :tn], [(wp[:, g, nn * 128:(nn + 1) * 128], PLT[:, g, t0:t0 + tn]) for g in range(4)],
                           [("WB", ip), ("RVp", ti)], ("ps", pP))
                        par = gcnt % 2
                        tA = tAs[par][:, :tn]
                        tB = tBs[par][:, :tn]
                        STT(tA, gt[:, 0, :tn], 1.0, PS[pA][:, :tn], ALU.add, ALU.mult, [("ps", pA), ("gt", gq)], [("tA", par)])
                        STT(tB, gt[:, 1, :tn], 1.0, PS[pR][:, :tn], ALU.add, ALU.mult, [("ps", pR), ("gt", gq)], [("tB", par)])
                        TT(tA, tA, tB, ALU.add, [("tA", par), ("tB", par)], [("tA", par)])
                        STT(tB, gt[:, 2, :tn], 1.0, PS[pP][:, :tn], ALU.add, ALU.mult, [("ps", pP), ("gt", gq)], [("tB", par)])
                        TT(R2[:, n, t0:t0 + tn], tA, tB, ALU.add, [("tA", par), ("tB", par)], [("R2", n, ti)])
                        gcnt += 1

            B()
            i0 = wb_pin()
            wo0 = load_slab(w_out[l][:, 0:512].rearrange("(k p) n -> p k n", p=128), i0, (KC, 512))
            i1 = wb_pin()
            wo1 = load_slab(w_out[l][:, 512:1024].rearrange("(k p) n -> p k n", p=128), i1, (KC, 512))
            wos = [(wo0, i0), (wo1, i1)]
            xts = [af32(O_R3 + q * 8192, KC, 512) for q in range(2)]
            sqs = [abf(WK + 3072 + q * 512, 512) for q in range(4)]
            XK = lambda q: [("xt", q, k) for k in range(KC)]
            def s9_load(ti_):
                t0_, tn_ = tiles_e[ti_]
                P.dma("sp", xts[ti_ % 2][:, :, :tn_], xsrc[b][:, t0_:t0_ + tn_].rearrange("(k p) t -> p k t", p=128),
                      reads=([("xs", b, ti_)] if l > 0 else []), writes=XK(ti_ % 2))

            s9_load(0)
            for ti, (t0, tn) in enumerate(tiles_e):
                v = 2 if ti == 4 else b
                q = ti % 2
                xt = xts[q][:, :, :tn]
                if ti + 1 < len(tiles_e):
                    s9_load(ti + 1)
                pend_sq = []

                def flush_sq(pend_sq=pend_sq, tn=tn):
                    sq_, n2_ = pend_sq.pop(0)
                    P.op("pe", lambda e: e.matmul(PS[7][:, :tn], ones_bf[:], sq_, start=(n2_ == 0), stop=(n2_ == 7)),
                         reads=[("sq", n2_ % 4), "ones_bf"], writes=[("ps", 7)])

                for n2 in range(8):
                    pi = nextps()
                    wo, iw = wos[n2 // 4]
                    mm(PS[pi][:, :tn], [(wo[:, k, (n2 % 4) * 128:(n2 % 4 + 1) * 128], R2[:, k, t0:t0 + tn]) for k in range(KC)],
                       [("WB", iw)] + [("R2", k, ti) for k in range(KC)], ("ps", pi))
                    if len(pend_sq) >= 2:
                        flush_sq()
                    STT(xt[:, n2, :], PS[pi][:, :tn], hg1[:, l, v, n2:n2 + 1], xt[:, n2, :], ALU.mult, ALU.add,
                        [("ps", pi), ("xt", q, n2), ("der", l)], [("xt", q, n2)])
                    sq = sqs[n2 % 4][:, :tn]
                    ACT(sq, xt[:, n2, :], AF.Square, [("xt", q, n2)], [("sq", n2 % 4)])
                    pend_sq.append((sq, n2))
                while pend_sq:
                    flush_sq()
                P.dma("sp", xs[b][:, t0:t0 + tn].rearrange("(k p) t -> p k t", p=128), xt, reads=XK(q), writes=[("xs", b, ti)])
                norm_tile(l, v, 1, xt, tn, 7, lambda k, t0=t0, tn=tn: R1[:, k, t0:t0 + tn], lambda k, ti=ti: ("R1", k, ti),
                          lambda k, q=q: [("xt", q, k)])
            wb_unpin(i1)
            wb_unpin(i0)
            if l == 0 and b == 0:
                dump("x1", xs[b], [("xs", b, t) for t in range(len(tiles_e))])
                for k in range(KC):
                    dump("h2", R1[:, k, :], [("R1", k, t) for t in range(len(tiles_e))], idx=k)

            B()
            G0s = [af32(O_R3 + q * 4608, T) for q in range(2)]
            C0s = [af32(O_R3 + (2 + q) * 4608, T) for q in range(2)]
            fd = abf(O_R2, NJ, 1024)

            def load_fdA():
                for hh in range(2):
                    P.dma("pool", fd[:, 0:18, hh * 512:(hh + 1) * 512], ffn_down[l][0:18 * 128, hh * 512:(hh + 1) * 512].rearrange("(j p) n -> p j n", p=128),
                          writes=[("fdA", hh)])
            asts = [abf(WK + q * T, T) for q in range(2)]
            TE = tiles_e[-1][0] + tiles_e[-1][1]
            do_ada = (b == 0 and l + 1 < nlayer)
            ada_q = []
            ada_next = [0]

            def ada_step():
                if not do_ada:
                    return
                if ada_q:
                    ada_slab_mm(l + 1, *ada_q.pop(0))
                if ada_next[0] < 12:
                    ii, wbb = ada_slab_load(l + 1, ada_next[0])
                    ada_q.append((ada_next[0], ii, wbb))
                    ada_next[0] += 1

            for s in range(11):
                i = wb_take()
                wb = load_slab(ffn_upP[l][:, s * 512:(s + 1) * 512].rearrange("(k p) c -> p k c", p=128), i, (KC, 512))
                ada_step()
                if s == 5:
                    ada_step()
                if s == 2:
                    load_fdA()
                for cc in range(2):
                    j = 2 * s + cc
                    q = j % 2
                    G0, C0, ast = G0s[q], C0s[q], asts[q]
                    for ti, (t0, tn) in enumerate(tiles_e):
                        pi = nextps()
                        mm(PS[pi][:, :tn], [(wb[:, k, (2 * cc + 1) * 128:(2 * cc + 2) * 128], R1[:, k, t0:t0 + tn]) for k in range(KC)],
                           [("WB", i)] + R1K(ti), ("ps", pi))
                        ACT(G0[:, t0:t0 + tn], PS[pi][:, :tn], AF.Identity, [("ps", pi)], [("G0", q, ti)])
                    GK = [("G0", q, t) for t in range(5)]
                    CK = [("C0", q, t) for t in range(5)]
                    for (s0, sn) in segs_e:
                        TS(C0[:, s0:s0 + sn], G0[:, s0:s0 + sn], V(l, "fcw", 22 + j), V(l, "fcb", j), ALU.mult, ALU.add, GK + ["vec"], CK)
                        for k in (0, 2):
                            o = k - 1
                            a = max(0, -o)
                            e_ = sn - max(0, o)
                            STT(C0[:, s0 + a:s0 + e_], G0[:, s0 + a + o:s0 + e_ + o], V(l, "fcw", k * 22 + j), C0[:, s0 + a:s0 + e_],
                                ALU.mult, ALU.add, GK + CK + ["vec"], CK)
                    ACT(C0[:, 0:TE], C0[:, 0:TE], AF.Silu, CK, CK)
                    for ti, (t0, tn) in enumerate(tiles_e):
                        pi = nextps()
                        mm(PS[pi][:, :tn], [(wb[:, k, 2 * cc * 128:(2 * cc + 1) * 128], R1[:, k, t0:t0 + tn]) for k in range(KC)],
                           [("WB", i)] + R1K(ti), ("ps", pi))
                        TT(ast[:, t0:t0 + tn], PS[pi][:, :tn], C0[:, t0:t0 + tn], ALU.mult, [("ps", pi)] + CK, [("ast", q)])
                    for ti, (t0, tn) in enumerate(tiles_e):
                        P.dma("sp", actT_h[b][ti][:, j, 0:tn], ast[:, t0:t0 + tn], reads=[("ast", q)], writes=[("actT", b, j, ti)])

            if do_ada:
                while ada_q:
                    ada_slab_mm(l + 1, *ada_q.pop(0))
                ada_finish(l + 1)

            B()
            for hh in range(2):
                P.dma("pool", fd[:, 18:NJ, hh * 512:(hh + 1) * 512], ffn_down[l][18 * 128:NJ * 128, hh * 512:(hh + 1) * 512].rearrange("(j p) n -> p j n", p=128),
                      writes=[("fdB", hh)])
            ats = [abf(O_R1, NJ, 512), abf(O_R2 + NJ * 1024, NJ, 512)]
            xts = [af32(O_RV, KC, 512), af32(O_WK, KC, 512)]
            sqs = [abf(WK + 8192 + q * 512, 512) for q in range(2)]
            rsf = af32(WK + 9216, 512)
            pend11 = []

            def s11_load(ti_):
                t0_, tn_ = tiles_e[ti_]
                P.dma("sp", xts[ti_ % 2][:, :, :tn_], xs[b][:, t0_:t0_ + tn_].rearrange("(k p) t -> p k t", p=128),
                      reads=[("xs", b, ti_)], writes=XK(ti_ % 2))
            for ti, (t0, tn) in enumerate(tiles_e):
                v = 2 if ti == 4 else b
                q = ti % 2
                at = ats[q][:, :, :tn]
                xt = xts[q][:, :, :tn]
                P.dma("pool", at, actT_h[b][ti][:, :, 0:tn], reads=[("actT", b, j, ti) for j in range(NJ)], writes=[("at", q)])
                if ti == 0:
                    s11_load(0)
                if ti + 1 < len(tiles_e):
                    s11_load(ti + 1)
                for n2 in range(8):
                    pi = nextps()
                    mm(PS[pi][:, :tn], [(fd[:, j, n2 * 128:(n2 + 1) * 128], at[:, j, :]) for j in range(NJ)], [("fdA", n2 // 4), ("fdB", n2 // 4), ("at", q)], ("ps", pi))
                    STT(xt[:, n2, :], PS[pi][:, :tn], MOD(l, v, 5, n2), xt[:, n2, :], ALU.mult, ALU.add,
                        [("ps", pi), ("xt", q, n2), ("modt", l)], [("xt", q, n2)])
                    if last and ti < 4:
                        sq = sqs[n2 % 2][:, :tn]
                        ACT(sq, xt[:, n2, :], AF.Square, [("xt", q, n2)], [("sq", n2 % 2)])
                        pend11.append((sq, n2, tn))
                    if len(pend11) >= 2 or (pend11 and n2 == 7):
                        while pend11 and (len(pend11) >= 2 or n2 == 7):
                            sq_, n2_, tn_ = pend11.pop(0)
                            P.op("pe", lambda e, sq_=sq_, tn_=tn_, n2_=n2_: e.matmul(PS[7][:, :tn_], ones_bf[:], sq_, start=(n2_ == 0), stop=(n2_ == 7)),
                                 reads=[("sq", n2_ % 2), "ones_bf"], writes=[("ps", 7)])
                if not last or "x2" in dbg_out:
                    P.dma("sp" if last else "act", xs[b][:, t0:t0 + tn].rearrange("(k p) t -> p k t", p=128), xt, reads=XK(q), writes=[("xs", b, ti)])
                if last and ti < 4:
                    rs_ = rsf[:, :tn]
                    ACT(rs_, PS[7][:, :tn], AF.Ln, [("ps", 7)], ["rsf"], bias=EPS, scale=1.0 / D)
                    ACT(rs_, rs_, AF.Exp, ["rsf"], ["rsf"], scale=-0.5)
                    for k in range(KC):
                        STT(xt[:, k, :], xt[:, k, :], V(l, "fng", k), rs_, ALU.mult, ALU.mult, [("xt", q, k), "rsf", "vec"], [("xt", q, k)])
                    P.dma("sp", yT[b][:, t0:t0 + tn].rearrange("(k p) t -> p k t", p=128), xt, reads=XK(q), writes=[("yT", b, ti)])
            if l == 0 and b == 0:
                dump("x2", xs[b], [("xs", b, t) for t in range(len(tiles_e))])

        for l in range(nlayer):
            for b in range(nb):
                try:
                    layer_batch(l, b)
                except Exception as ex:
                    if type(ex).__name__ != "_Stop":
                        raise
        P.barrier(bscr[:, 0:1])
        P.emit()
    return nc


def _colify(a):
    a = np.asarray(a, np.float32)
    lead = a.shape[:-1]
    n = a.shape[-1] // 128
    a = a.reshape(*lead, n, 128)
    a = np.moveaxis(a, -1, 0)
    return a.reshape(128, -1)


def prep_shared(inp, nlayer=NLAYER):
    f = lambda k: np.asarray(inp[k], np.float32)
    vecs = np.zeros((128, nlayer, NV), np.float32)
    for l in range(nlayer):
        def put(name, arr):
            c = _colify(arr)
            vecs[:, l, VOFF[name]:VOFF[name] + c.shape[1]] = c
        put("n1g", f("norm1_g")[l]); put("n2g", f("norm2_g")[l]); put("adab", f("ada_b")[l]); put("qng", f("q_norm_g")[l])
        put("kvng", f("kv_norm_g")[l]); put("lcw", f("lru_conv_w")[l]); put("lcb", f("lru_conv_b")[l]); put("lba", f("lru_ba")[l])
        put("lbi", f("lru_bi")[l]); put("llam", f("lru_lambda")[l]); put("pb", f("pool_b")[l]); put("psc", f("pool_scale")[l])
        put("fcw", f("ffn_conv_w")[l]); put("fcb", f("ffn_conv_b")[l]); put("fng", f("final_norm_g"))
    half = 16
    inv = (10000.0 ** (-np.arange(0, half, 2, dtype=np.float32) / half)).astype(np.float32)
    t = np.arange(SEQ)
    ang_r = (t // 64).astype(np.float32)[:, None] * inv
    ang_c = (t % 64).astype(np.float32)[:, None] * inv
    cr, sr, cc, sc = np.cos(ang_r).T, np.sin(ang_r).T, np.cos(ang_c).T, np.sin(ang_c).T
    ropeC = np.ones((32, T), np.float32)
    ropeS = np.zeros((32, T), np.float32)
    ropeC[:, :SEQ] = np.concatenate([cr, cr, cc, cc], 0)
    ropeS[:, :SEQ] = np.concatenate([-sr, sr, -sc, sc], 0)
    ptab = np.ones((128, 4, 16), np.float32)
    for g, w in enumerate(POOL_WIN):
        left = w // 2
        right = w - 1 - left
        for tt in range(left):
            ptab[:, g, tt] = 1.0 / (tt + right + 1)
        for i in range(right):
            ptab[:, g, 8 + i] = 1.0 / (right - i + left)
    w_in = f("w_in")[:nlayer]
    perm = _win_perm()
    w_uq = f("w_uq")[:nlayer]
    w_uq_sw = w_uq.copy()
    for h in range(8):
        w_uq_sw[:, :, h * 96 + 64:h * 96 + 96] = w_uq[:, :, h * 96 + 64 + KR_SWAP]
    w_ukv = f("w_ukv")[:nlayer].reshape(nlayer, 128, 8, 128)
    ffn_up = f("ffn_up")[:nlayer]
    upcols = []
    for j in range(NJ):
        upcols += list(range(j * 128, (j + 1) * 128)) + list(range(DFF + j * 128, DFF + (j + 1) * 128))
    kr = w_in[:, :, COL_KV:COL_KR]
    sh = dict(
        vecs=vecs, ropeC=ropeC, ropeS=ropeS, ptab=ptab,
        ada_w=np.ascontiguousarray(f("ada_w")[:nlayer]),
        w_inP=np.ascontiguousarray(w_in[:, :, perm]),
        w_kr2=np.ascontiguousarray(np.concatenate([kr, kr[:, :, KR_SWAP]], -1)),
        w_uq2=np.ascontiguousarray(np.stack([w_uq, w_uq_sw], 2)),
        w_ukvP=np.ascontiguousarray(np.concatenate([w_ukv[..., :64].reshape(nlayer, 128, 512), w_ukv[..., 64:].reshape(nlayer, 128, 512)], -1)),
        lru_wa=f("lru_wa")[:nlayer], lru_wi=f("lru_wi")[:nlayer], pool_w=f("pool_w")[:nlayer],
        proj_mla=f("proj_mla")[:nlayer], proj_lru=f("proj_lru")[:nlayer], proj_pool=f("proj_pool")[:nlayer],
        w_out=f("w_out")[:nlayer], ffn_upP=np.ascontiguousarray(ffn_up[:, :, np.array(upcols)]), ffn_down=f("ffn_down")[:nlayer],
    )
    return sh


def prep_core(inp, batches):
    x = np.asarray(inp["x"], np.float32)
    ctx = np.asarray(inp["ctx"], np.float32)
    c = np.asarray(inp["c"], np.float32)
    xT = np.stack([np.concatenate([x[b].T, ctx[b].T], axis=1) for b in batches], 0)
    cv = [c[batches[0]], c[batches[-1]], np.asarray(inp["c_ctx"], np.float32)]
    cT = np.stack([_colify(v)[:, :] for v in cv], -1)
    return dict(xT=np.ascontiguousarray(xT), cT=np.ascontiguousarray(cT))


_CACHE = {}


def kernel(**inputs):
    if "nc" not in _CACHE:
        _CACHE["nc"] = build_program()
    nc = _CACHE["nc"]
    sh = prep_shared(inputs)
    in_maps = []
    for core in range(NCORES):
        m = dict(sh)
        m.update(prep_core(inputs, [2 * core, 2 * core + 1]))
        in_maps.append(m)
    res = run_bass_kernel_spmd(nc, in_maps, core_ids=list(range(NCORES)))
    out = np.empty((16, SEQ, D), np.float32)
    for core in range(NCORES):
        y = np.asarray(res.results[core]["yT"])
        out[2 * core] = y[0].T
        out[2 * core + 1] = y[1].T
    return out
```
